# Optimizing a Trainium2 kernel written in Bass

```python
import math
import jax, jax.numpy as jnp
from jax import lax
import numpy as np

D_MODEL = 1024
BATCH = 4
SEQ = 4096
DEPTH = 2

CHUNK = 64
N_MIXERS = 2
N_S5 = (DEPTH + 1) // 2
N_GLA = DEPTH // 2
S5_GROUP = 16
S5_GROUPS = D_MODEL // S5_GROUP
S5_STATE = 64
DT_MIN = 1e-3
DT_MAX = 1e-1
GLA_HEADS = 4
GLA_QK = D_MODEL // 2
GLA_DK = GLA_QK // GLA_HEADS
GLA_DV = D_MODEL // GLA_HEADS
GLA_GATE_RANK = 16
GLA_GATE_TAU = 16.0
GLA_IN = 2 * GLA_QK + D_MODEL + GLA_GATE_RANK + D_MODEL
D_FF = 4 * D_MODEL
EPS = 1e-6

kernel_name = "chunk_causal_s5_gla_hybrid"


def rmsnorm(x, g):
    xf = x.astype(jnp.float32)
    y = xf * lax.rsqrt(jnp.mean(xf * xf, axis=-1, keepdims=True) + EPS) * g.astype(jnp.float32)
    return y.astype(x.dtype)


def modulate(h, shift, scale):
    return h * (1.0 + scale[:, None, :]) + shift[:, None, :]


def s5_mixer(u, a_re, a_im, log_dt, b_re, b_im, c_re, c_im, d_skip, w_glu):
    bsz, seq, _ = u.shape
    f32 = jnp.float32
    uf = u.astype(f32).reshape(bsz, seq, S5_GROUPS, S5_GROUP)
    dt = jnp.exp(log_dt.astype(f32))[:, None]
    ar = a_re.astype(f32)
    ai = a_im.astype(f32)
    mag = jnp.exp(ar * dt)
    ph = ai * dt
    lb_re = mag * jnp.cos(ph)
    lb_im = mag * jnp.sin(ph)
    den = ar * ar + ai * ai
    nr = lb_re - 1.0
    ni = lb_im
    f_re = (nr * ar + ni * ai) / den
    f_im = (ni * ar - nr * ai) / den
    br = b_re.astype(f32)
    bi = b_im.astype(f32)
    bb_re = f_re[..., None] * br - f_im[..., None] * bi
    bb_im = f_re[..., None] * bi + f_im[..., None] * br
    bu_re = jnp.einsum('blgh,gph->blgp', uf, bb_re)
    bu_im = jnp.einsum('blgh,gph->blgp', uf, bb_im)
    la_re = jnp.broadcast_to(lb_re, bu_re.shape)
    la_im = jnp.broadcast_to(lb_im, bu_im.shape)

    def combine(left, right):
        a1r, a1i, b1r, b1i = left
        a2r, a2i, b2r, b2i = right
        return (a2r * a1r - a2i * a1i,
                a2r * a1i + a2i * a1r,
                a2r * b1r - a2i * b1i + b2r,
                a2r * b1i + a2i * b1r + b2i)

    _, _, xr, xi = lax.associative_scan(combine, (la_re, la_im, bu_re, bu_im), axis=1)
    y = (jnp.einsum('blgp,ghp->blgh', xr, c_re.astype(f32))
         - jnp.einsum('blgp,ghp->blgh', xi, c_im.astype(f32))
         + d_skip.astype(f32).reshape(S5_GROUPS, S5_GROUP) * uf)
    z = jax.nn.gelu(y.reshape(bsz, seq, D_MODEL)).astype(u.dtype)
    val, gate = jnp.split(z @ w_glu, 2, axis=-1)
    return val * jax.nn.sigmoid(gate)


def gla_mixer(h, w_in, w_gate2, b_gate, g_norm, w_out):
    bsz, seq, _ = h.shape
    n = seq // CHUNK
    f32 = jnp.float32
    proj = h @ w_in
    q, k, v, glr, r = jnp.split(
        proj, [GLA_QK, 2 * GLA_QK, 2 * GLA_QK + D_MODEL, 2 * GLA_QK + D_MODEL + GLA_GATE_RANK], axis=-1)
    log_a = jax.nn.log_sigmoid((glr @ w_gate2 + b_gate).astype(f32)) / GLA_GATE_TAU

    def heads(t, dh):
        return t.reshape(bsz, n, CHUNK, GLA_HEADS, dh).transpose(0, 3, 1, 2, 4).astype(f32)

    q = heads(q, GLA_DK) * (GLA_DK ** -0.5)
    k = heads(k, GLA_DK)
    v = heads(v, GLA_DV)
    gc = jnp.cumsum(heads(log_a, GLA_DK), axis=3)
    g_end = gc[:, :, :, -1:, :]
    k_dec = k * jnp.exp(g_end - gc)
    scores = jnp.einsum('bhncd,bhnsd->bhncs', q, k_dec)
    o_intra = jnp.einsum('bhncs,bhnse->bhnce', scores, v)
    kv = jnp.einsum('bhnsd,bhnse->bhnde', k_dec, v)
    decay = jnp.exp(g_end[:, :, :, 0, :])

    def step(state, inp):
        kv_c, dec_c = inp
        return dec_c[..., None] * state + kv_c, state

    s0 = jnp.zeros((bsz, GLA_HEADS, GLA_DK, GLA_DV), f32)
    _, s_prev = lax.scan(step, s0, (kv.transpose(2, 0, 1, 3, 4), decay.transpose(2, 0, 1, 3)))
    s_prev = s_prev.transpose(1, 2, 0, 3, 4)
    o_inter = jnp.einsum('bhncd,bhnde->bhnce', q * jnp.exp(g_end), s_prev)
    o = o_intra + o_inter
    o = o * lax.rsqrt(jnp.mean(o * o, axis=-1, keepdims=True) + EPS)
    o = o.transpose(0, 2, 3, 1, 4).reshape(bsz, seq, D_MODEL) * g_norm.astype(f32)
    o = o.astype(h.dtype) * jax.nn.silu(r)
    return o @ w_out


def sqrelu_mlp(h, w1, w2):
    a = jax.nn.relu(h @ w1)
    return (a * a) @ w2


def setup_inputs(seed: int = 0) -> dict:
    key = jax.random.key(seed)
    ks = jax.random.split(key, 24)
    f32 = jnp.float32
    nrm = lambda k, shape, s: jax.random.normal(k, shape, f32) * s
    x = jax.random.normal(ks[0], (BATCH, SEQ, D_MODEL), f32)
    c = jax.random.normal(ks[1], (BATCH, D_MODEL), f32)
    w_ada = nrm(ks[2], (DEPTH, D_MODEL, 6 * D_MODEL), 0.5 * D_MODEL ** -0.5)
    b_ada = nrm(ks[3], (DEPTH, 6 * D_MODEL), 0.02)
    norm_mix = 1.0 + nrm(ks[4], (DEPTH, D_MODEL), 0.02)
    norm_mlp = 1.0 + nrm(ks[5], (DEPTH, D_MODEL), 0.02)
    n_idx = jnp.arange(S5_STATE, dtype=f32)
    s5_a_re = -0.5 * jnp.exp(nrm(ks[6], (N_S5, S5_GROUPS, S5_STATE), 0.05))
    s5_a_im = math.pi * n_idx + nrm(ks[7], (N_S5, S5_GROUPS, S5_STATE), 0.05)
    s5_log_dt = jax.random.uniform(ks[8], (N_S5, S5_GROUPS), f32, math.log(DT_MIN), math.log(DT_MAX))
    s5_b_re = nrm(ks[9], (N_S5, S5_GROUPS, S5_STATE, S5_GROUP), (2.0 * S5_GROUP) ** -0.5)
    s5_b_im = nrm(ks[10], (N_S5, S5_GROUPS, S5_STATE, S5_GROUP), (2.0 * S5_GROUP) ** -0.5)
    s5_c_re = nrm(ks[11], (N_S5, S5_GROUPS, S5_GROUP, S5_STATE), S5_STATE ** -0.5)
    s5_c_im = nrm(ks[12], (N_S5, S5_GROUPS, S5_GROUP, S5_STATE), S5_STATE ** -0.5)
    s5_d = nrm(ks[13], (N_S5, D_MODEL), 1.0)
    s5_w_glu = nrm(ks[14], (N_S5, D_MODEL, 2 * D_MODEL), D_MODEL ** -0.5)
    gla_w_in = nrm(ks[15], (N_GLA, D_MODEL, GLA_IN), D_MODEL ** -0.5)
    gla_w_gate2 = nrm(ks[16], (N_GLA, GLA_GATE_RANK, GLA_QK), GLA_GATE_RANK ** -0.5)
    gla_b_gate = nrm(ks[17], (N_GLA, GLA_QK), 0.1)
    gla_g_norm = 1.0 + nrm(ks[18], (N_GLA, D_MODEL), 0.02)
    gla_w_out = nrm(ks[19], (N_GLA, D_MODEL, D_MODEL), D_MODEL ** -0.5)
    w_ff1 = nrm(ks[20], (DEPTH, D_MODEL, D_FF), D_MODEL ** -0.5)
    w_ff2 = nrm(ks[21], (DEPTH, D_FF, D_MODEL), D_FF ** -0.5)
    norm_final = 1.0 + nrm(ks[22], (D_MODEL,), 0.02)
    return {"x": x, "c": c, "w_ada": w_ada, "b_ada": b_ada, "norm_mix": norm_mix, "norm_mlp": norm_mlp,
            "s5_a_re": s5_a_re, "s5_a_im": s5_a_im, "s5_log_dt": s5_log_dt, "s5_b_re": s5_b_re,
            "s5_b_im": s5_b_im, "s5_c_re": s5_c_re, "s5_c_im": s5_c_im, "s5_d": s5_d, "s5_w_glu": s5_w_glu,
            "gla_w_in": gla_w_in, "gla_w_gate2": gla_w_gate2, "gla_b_gate": gla_b_gate,
            "gla_g_norm": gla_g_norm, "gla_w_out": gla_w_out, "w_ff1": w_ff1, "w_ff2": w_ff2,
            "norm_final": norm_final}


def reference(x, c, w_ada, b_ada, norm_mix, norm_mlp, s5_a_re, s5_a_im, s5_log_dt, s5_b_re, s5_b_im,
              s5_c_re, s5_c_im, s5_d, s5_w_glu, gla_w_in, gla_w_gate2, gla_b_gate, gla_g_norm, gla_w_out,
              w_ff1, w_ff2, norm_final):
    cs = jax.nn.silu(c)
    for i in range(DEPTH):
        mod = cs @ w_ada[i] + b_ada[i]
        sh1, sc1, gt1, sh2, sc2, gt2 = jnp.split(mod, 6, axis=-1)
        h = modulate(rmsnorm(x, norm_mix[i]), sh1, sc1)
        j = i // N_MIXERS
        if i % N_MIXERS == 0:
            y = s5_mixer(h, s5_a_re[j], s5_a_im[j], s5_log_dt[j], s5_b_re[j], s5_b_im[j],
                         s5_c_re[j], s5_c_im[j], s5_d[j], s5_w_glu[j])
        else:
            y = gla_mixer(h, gla_w_in[j], gla_w_gate2[j], gla_b_gate[j], gla_g_norm[j], gla_w_out[j])
        x = x + gt1[:, None, :] * y
        h = modulate(rmsnorm(x, norm_mlp[i]), sh2, sc2)
        x = x + gt2[:, None, :] * sqrelu_mlp(h, w_ff1[i], w_ff2[i])
    return rmsnorm(x, norm_final)
```

```python
import math
from contextlib import ExitStack
import numpy as np
import concourse.bass as bass
import concourse.mybir as mybir
from concourse.bass_utils import run_bass_kernel_spmd

F32 = mybir.dt.float32
BF = mybir.dt.bfloat16
ALU = mybir.AluOpType
AF = mybir.ActivationFunctionType
NTOK = 2048
SEQ = 4096
EPS = 1e-6
PAIRS = [[0, 1], [2, 3], [4, 5], [6, 7]]
_DBG_NOCC = 0
_DBG_STAGE = 0


class _Stop(Exception):
    pass


class Buf:
    def __init__(self, name=""):
        self.name = name
        self.w = None
        self.r = {}


class Prog:
    ENGS = ["pe", "act", "dve", "pool", "sync"]

    def __init__(self, ndma=48):
        self.ops = {e: [] for e in self.ENGS}
        self.cnt = {e: 0 for e in self.ENGS}
        self.waited = {}
        self.ndma = ndma
        self.dma_use = [0] * ndma
        self.pools = {"sync": list(range(0, ndma - 16)), "pool": list(range(ndma - 16, ndma))}
        self.rrq = {"sync": 0, "pool": 0}

    def _wait(self, eng, dep):
        key, val = dep
        if eng == "pe" and key == "pe":
            return
        if self.waited.get((eng, key), 0) >= val:
            return
        self.waited[(eng, key)] = val
        self.ops[eng].append(("wait", key, val))

    def _deps(self, eng, reads, writes):
        for b in reads:
            if b.w is not None:
                self._wait(eng, b.w)
        for b in writes:
            if b.w is not None:
                self._wait(eng, b.w)
            for k, v in b.r.items():
                self._wait(eng, (k, v))

    def _mark(self, me, reads, writes):
        for b in reads:
            b.r[me[0]] = max(b.r.get(me[0], 0), me[1])
        for b in writes:
            b.w = me
            b.r = {}

    def op(self, eng, fn, reads=(), writes=()):
        self._deps(eng, reads, writes)
        self.cnt[eng] += 1
        self.ops[eng].append(("op", fn))
        self._mark((eng, self.cnt[eng]), reads, writes)

    def dma(self, q, fn, reads=(), writes=()):
        pl = self.pools[q]
        i = pl[self.rrq[q] % len(pl)]
        self.rrq[q] += 1
        if self.dma_use[i] > 0:
            self._wait(q, (("d", i), 16 * self.dma_use[i]))
        self._deps(q, reads, writes)
        self.dma_use[i] += 1
        self.ops[q].append(("dma", fn, i))
        self._mark((("d", i), 16 * self.dma_use[i]), reads, writes)

    def cc(self, fn, reads=(), writes=()):
        if _DBG_NOCC:
            return
        self._deps("pool", reads, writes)
        self.cnt.setdefault("cc", 0)
        self.cnt["cc"] += 1
        self.ops["pool"].append(("cc", fn))
        self._mark(("cc", self.cnt["cc"]), reads, writes)

    def final_wait(self, eng, bufs):
        for b in bufs:
            if b.w is not None:
                self._wait(eng, b.w)

    def emit(self, nc, es):
        sems = {e: es.enter_context(nc.semaphore("s_" + e)) for e in ["pe", "act", "dve", "pool", "cc"]}
        dsem = [es.enter_context(nc.semaphore("d%d" % i)) for i in range(self.ndma)]

        def sem_of(key):
            return dsem[key[1]] if isinstance(key, tuple) else sems[key]

        block = es.enter_context(nc.Block())

        def run(name, eng):
            for o in self.ops[name]:
                if o[0] == "wait":
                    eng.wait_ge(sem_of(o[1]), o[2])
                elif o[0] == "op":
                    o[1](eng).then_inc(sems[name], 1)
                elif o[0] == "dma":
                    o[1](eng).then_inc(dsem[o[2]], 16)
                elif o[0] == "cc":
                    o[1](eng).then_inc(sems["cc"], 1)

        @block.tensor
        def _(e):
            run("pe", e)

        @block.scalar
        def _(e):
            run("act", e)

        @block.vector
        def _(e):
            run("dve", e)

        @block.gpsimd
        def _(e):
            run("pool", e)

        @block.sync
        def _(e):
            run("sync", e)


def build():
    nc = bass.Bass("TRN2", target_bir_lowering=False)
    es = ExitStack()
    P = Prog()

    def din(name, shape, dt=F32):
        return nc.dram_tensor(name, list(shape), dt, kind="ExternalInput").ap()

    xT_own = din("xT_own", [128, 8, NTOK])
    xT_full = din("xT_full", [128, 8, SEQ])
    c_col = din("c_col", [128, 8])
    w_ada = din("w_ada", [2, 1024, 6144])
    b_ada_col = din("b_ada_col", [128, 2, 48])
    nmix_col = din("nmix_col", [128, 2, 8])
    nmlp_col = din("nmlp_col", [128, 2, 8])
    nfin_col = din("nfin_col", [128, 8])
    gn_col = din("gn_col", [128, 8])
    a_re_st = din("a_re_st", [128, 16])
    a_im_st = din("a_im_st", [128, 16])
    ldt_st = din("ldt_st", [128, 16])
    bexp_re = din("bexp_re", [128, 4, 128])
    bexp_im = din("bexp_im", [128, 4, 128])
    cexp_re = din("cexp_re", [128, 16, 32])
    cexp_im = din("cexp_im", [128, 16, 32])
    d_col = din("d_col", [128, 4])
    w_glu = din("w_glu", [1024, 2048])
    w_in = din("w_in", [1024, 3088])
    w_g2 = din("w_g2", [16, 512])
    b_gate = din("b_gate", [1, 512])
    w_out = din("w_out", [1024, 1024])
    w_ff1 = din("w_ff1", [2, 1024, 4096])
    w_ff2 = din("w_ff2", [2, 4096, 1024])
    sel = din("sel", [128, 2])
    umat = din("umat", [128, 128])
    ind = din("ind", [128, 2])
    eye32 = din("eye32", [128, 32])
    outT = nc.dram_tensor("outT", [128, 8, NTOK], F32, kind="ExternalOutput").ap()
    z_ibs = [nc.dram_tensor("z_ib%d" % a, [256, SEQ], BF) for a in range(2)]
    z_obs = [nc.dram_tensor("z_ob%d" % a, [512, SEQ], BF) for a in range(2)]
    s_ib = nc.dram_tensor("s_ib", [128, 1024], F32)
    s_ob = nc.dram_tensor("s_ob", [256, 1024], F32)

    def sb(name, shape, dt=F32):
        return es.enter_context(nc.sbuf_tensor(name, list(shape), dt))

    def ps(name, shape, dt=F32):
        return es.enter_context(nc.psum_tensor(name, list(shape), dt))

    A_x = sb("A_x", [128, 8, NTOK])
    A_h = sb("A_h", [128, 8, NTOK], BF)
    A_g = sb("A_g", [128, 16384], BF)
    A_w = [sb("A_w0", [128, 8192], BF), sb("A_w1", [128, 8192], BF)]
    A_s = sb("A_s", [128, 4096])
    A_t = sb("A_t", [128, 4096], BF)
    psA = ps("psA", [128, 4, 512])
    psB = ps("psB", [128, 2, 512])
    psC = ps("psC", [128, 1024])
    bA = [Buf("psA%d" % i) for i in range(4)]
    bB = [Buf("psB%d" % i) for i in range(2)]
    bC = Buf("psC")
    b_x = [[Buf() for _ in range(4)] for _ in range(8)]
    b_h = [Buf() for _ in range(4)]
    b_w = [Buf("w0"), Buf("w1")]
    b_g = Buf("g")
    b_s = Buf("s")
    b_t = Buf("t")

    cst = sb("cst", [128, 16])
    b_c = Buf("cst")
    ones_bf = sb("ones_bf", [128, 128], BF)

    def memset(t, v):
        return lambda e: e.memset(t, v)

    CV = {"one": 1.0, "negpi": -math.pi, "eps": EPS, "zero": 0.0}
    cidx = {k: i for i, k in enumerate(CV)}
    for k, i in cidx.items():
        P.op("dve", memset(cst[:, i:i + 1], CV[k]), writes=[b_c])
    P.op("dve", memset(ones_bf[:], 1.0), writes=[b_c])

    def C(k):
        return cst[:, cidx[k]:cidx[k] + 1]

    small = {}
    b_sm = Buf("small")

    def load_small(name, ap, shape, dt=F32):
        t = sb("sm_" + name, shape, dt)
        P.dma("sync", lambda e, t=t, ap=ap: e.dma_start(out=t[:], in_=ap), writes=[b_sm])
        small[name] = t
        return t

    ccol = sb("sm_ccol", [128, 8])
    b_cc = Buf("ccol")
    P.dma("sync", lambda e: e.dma_start(out=ccol[:], in_=c_col), writes=[b_cc])
    bada = load_small("bada", b_ada_col, [128, 2, 48])
    nmix = load_small("nmix", nmix_col, [128, 2, 8])
    nmlp = load_small("nmlp", nmlp_col, [128, 2, 8])
    nfin = load_small("nfin", nfin_col, [128, 8])
    gnc = load_small("gnc", gn_col, [128, 8])
    arst = load_small("arst", a_re_st, [128, 16])
    aist = load_small("aist", a_im_st, [128, 16])
    ldst = load_small("ldst", ldt_st, [128, 16])
    dcol = load_small("dcol", d_col, [128, 4])
    selt = load_small("selt", sel, [128, 2])
    umt = load_small("umt", umat, [128, 128])
    indt = load_small("indt", ind, [128, 2])
    eyet = load_small("eyet", eye32, [128, 32])
    BT = sb("BT", [128, 2, 4, 128], BF)
    P.dma("pool", lambda e: e.dma_start(out=BT[:, 0], in_=bexp_re), writes=[b_sm])
    P.dma("pool", lambda e: e.dma_start(out=BT[:, 1], in_=bexp_im), writes=[b_sm])
    wg2 = sb("wg2", [16, 512], BF)
    bgb = sb("bgb", [1, 512], BF)
    P.dma("pool", lambda e: e.dma_start(out=wg2[:], in_=w_g2), writes=[b_sm])
    P.dma("pool", lambda e: e.dma_start(out=bgb[:], in_=b_gate), writes=[b_sm])

    def tt(eng, out, a, b, op, reads, writes):
        P.op(eng, lambda e: e.tensor_tensor(out=out, in0=a, in1=b, op=op), reads=reads, writes=writes)

    def ts(eng, out, a, s1, s2, op0, op1, reads, writes):
        if s2 is None:
            P.op(eng, lambda e: e.tensor_scalar(out=out, in0=a, scalar1=s1, scalar2=None, op0=op0),
                 reads=reads, writes=writes)
            return
        P.op(eng, lambda e: e.tensor_scalar(out=out, in0=a, scalar1=s1, scalar2=s2, op0=op0, op1=op1),
             reads=reads, writes=writes)

    def stt(out, a, s, b, op0, op1, reads, writes):
        P.op("dve", lambda e: e.scalar_tensor_tensor(out=out, in0=a, scalar=s, in1=b, op0=op0, op1=op1),
             reads=reads, writes=writes)

    def act(out, a, func, reads, writes, bias=None, scale=1.0):
        kw = {} if bias is None else {"bias": bias}
        P.op("act", lambda e: e.activation(out=out, in_=a, func=func, scale=scale, **kw),
             reads=reads, writes=writes)

    def mm(out, pairs, reads, writes, tp=None):
        def fn(e):
            n = len(pairs)
            ins = None
            for j, (l, r) in enumerate(pairs):
                kw = {} if tp is None else {"tile_position": tp}
                ins = e.matmul(out, l, r, start=(j == 0), stop=(j == n - 1), **kw)
            return ins
        P.op("pe", fn, reads=reads, writes=writes)

    def dump(stage, src, bufs, bf=False):
        if _DBG_STAGE != stage:
            return
        bo = Buf("dbg")
        if bf:
            v = A_x[:].rearrange("p k t -> p (k t)")
            P.op("act", lambda e: e.activation(out=v, in_=src, func=AF.Identity), reads=bufs, writes=[bo])
            P.dma("sync", lambda e: e.dma_start(out=outT, in_=A_x[:]), reads=[bo], writes=[bo])
        else:
            P.dma("sync", lambda e: e.dma_start(out=src[0], in_=src[1]), reads=bufs, writes=[bo])
        P.final_wait("sync", [bo])
        raise _Stop()

    def body():
        cs = sb("cs", [128, 8])
        b_cs = Buf("cs")
        act(cs[:], ccol[:], AF.Silu, [b_cc, b_c], [b_cs])
        modc = sb("modc", [128, 2, 48])
        b_mod = Buf("mod")
        pcount = [0]

        bCs = []
        ada_n = [0]
        b_ada2 = Buf("ada2")

        def ada_issue(l, j):
            wi = 0
            half = ada_n[0] % 2 if ada_n[0] < 8 else 0
            ada_n[0] += 1
            bw_ = b_w[0] if half == 0 else b_ada2
            wv = A_w[wi][:].bitcast(F32)[:, 2048 * half:2048 * (half + 1)].rearrange("p (k c) -> p k c", k=8)
            src = w_ada[l, :, j * 256:(j + 1) * 256].rearrange("(k p) c -> p k c", p=128)
            P.dma("sync", lambda e, wv=wv, src=src: e.dma_start(out=wv, in_=src), writes=[bw_])
            for t in range(2):
                col = l * 48 + 2 * j + t
                mm(psC[:, col:col + 1],
                   [(wv[:, k, t * 128:(t + 1) * 128], cs[:, k:k + 1]) for k in range(8)],
                   [bw_, b_cs], [bC])

        def ada_evac(c0, c1):
            tt("dve", modc[:].rearrange("p l c -> p (l c)")[:, c0:c1], psC[:, c0:c1],
               bada[:].rearrange("p l c -> p (l c)")[:, c0:c1], ALU.add, [bC, b_sm], [b_mod])

        def ada_evac_all():
            pass

        def ada_piece(l, j):
            ada_issue(l, j)

        ada_rest = [(0, j) for j in range(8, 24)] + [(1, j) for j in range(24)]
        for j in range(8):
            ada_piece(0, j)
        ada_evac(0, 16)
        if _DBG_STAGE == 1:
            for lj in ada_rest:
                ada_piece(*lj)
            ada_rest = []
            ada_evac(16, 96)
        dump(1, (outT[:, 0, 0:96], modc[:].rearrange("p l c -> p (l c)")), [b_mod])
        modA = sb("modA", [128, 2, 2, 8])

        def mk_modA(l, s_):
            nrm = nmix if s_ == 0 else nmlp
            stt(modA[:, l, s_, :], modc[:, l, 24 * s_ + 8:24 * s_ + 16], 1.0, nrm[:, l, :], ALU.add, ALU.mult,
                [b_mod, b_sm], [b_mod])

        mk_modA(0, 0)

        def shift(l, s_):
            return modc[:, l, 24 * s_:24 * s_ + 8]

        def gate(l, s_):
            return modc[:, l, 24 * s_ + 16:24 * s_ + 24]

        def rstd_block(xin, nt, div, xreads, tbw=512, br=None):
            br = b_s if br is None else br
            sq = A_t[:, 0:nt * tbw].rearrange("p (k c) -> p k c", k=nt)
            act(sq, xin, AF.Square, xreads, [b_t])
            pb = psB[:, 0, 0:tbw]
            mm(pb, [(ones_bf[:], sq[:, k, :]) for k in range(nt)], [b_t, b_c], [bB[0]])
            r = A_s[:, 3584:3584 + tbw]
            ts("dve", r, pb, 1.0 / div, EPS, ALU.mult, ALU.add, [bB[0]], [br])
            act(r, r, AF.Sqrt, [br], [br])
            P.op("dve", lambda e: e.reciprocal(out=r, in_=r), reads=[br], writes=[br])
            return r

        myA = sb("myA", [128, 4])
        myB = sb("myB", [128, 4])
        for dst, srcv in ((myA, modA[:, 0, 0, :]), (myB, shift(0, 0))):
            ts("dve", dst[:], srcv[:, 0:4], selt[:, 0:1], None, ALU.mult, ALU.bypass, [b_mod, b_sm], [b_mod])
            stt(dst[:], srcv[:, 4:8], selt[:, 1:2], dst[:], ALU.mult, ALU.add, [b_mod, b_sm], [b_mod])

        uT = A_h[:].rearrange("p k t -> p (k t)").rearrange("p (k t) -> p k t", k=4)
        b_u = Buf("uT")
        TB0 = 256
        xfs = [A_s[:, i_ * 2048:(i_ + 1) * 2048].rearrange("p (k c) -> p k c", k=8) for i_ in range(2)]
        b_xf = [Buf(), Buf()]
        sq0 = A_t[:, 0:2048].rearrange("p (k c) -> p k c", k=8)
        atf = A_t[:].bitcast(F32)
        r0 = [atf[:, 1024:1280], atf[:, 1280:1536]]
        tmp0 = [atf[:, 1536:1792], atf[:, 1792:2048]]
        b_sq, b_r0, b_tmp0 = Buf(), [Buf(), Buf()], [Buf(), Buf()]
        for tb in range(SEQ // TB0):
            xi_ = tb % 2
            xf = xfs[xi_]
            P.dma("pool", lambda e, tb=tb, xf=xf: e.dma_start(out=xf, in_=xT_full[:, :, tb * TB0:(tb + 1) * TB0]),
                  writes=[b_xf[xi_]])
            act(sq0, xf, AF.Square, [b_xf[xi_]], [b_sq])
            pb = psB[:, tb % 2, 0:TB0]
            mm(pb, [(ones_bf[:], sq0[:, k, :]) for k in range(8)], [b_sq, b_c], [bB[tb % 2]])
            r = r0[xi_]
            ts("dve", r, pb, 1.0 / 1024.0, EPS, ALU.mult, ALU.add, [bB[tb % 2]], [b_r0[xi_]])
            act(r, r, AF.Sqrt, [b_r0[xi_]], [b_r0[xi_]])
            P.op("dve", lambda e, r=r: e.reciprocal(out=r, in_=r), reads=[b_r0[xi_]], writes=[b_r0[xi_]])
            for k in range(4):
                tmp = tmp0[k % 2]
                tt("dve", tmp, xf[:, k, :], r, ALU.mult, [b_xf[xi_], b_r0[xi_]], [b_tmp0[k % 2]])
                act(uT[:, k, tb * TB0:(tb + 1) * TB0], tmp, AF.Identity, [b_tmp0[k % 2], b_mod], [b_u],
                    bias=myB[:, k:k + 1], scale=myA[:, k:k + 1])
        P.op("dve", memset(cst[:, 11:12], 0.0), reads=[], writes=[b_s, b_t] + b_xf + [b_sq] + b_r0 + b_tmp0)
        dump(2, uT.rearrange("p k t -> p (k t)"), [b_u], bf=True)
        lam = sb("lam", [128, 16, 16])
        b_l = Buf("lam")
        L_ = lambda i: lam[:, i, :]
        DT, MAG, PH, SN, CSN, LR, LI, DEN, NR, T1, T2, FR, FI = range(13)
        act(L_(DT), ldst[:], AF.Exp, [b_sm, b_c], [b_l])
        tt("dve", L_(MAG), arst[:], L_(DT), ALU.mult, [b_l, b_sm], [b_l])
        act(L_(MAG), L_(MAG), AF.Exp, [b_l], [b_l])
        tt("dve", L_(PH), aist[:], L_(DT), ALU.mult, [b_l, b_sm], [b_l])
        for dst, off in ((SN, 0.0), (CSN, 0.25)):
            ts("dve", L_(T1), L_(PH), 1.0 / (2 * math.pi), off, ALU.mult, ALU.add, [b_l], [b_l])
            for _ in range(5):
                ts("dve", L_(T2), L_(T1), 0.5, None, ALU.is_gt, None, [b_l], [b_l])
                tt("dve", L_(T1), L_(T1), L_(T2), ALU.subtract, [b_l], [b_l])
            act(L_(dst), L_(T1), AF.Sin, [b_l, b_c], [b_l], scale=2 * math.pi)
        tt("dve", L_(LR), L_(MAG), L_(CSN), ALU.mult, [b_l], [b_l])
        tt("dve", L_(LI), L_(MAG), L_(SN), ALU.mult, [b_l], [b_l])
        tt("dve", L_(DEN), arst[:], arst[:], ALU.mult, [b_sm], [b_l])
        tt("dve", L_(T1), aist[:], aist[:], ALU.mult, [b_sm], [b_l])
        tt("dve", L_(DEN), L_(DEN), L_(T1), ALU.add, [b_l], [b_l])
        P.op("dve", lambda e: e.reciprocal(out=L_(DEN), in_=L_(DEN)), reads=[b_l], writes=[b_l])
        ts("dve", L_(NR), L_(LR), -1.0, None, ALU.add, ALU.bypass, [b_l], [b_l])
        tt("dve", L_(T1), L_(NR), arst[:], ALU.mult, [b_l, b_sm], [b_l])
        tt("dve", L_(T2), L_(LI), aist[:], ALU.mult, [b_l, b_sm], [b_l])
        tt("dve", L_(T1), L_(T1), L_(T2), ALU.add, [b_l], [b_l])
        tt("dve", L_(FR), L_(T1), L_(DEN), ALU.mult, [b_l], [b_l])
        tt("dve", L_(T1), L_(LI), arst[:], ALU.mult, [b_l, b_sm], [b_l])
        tt("dve", L_(T2), L_(NR), aist[:], ALU.mult, [b_l, b_sm], [b_l])
        tt("dve", L_(T1), L_(T1), L_(T2), ALU.subtract, [b_l], [b_l])
        tt("dve", L_(FI), L_(T1), L_(DEN), ALU.mult, [b_l], [b_l])
        CT = sb("CT", [128, 2, 16, 32], BF)
        CTf = A_s[:, 1024:2048].rearrange("p (r q c) -> p r q c", r=2, q=16)
        cre = A_s[:, 0:512].rearrange("p (q c) -> p q c", q=16)
        cim = A_s[:, 512:1024].rearrange("p (q c) -> p q c", q=16)
        P.dma("sync", lambda e: e.dma_start(out=cre, in_=cexp_re), writes=[b_s])
        P.dma("sync", lambda e: e.dma_start(out=cim, in_=cexp_im), writes=[b_s])
        ctmp = sb("ctmp", [128, 2, 32])
        for q in range(16):
            fr, fi = lam[:, FR, q:q + 1], lam[:, FI, q:q + 1]
            ts("dve", ctmp[:, 0, :], cim[:, q, :], fi, None, ALU.mult, ALU.bypass, [b_l, b_s], [b_l])
            stt(CTf[:, 0, q, :], cre[:, q, :], fr, ctmp[:, 0, :], ALU.mult, ALU.subtract, [b_l, b_s], [b_s])
            P.op("dve", lambda e, q=q: e.tensor_copy(out=CT[:, 0, q, :], in_=CTf[:, 0, q, :]), reads=[b_s], writes=[b_l])
            ts("dve", ctmp[:, 1, :], cim[:, q, :], fr, -1.0, ALU.mult, ALU.mult, [b_l, b_s], [b_l])
            ts("dve", ctmp[:, 0, :], cre[:, q, :], fi, None, ALU.mult, ALU.bypass, [b_l, b_s], [b_l])
            tt("dve", CT[:, 1, q, :], ctmp[:, 1, :], ctmp[:, 0, :], ALU.subtract, [b_l], [b_l])
            tt("dve", CTf[:, 1, q, :], ctmp[:, 0, :], ctmp[:, 1, :], ALU.subtract, [b_l], [b_s])
        LP = sb("LP", [128, 12, 3, 16])
        P.op("dve", lambda e: e.tensor_copy(out=LP[:, 0, 0, :], in_=L_(LR)), reads=[b_l], writes=[b_l])
        P.op("dve", lambda e: e.tensor_copy(out=LP[:, 0, 1, :], in_=L_(LI)), reads=[b_l], writes=[b_l])
        for k in range(12):
            ts("dve", LP[:, k, 2, :], LP[:, k, 1, :], -1.0, None, ALU.mult, ALU.bypass, [b_l], [b_l])
            if k < 11:
                tt("dve", L_(T1), LP[:, k, 0, :], LP[:, k, 0, :], ALU.mult, [b_l], [b_l])
                tt("dve", L_(T2), LP[:, k, 1, :], LP[:, k, 1, :], ALU.mult, [b_l], [b_l])
                tt("dve", LP[:, k + 1, 0, :], L_(T1), L_(T2), ALU.subtract, [b_l], [b_l])
                tt("dve", L_(T1), LP[:, k, 0, :], LP[:, k, 1, :], ALU.mult, [b_l], [b_l])
                ts("dve", LP[:, k + 1, 1, :], L_(T1), 2.0, None, ALU.mult, ALU.bypass, [b_l], [b_l])

        L = 8
        NCH = SEQ // L
        LPS = sb("LPS", [128, L, 3, 16])
        for c3 in range(3):
            P.op("dve", lambda e, c3=c3: e.tensor_copy(out=LPS[:, 0, c3, :], in_=LP[:, 0, c3, :]),
                 reads=[b_l], writes=[b_l])
        for s1 in range(1, L):
            pr, pi_ = LPS[:, s1 - 1, 0, :], LPS[:, s1 - 1, 1, :]
            tt("dve", L_(T1), pr, LP[:, 0, 0, :], ALU.mult, [b_l], [b_l])
            tt("dve", L_(T2), pi_, LP[:, 0, 1, :], ALU.mult, [b_l], [b_l])
            tt("dve", LPS[:, s1, 0, :], L_(T1), L_(T2), ALU.subtract, [b_l], [b_l])
            tt("dve", L_(T1), pr, LP[:, 0, 1, :], ALU.mult, [b_l], [b_l])
            tt("dve", L_(T2), pi_, LP[:, 0, 0, :], ALU.mult, [b_l], [b_l])
            tt("dve", LPS[:, s1, 1, :], L_(T1), L_(T2), ALU.add, [b_l], [b_l])
            ts("dve", LPS[:, s1, 2, :], LPS[:, s1, 1, :], -1.0, None, ALU.mult, ALU.bypass, [b_l], [b_l])
        CH = A_w[0][:].bitcast(F32)[:, 2048:4096].rearrange("p (a r n) -> p a r n", a=2, r=2)
        bCH = [[Buf(), Buf()], [Buf(), Buf()]]
        P.op("dve", memset(cst[:, 13:14], 0.0), reads=[], writes=[b_ada2] + sum(bCH, []))
        X = A_x[:].rearrange("p k t -> p (k t)").rearrange("p (a r t) -> p a r t", a=2, r=2)
        bX = [[Buf(), Buf()], [Buf(), Buf()]]
        zT = A_g[:].rearrange("p (k t) -> p k t", k=4)
        b_z = Buf("zT")
        xb = A_s[:].bitcast(BF).rearrange("p (r t) -> p r t", r=2)
        ysc = A_t[:].bitcast(F32)[:, 0:1536].rearrange("p (a c) -> p a c", a=3)
        b_y = b_t
        Sb = A_t[:, 3072:4096].rearrange("p (r n) -> p r n", r=2)
        b_sb = Buf("Sb")
        P.op("dve", memset(Sb[:, :, 0:1], 0.0), reads=[], writes=[b_sb, b_t])
        CQ = A_w[1][:].rearrange("p (q r s c) -> p q r s c", q=16, r=2, s=L)
        b_cq = Buf("CQ")
        for s1 in range(L):
            lr_b = LPS[:, s1, 0, :].unsqueeze(2).broadcast_to([128, 16, 32])
            li_b = LPS[:, s1, 1, :].unsqueeze(2).broadcast_to([128, 16, 32])
            cfr, cfi = CTf[:, 0], CTf[:, 1]
            t1 = A_s[:, 2048:2560].rearrange("p (q c) -> p q c", q=16)
            t2 = A_s[:, 2560:3072].rearrange("p (q c) -> p q c", q=16)
            tt("dve", t1, cfr, lr_b, ALU.mult, [b_s, b_l], [b_s])
            tt("dve", t2, cfi, li_b, ALU.mult, [b_s, b_l], [b_s])
            tt("dve", CQ[:, :, 0, s1, :], t1, t2, ALU.subtract, [b_s], [b_cq, b_w[1]])
            tt("dve", t1, cfr, li_b, ALU.mult, [b_s, b_l], [b_s])
            tt("dve", t2, cfi, lr_b, ALU.mult, [b_s, b_l], [b_s])
            tt("dve", t1, t1, t2, ALU.add, [b_s], [b_s])
            ts("dve", CQ[:, :, 1, s1, :], t1, -1.0, None, ALU.mult, ALU.bypass, [b_s], [b_cq])
        pcnt = 0

        def cview(a_, ri):
            return X[:, a_, ri, :].rearrange("p (c j) -> p c j", j=L)

        def hs_level(src, dst, bs, bd, a, b, nb, d, n, sl):
            sr, si, dr, di = src[0], src[1], dst[0], dst[1]
            stt(sl(dr, d, n), sl(sr, 0, n - d), a, sl(sr, d, n), ALU.mult, ALU.add, [bs[0], b_l], [bd[0]])
            stt(sl(di, d, n), sl(si, 0, n - d), a, sl(si, d, n), ALU.mult, ALU.add, [bs[1], b_l], [bd[1]])
            h0 = d // 2 if d > 1 else 0
            bh = [Buf(), Buf()]
            P.op("dve", lambda e: e.tensor_copy(out=sl(dr, h0, d), in_=sl(sr, h0, d)), reads=[bs[0]], writes=[bh[0]])
            P.op("dve", lambda e: e.tensor_copy(out=sl(di, h0, d), in_=sl(si, h0, d)), reads=[bs[1]], writes=[bh[1]])
            stt(sl(dr, d, n), sl(si, 0, n - d), nb, sl(dr, d, n), ALU.mult, ALU.add, [bs[1], b_l, bh[0]], [bd[0]])
            stt(sl(di, d, n), sl(sr, 0, n - d), b, sl(di, d, n), ALU.mult, ALU.add, [bs[0], b_l, bh[1]], [bd[1]])

        def madd(vr, vi, dsel, ssel, k, q, bx):
            a, b, nb = LP[:, k, 0, q:q + 1], LP[:, k, 1, q:q + 1], LP[:, k, 2, q:q + 1]
            dr, di, sr, si = dsel(vr), dsel(vi), ssel(vr), ssel(vi)
            stt(dr, sr, a, dr, ALU.mult, ALU.add, [bx[0], b_l], [bx[0]])
            stt(di, si, a, di, ALU.mult, ALU.add, [bx[1], b_l], [bx[1]])
            stt(dr, si, nb, dr, ALU.mult, ALU.add, [bx[0], bx[1], b_l], [bx[0]])
            stt(di, sr, b, di, ALU.mult, ALU.add, [bx[0], bx[1], b_l], [bx[1]])

        pcb = [0, 0]
        b_g2 = [Buf(), Buf()]
        P.op("dve", memset(cst[:, 9:10], 0.0), reads=[], writes=[b_t] + b_g2)
        Ddiag = sb("Ddiag", [128, 4, 32], BF)
        for i_ in range(4):
            ts("dve", Ddiag[:, i_, :], eyet[:], dcol[:, i_:i_ + 1], None, ALU.mult, ALU.bypass, [b_sm], [b_l])

        def stA(q):
            i, q4, xp = q // 4, q % 4, q % 2
            rows = slice(32 * q4, 32 * q4 + 32)
            for ri in range(2):
                for tb in range(8):
                    pb = pcb[0] % 2
                    pcb[0] += 1
                    mm(psB[:, pb, :], [(BT[rows, ri, i, :], uT[rows, i, tb * 512:(tb + 1) * 512])],
                       [b_sm, b_u], [bB[pb]], tp=(32 * q4, 0))
                    act(X[:, xp, ri, tb * 512:(tb + 1) * 512], psB[:, pb, :], AF.Identity, [bB[pb], b_c],
                        [bX[xp][ri]])

        def stB1(q):
            xp = q % 2
            vr, vi = cview(xp, 0), cview(xp, 1)
            madd(vr, vi, lambda v: v[:, :, 1::2], lambda v: v[:, :, 0::2], 0, q, bX[xp])
            madd(vr, vi, lambda v: v[:, :, 3::4], lambda v: v[:, :, 1::4], 1, q, bX[xp])
            madd(vr, vi, lambda v: v[:, :, 7], lambda v: v[:, :, 3], 2, q, bX[xp])
            madd(vr, vi, lambda v: v[:, :, 5], lambda v: v[:, :, 3], 1, q, bX[xp])
            madd(vr, vi, lambda v: v[:, :, 2::2], lambda v: v[:, :, 1:6:2], 0, q, bX[xp])

        def stB2(q):
            xp = q % 2
            for ri in range(2):
                act(CH[:, 0, ri, :], cview(xp, ri)[:, :, L - 1], AF.Identity, [bX[xp][ri], b_c], [bCH[0][ri]])
            for k in range(9):
                s_, d_ = k % 2, 1 - (k % 2)
                hs_level([CH[:, s_, 0, :], CH[:, s_, 1, :]], [CH[:, d_, 0, :], CH[:, d_, 1, :]], bCH[s_], bCH[d_],
                         LP[:, k + 3, 0, q:q + 1], LP[:, k + 3, 1, q:q + 1], LP[:, k + 3, 2, q:q + 1], 1 << k, NCH,
                         lambda v, lo, hi: v[:, lo:hi])

        def stCconv(q):
            i, q4, xp = q // 4, q % 4, q % 2
            rows = slice(32 * q4, 32 * q4 + 32)
            for ri in range(2):
                act(Sb[:, ri, 1:NCH], CH[:, 1, ri, 0:NCH - 1], AF.Identity, [bCH[1][ri], b_c], [b_sb])
                act(xb[:, ri, :], X[:, xp, ri, :], AF.Identity, [bX[xp][ri], b_c], [b_s])
            for tb in range(4):
                ymm_emit(q, tb)

        def ymm_emit(q, tb):
            i, q4 = q // 4, q % 4
            rows = slice(32 * q4, 32 * q4 + 32)
            tsl = slice(tb * 512, (tb + 1) * 512)
            pb = (pcb[1] + tb) % 4
            pv = psA[rows, pb, :].rearrange("p (c j) -> p c j", j=L)

            def ymm(e):
                tp = (0, 32 * q4)
                e.matmul(psA[rows, pb, :], Ddiag[rows, i, :], uT[rows, i, tsl], start=True, stop=False,
                         tile_position=(32 * q4, 32 * q4))
                e.matmul(psA[rows, pb, :], CT[:, 0, q, :], xb[:, 0, tsl], start=False, stop=False, tile_position=tp)
                ins = e.matmul(psA[rows, pb, :], CT[:, 1, q, :], xb[:, 1, tsl], start=False, stop=False,
                               tile_position=tp)
                for s1 in range(L):
                    for ri in range(2):
                        ins = e.matmul(pv[:, :, s1], CQ[:, q, ri, s1, :], Sb[:, ri, tb * 64:(tb + 1) * 64],
                                       start=False, stop=(ri == 1), tile_position=tp,
                                       skip_group_check=True)
                return ins
            P.op("pe", ymm, reads=[b_l, b_s, b_cq, b_sb, b_u], writes=[bA[pb]])

        def stCevac(q):
            i, q4 = q // 4, q % 4
            rows = slice(32 * q4, 32 * q4 + 32)
            for tb in range(8):
                tsl = slice(tb * 512, (tb + 1) * 512)
                pb = (pcb[1] + tb) % 4
                act(zT[rows, i, tsl], psA[rows, pb, :], AF.Identity, [bA[pb], b_c], [b_z], bias=C("zero")[rows])
                if tb + 4 < 8:
                    ymm_emit(q, tb + 4)
            pcb[1] += 8
            if q4 == 3:
                def gelu_prep(tb):
                    tsl = slice(tb * 512, (tb + 1) * 512)
                    yb, t1, bt1 = zT[:, i, tsl], ysc[:, tb % 2, :], b_g2[tb % 2]
                    tt("dve", t1, yb, yb, ALU.mult, [b_z], [bt1])
                    ts("dve", t1, t1, 0.044715, 1.0, ALU.mult, ALU.add, [bt1], [bt1])
                    tt("dve", t1, t1, yb, ALU.mult, [bt1, b_z], [bt1])
                    act(t1, t1, AF.Sigmoid, [bt1, b_c], [bt1], scale=2.0 * math.sqrt(2.0 / math.pi))

                gelu_prep(0)
                for tb in range(8):
                    if tb + 1 < 8:
                        gelu_prep(tb + 1)
                    tsl = slice(tb * 512, (tb + 1) * 512)
                    yb = zT[:, i, tsl]
                    tt("dve", yb, yb, ysc[:, tb % 2, :], ALU.mult, [b_g2[tb % 2], b_z], [b_z])

        stA(0)
        stB1(0)
        stB2(0)
        for q in range(16):
            if q + 1 < 16:
                stA(q + 1)
            stCconv(q)
            for _ in range(2):
                if ada_rest:
                    ada_issue(*ada_rest.pop(0))
            if q + 1 < 16:
                stB1(q + 1)
            stCevac(q)
            if q + 1 < 16:
                stB2(q + 1)
            if q >= 12:
                for _ in range(2):
                    if ada_rest:
                        ada_issue(*ada_rest.pop(0))
        P.op("dve", memset(cst[:, 10:11], 0.0), reads=[], writes=[b_t, b_sb, b_cq, b_w[1], b_w[0], bC] + bCs + b_g2 + sum(bCH, []))

        dump(3, zT.rearrange("p k t -> p (k t)"), [b_z], bf=True)
        b_zo = Buf("z_ob")
        for a in range(2):
            P.dma("sync", lambda e, a=a: e.dma_start(out=z_ibs[a].ap().rearrange("(j p) t -> p j t", p=128),
                                                      in_=zT[:, 2 * a:2 * a + 2, :]), reads=[b_z], writes=[b_zo])
        for a in range(2):
            P.cc(lambda e, a=a: e.collective_compute("AllGather", ALU.bypass, replica_groups=PAIRS,
                                                     ins=[z_ibs[a][:, :]], outs=[z_obs[a][:, :]]),
                 reads=[b_zo], writes=[b_zo])
        while ada_rest:
            ada_piece(*ada_rest.pop(0))
        ada_evac(16, 96)
        mk_modA(0, 1)
        mk_modA(1, 0)
        mk_modA(1, 1)
        for k in range(8):
            P.dma("sync", lambda e, k=k: e.dma_start(out=A_x[:, k, :], in_=xT_own[:, k, :]),
                  reads=[], writes=[bX[0][0], bX[0][1], bX[1][0], bX[1][1]] + b_x[k])
        zc = A_t[:].rearrange("p (k c) -> p k c", k=8)
        zc2 = A_s[:].bitcast(BF)[:, 0:4096].rearrange("p (k c) -> p k c", k=8)
        for tb in range(4):
            for dstz, off, bz in ((zc, 0, b_t), (zc2, NTOK, b_s)):
                for a in range(2):
                    for r in range(2):
                        srcz = z_obs[a][r * 256:(r + 1) * 256, off + tb * 512:off + (tb + 1) * 512].rearrange(
                            "(j p) t -> p j t", p=128)
                        t0 = r * 4 + a * 2
                        P.dma("sync", lambda e, d_=dstz[:, t0:t0 + 2, :], s_=srcz: e.dma_start(out=d_, in_=s_),
                              reads=[b_zo], writes=[bz])
            hv = A_h[:, :, tb * 512:(tb + 1) * 512]
            ts("dve", hv, zc, selt[:, 0:1], None, ALU.mult, ALU.bypass, [b_t, b_sm, b_u], [b_h[tb], b_u])
            stt(hv, zc2, selt[:, 1:2], hv, ALU.mult, ALU.add, [b_s, b_sm], [b_h[tb]])

        wq_cnt = [0]

        def load_w(dst, src, bw):
            P.dma("pool", lambda e: e.dma_start(out=dst, in_=src), writes=[bw])

        for j in range(2):
            wi = wq_cnt[0] % 2
            wq_cnt[0] += 1
            W = A_w[wi][:].rearrange("p (k c) -> p k c", k=8)
            load_w(W[:, :, 0:512], w_glu[:, j * 512:(j + 1) * 512].rearrange("(k p) c -> p k c", p=128), b_w[wi])
            load_w(W[:, :, 512:1024], w_glu[:, 1024 + j * 512:1024 + (j + 1) * 512].rearrange("(k p) c -> p k c", p=128),
                   b_w[wi])
            for tb in range(4):
                tsl = slice(tb * 512, (tb + 1) * 512)
                for t in range(4):
                    n = 4 * j + t
                    pb = pcnt % 4
                    pcnt += 1
                    mm(psA[:, pb, :], [(W[:, k, 512 + t * 128:512 + (t + 1) * 128], A_h[:, k, tsl]) for k in range(8)],
                       [b_w[wi], b_h[tb]], [bA[pb]])
                    sg = ysc[:, 0, :]
                    act(sg, psA[:, pb, :], AF.Sigmoid, [bA[pb], b_c], [b_y])
                    pb2 = pcnt % 4
                    pcnt += 1
                    mm(psA[:, pb2, :], [(W[:, k, t * 128:(t + 1) * 128], A_h[:, k, tsl]) for k in range(8)],
                       [b_w[wi], b_h[tb]], [bA[pb2]])
                    y = ysc[:, 1, :]
                    tt("dve", y, psA[:, pb2, :], sg, ALU.mult, [bA[pb2], b_y], [b_y])
                    stt(A_x[:, n, tsl], y, gate(0, 0)[:, n:n + 1], A_x[:, n, tsl], ALU.mult, ALU.add,
                        [b_y, b_mod, b_x[n][tb]], [b_x[n][tb]])

        b_tm = [Buf(), Buf(), Buf(), Buf()]
        b_rr = Buf("rstd")

        def norm_mod(l, s_):
            P.op("dve", memset(cst[:, 15:16], 0.0), reads=[], writes=[b_s, b_rr] + b_tm)
            for tb in range(4):
                tsl = slice(tb * 512, (tb + 1) * 512)
                xr = [b_x[k][tb] for k in range(8)]
                r = rstd_block(A_x[:, :, tsl], 8, 1024.0, xr, br=b_rr)
                for k in range(8):
                    tmp = A_s[:, (k % 4) * 512:(k % 4 + 1) * 512]
                    tt("dve", tmp, A_x[:, k, tsl], r, ALU.mult, [b_rr, b_x[k][tb]], [b_tm[k % 4]])
                    act(A_h[:, k, tsl], tmp, AF.Identity, [b_tm[k % 4], b_mod], [b_h[tb]],
                        bias=shift(l, s_)[:, k:k + 1], scale=modA[:, l, s_, k:k + 1])
            P.op("dve", memset(cst[:, 15:16], 0.0), reads=[], writes=[b_s, b_rr] + b_tm)

        bAT = [Buf(), Buf()]

        b_ys = [Buf(), Buf(), Buf()]

        def mlp(l):
            nonlocal pcnt
            aT = A_g[:, 0:4096].rearrange("p (a f c) -> p a f c", a=2, f=4)
            first = [True]
            Wv_ = {}
            P.op("dve", memset(cst[:, 12:13], 0.0), reads=[], writes=[b_t] + b_ys)

            def s1(n):
                nonlocal pcnt
                e8, tb = n // 4, n % 4
                if tb == 0:
                    wi = wq_cnt[0] % 2
                    wq_cnt[0] += 1
                    W1 = A_w[wi][:, 0:4096].rearrange("p (k c) -> p k c", k=8)
                    W2 = A_w[wi][:, 4096:8192].rearrange("p (k c) -> p k c", k=4)
                    load_w(W1, w_ff1[l, :, e8 * 512:(e8 + 1) * 512].rearrange("(k p) c -> p k c", p=128), b_w[wi])
                    load_w(W2, w_ff2[l, e8 * 512:(e8 + 1) * 512, :].rearrange("(k p) c -> p k c", p=128), b_w[wi])
                    Wv_[e8] = (wi, W1, W2)
                wi, W1, W2 = Wv_[e8]
                tsl = slice(tb * 512, (tb + 1) * 512)
                ap_ = n % 2
                for f in range(4):
                    pb = pcnt % 4
                    pcnt += 1
                    mm(psA[:, pb, :], [(W1[:, k, f * 128:(f + 1) * 128], A_h[:, k, tsl]) for k in range(8)],
                       [b_w[wi], b_h[tb]], [bA[pb]])
                    sl3 = (4 * n + f) % 3
                    rl = ysc[:, sl3, :]
                    act(rl, psA[:, pb, :], AF.Relu, [bA[pb], b_c], [b_ys[sl3]])
                    extra = [b_g, b_z] if first[0] else []
                    first[0] = False
                    tt("dve", aT[:, ap_, f, :], rl, rl, ALU.mult, [b_ys[sl3]], [bAT[ap_]] + extra)

            def s2(n):
                nonlocal pcnt
                e8, tb = n // 4, n % 4
                wi, W1, W2 = Wv_[e8]
                tsl = slice(tb * 512, (tb + 1) * 512)
                ap_ = n % 2
                for nn in range(8):
                    pb = pcnt % 2
                    pcnt += 1
                    mm(psB[:, pb, :], [(W2[:, f, nn * 128:(nn + 1) * 128], aT[:, ap_, f, :]) for f in range(4)],
                       [b_w[wi], bAT[ap_]], [bB[pb]])
                    stt(A_x[:, nn, tsl], psB[:, pb, :], gate(l, 1)[:, nn:nn + 1], A_x[:, nn, tsl], ALU.mult, ALU.add,
                        [bB[pb], b_mod, b_x[nn][tb]], [b_x[nn][tb]])

            s1(0)
            for n in range(32):
                if n + 1 < 32:
                    s1(n + 1)
                s2(n)
            P.op("dve", memset(cst[:, 12:13], 0.0), reads=[], writes=[b_t] + b_ys)

        allx = [b_x[k][tb] for k in range(8) for tb in range(4)]
        dump(4, (outT, A_x[:]), allx)
        norm_mod(0, 1)
        mlp(0)
        dump(5, (outT, A_x[:]), allx)

        norm_mod(1, 0)
        Wq = A_w[0][:, 0:4096].rearrange("p (k c) -> p k c", k=8)
        Wk = A_w[0][:, 4096:8192].rearrange("p (k c) -> p k c", k=8)
        Wv = A_w[1][:].rearrange("p (k c) -> p k c", k=8)
        Wr = A_g[:, 0:8192].rearrange("p (k c) -> p k c", k=8)
        Wo = A_g[:, 8192:16384].rearrange("p (k c) -> p k c", k=8)
        Wg = sb("Wg", [128, 8, 16], BF)
        win = lambda a, b: w_in[:, a:b].rearrange("(k p) c -> p k c", p=128)
        load_w(Wq, win(0, 512), b_w[0])
        load_w(Wk, win(512, 1024), b_w[0])
        load_w(Wv, win(1024, 2048), b_w[1])
        load_w(Wg[:], win(2048, 2064), b_g)
        P.dma("pool", lambda e: e.dma_start(out=Wr, in_=win(2064, 3088)), writes=[b_g, bAT[0], bAT[1]])
        load_w(Wo, w_out.rearrange("(k p) c -> p k c", p=128), b_g)

        S = sb("S", [128, 4, 256])
        Sbf = sb("Sbf", [128, 1, 4, 256], BF)
        b_S = Buf("S")
        b_Sbf = [Buf(), Buf()]
        Sbfs = [Sbf[:, 0],
                LP[:].rearrange("p a b c -> p (a b c)").bitcast(BF)[:, 0:1024].rearrange("p (h e) -> p h e", h=4)]
        gl = sb("gl", [16, 128], BF)
        la = A_s[:, 0:512]
        Ef = A_s[:, 512:1024]
        gsc = A_s[:, 1024:2048].rearrange("p (k t) -> p k t", k=8)
        rsd = A_s[:, 2048:2560].rearrange("p (k t) -> p k t", k=4)
        asb = A_s[:].bitcast(BF)
        kdec = [A_t[:, 0:512], asb[:, 5376:5888]]
        vtok = [A_t[:, 512:1536], asb[:, 5888:6912]]
        qTt = [A_t[:, 3584:4096].rearrange("p (k t) -> p k t", k=4),
               asb[:, 6912:7424].rearrange("p (k t) -> p k t", k=4)]
        srs = [BT[:].rearrange("p a b c -> p (a b c)").rearrange("p (k t) -> p k t", k=8),
               CT[:].rearrange("p a b c -> p (a b c)").rearrange("p (k t) -> p k t", k=8)]
        osq = A_t[:, 1536:2560].rearrange("p (k t) -> p k t", k=8)
        gat = A_t[:, 2560:3584].rearrange("p (k t) -> p k t", k=8)
        dec = sb("dec", [128, 2, 4, 2])
        b_gl, b_la, b_gs, b_osq, b_rsd, b_gat = [Buf() for _ in range(6)]
        b_kd, b_v, b_dec, b_q, b_sr = [[Buf(), Buf()] for _ in range(5)]
        P.op("dve", memset(S[:], 0.0), writes=[b_S])

        def gla_front(tt_, final):
            st = tt_ % 2
            tk = slice(tt_ * 128, (tt_ + 1) * 128)
            tb = tt_ // 4
            hT = lambda k: A_h[:, k, tk]
            mm(psB[0:16, 0, 0:128], [(Wg[:, k, :], hT(k)) for k in range(8)], [b_g, b_h[tb]], [bB[0]])
            act(gl[:], psB[0:16, 0, 0:128], AF.Identity, [bB[0], b_c], [b_gl], bias=C("zero")[0:16])
            mm(psA[:, 0, :], [(ones_bf[0:1, :], bgb[:]), (gl[:], wg2[:])], [b_c, b_sm, b_gl], [bA[0]])
            act(Ef, psA[:, 0, :], AF.Exp, [bA[0], b_c], [b_la], scale=-1.0)
            act(la, Ef, AF.Ln, [b_la, b_c], [b_la], bias=C("one"))
            mm(psA[:, 1, :], [(umt[:], la)], [b_sm, b_la], [bA[1]])
            for h in range(4):
                mm(psB[:, 1, 2 * h:2 * h + 2], [(la[:, h * 128:(h + 1) * 128], indt[:])], [b_la, b_sm], [bB[1]])
            mm(psA[:, 2, :], [(hT(k), Wk[:, k, :]) for k in range(8)], [b_w[0], b_h[tb]], [bA[2]])
            act(Ef, psA[:, 1, :], AF.Exp, [bA[1], b_c], [b_la], scale=-1.0 / 16.0)
            act(dec[:, st].rearrange("p h c -> p (h c)"), psB[:, 1, 0:8], AF.Exp, [bB[1], b_c], [b_dec[st]],
                scale=-1.0 / 16.0)
            tt("dve", kdec[st], psA[:, 2, :], Ef, ALU.mult, [bA[2], b_la], [b_kd[st]])
            for hh in range(2):
                mm(psA[:, 3, :], [(hT(k), Wv[:, k, hh * 512:(hh + 1) * 512]) for k in range(8)],
                   [b_w[1], b_h[tb]], [bA[3]])
                act(vtok[st][:, hh * 512:(hh + 1) * 512], psA[:, 3, :], AF.Identity, [bA[3], b_c], [b_v[st]])
            if final:
                for h in range(4):
                    mm(psB[:, 0, h * 128:(h + 1) * 128], [(Wq[:, k, h * 128:(h + 1) * 128], hT(k)) for k in range(8)],
                       [b_w[0], b_h[tb]], [bB[0]])
                act(qTt[st].rearrange("p k t -> p (k t)"), psB[:, 0, :], AF.Identity, [bB[0], b_c], [b_q[st]],
                    scale=128.0 ** -0.5)
                pr = psA[:, 0:2, :].rearrange("p a c -> p (a c)")
                for t8 in range(8):
                    mm(pr[:, t8 * 128:(t8 + 1) * 128], [(Wr[:, k, t8 * 128:(t8 + 1) * 128], hT(k)) for k in range(8)],
                       [b_g, b_h[tb]], [bA[0], bA[1]])
                act(srs[st].rearrange("p k t -> p (k t)"), pr, AF.Silu, [bA[0], bA[1], b_c], [b_sr[st]])

        def gla_back(tt_, final):
            st = tt_ % 2
            tk = slice(tt_ * 128, (tt_ + 1) * 128)
            tb = tt_ // 4
            for cc in range(2):
                rws = slice(64 * cc, 64 * cc + 64)
                for h in range(4):
                    mm(psC[:, h * 256:(h + 1) * 256],
                       [(kdec[st][rws, h * 128:(h + 1) * 128], vtok[st][rws, h * 256:(h + 1) * 256])],
                       [b_kd[st], b_v[st]], [bC], tp=(64 * cc, 0))
                for h in range(4):
                    stt(S[:, h, :], S[:, h, :], dec[:, st, h, cc:cc + 1], psC[:, h * 256:(h + 1) * 256], ALU.mult,
                        ALU.add, [b_S, b_dec[st], bC], [b_S])
                if final:
                    act(Sbfs[cc].rearrange("p h e -> p (h e)"), S[:].rearrange("p h e -> p (h e)"), AF.Identity,
                        [b_S, b_c], [b_Sbf[cc]])
            if final:
                for cc in range(2):
                    for h in range(4):
                        for e2 in range(2):
                            t8 = 2 * h + e2
                            mm(psB[:, 1, t8 * 64:(t8 + 1) * 64],
                               [(Sbfs[cc][:, h, e2 * 128:(e2 + 1) * 128], qTt[st][:, h, 64 * cc:64 * cc + 64])],
                               [b_Sbf[cc], b_q[st]], [bB[1]])
                    act(gsc[:, :, 64 * cc:64 * cc + 64], psB[:, 1, :].rearrange("p (k t) -> p k t", k=8),
                        AF.Identity, [bB[1], b_c], [b_gs])
            if final:
                act(osq, gsc, AF.Square, [b_gs, b_c], [b_osq])
                for h in range(4):
                    mm(psB[:, 0, h * 128:(h + 1) * 128],
                       [(ones_bf[:], osq[:, 2 * h, :]), (ones_bf[:], osq[:, 2 * h + 1, :])], [b_osq, b_c], [bB[0]])
                ts("dve", A_s[:, 2048:2560], psB[:, 0, :], 1.0 / 256.0, EPS, ALU.mult, ALU.add, [bB[0]], [b_rsd])
                act(A_s[:, 2048:2560], A_s[:, 2048:2560], AF.Sqrt, [b_rsd], [b_rsd])
                P.op("dve", lambda e: e.reciprocal(out=A_s[:, 2048:2560], in_=A_s[:, 2048:2560]),
                     reads=[b_rsd], writes=[b_rsd])
                for t8 in range(8):
                    stt(gsc[:, t8, :], gsc[:, t8, :], gnc[:, t8:t8 + 1], rsd[:, t8 // 2, :], ALU.mult, ALU.mult,
                        [b_gs, b_sm, b_rsd], [b_gs])
                tt("dve", gat, gsc, srs[st], ALU.mult, [b_gs, b_sr[st]], [b_gat])
                po = psA[:, 2:4, :].rearrange("p a c -> p (a c)")
                for n in range(8):
                    mm(po[:, n * 128:(n + 1) * 128], [(Wo[:, k, n * 128:(n + 1) * 128], gat[:, k, :]) for k in range(8)],
                       [b_g, b_gat], [bA[2], bA[3]])
                for n in range(8):
                    stt(A_x[:, n, tk], po[:, n * 128:(n + 1) * 128], gate(1, 0)[:, n:n + 1], A_x[:, n, tk],
                        ALU.mult, ALU.add, [bA[2], bA[3], b_mod, b_x[n][tb]], [b_x[n][tb]])

        def gla_pass(final, first_front_done=False):
            if not first_front_done:
                gla_front(0, final)
            for tt_ in range(16):
                if tt_ + 1 < 16:
                    gla_front(tt_ + 1, final)
                gla_back(tt_, final)

        gla_tmp = [b_s, b_t, b_gl, b_la, b_gs, b_osq, b_rsd, b_gat, b_sm, b_l] + sum([b_kd, b_v, b_dec, b_q, b_sr], [])
        bar = sb("bar", [128, 1])

        def barrier(bufs):
            P.op("dve", memset(bar[:], 0.0), writes=bufs)

        barrier(gla_tmp)
        gla_pass(False)
        b_so = Buf("s_ob")
        P.dma("sync", lambda e: e.dma_start(out=s_ib[:, :], in_=S[:].rearrange("p h e -> p (h e)")), reads=[b_S], writes=[b_so])
        P.cc(lambda e: e.collective_compute("AllGather", ALU.bypass, replica_groups=PAIRS,
                                            ins=[s_ib[:, :]], outs=[s_ob[:, :]]), reads=[b_so], writes=[b_so])
        P.dma("sync", lambda e: e.dma_start(out=S[:].rearrange("p h e -> p (h e)"), in_=s_ob[0:128, :]),
              reads=[b_so], writes=[b_S])
        gla_front(0, True)
        ts("dve", S[:].rearrange("p h e -> p (h e)"), S[:].rearrange("p h e -> p (h e)"), selt[:, 1:2], None,
           ALU.mult, ALU.bypass, [b_S, b_sm], [b_S])
        gla_pass(True, first_front_done=True)
        barrier(gla_tmp)
        dump(6, (outT, A_x[:]), allx)

        norm_mod(1, 1)
        mlp(1)
        dump(7, (outT, A_x[:]), allx)

        b_out = Buf("out")
        for tb in range(4):
            tsl = slice(tb * 512, (tb + 1) * 512)
            xr = [b_x[k][tb] for k in range(8)]
            r = rstd_block(A_x[:, :, tsl], 8, 1024.0, xr)
            for k in range(8):
                stt(A_x[:, k, tsl], A_x[:, k, tsl], nfin[:, k:k + 1], r, ALU.mult, ALU.mult,
                    [b_s, b_sm, b_x[k][tb]], [b_x[k][tb]])
            P.dma("sync", lambda e, tsl=tsl: e.dma_start(out=outT[:, :, tsl], in_=A_x[:, :, tsl]),
                  reads=xr, writes=[b_out])
        P.final_wait("sync", [b_out])
    try:
        body()
    except _Stop:
        pass
    P.emit(nc, es)
    es.close()
    return nc


_NC = None
_LAST = None


def _tiles(a2d):
    C, T = a2d.shape
    return np.ascontiguousarray(a2d.reshape(C // 128, 128, T).transpose(1, 0, 2))


def _col(v):
    return np.ascontiguousarray(v.reshape(-1, 128).T)


def kernel(x, c, w_ada, b_ada, norm_mix, norm_mlp, s5_a_re, s5_a_im, s5_log_dt, s5_b_re, s5_b_im,
           s5_c_re, s5_c_im, s5_d, s5_w_glu, gla_w_in, gla_w_gate2, gla_b_gate, gla_g_norm, gla_w_out,
           w_ff1, w_ff2, norm_final):
    global _NC
    f = lambda a: np.ascontiguousarray(np.asarray(a, dtype=np.float32))
    x, c = f(x), f(c)
    if _NC is None:
        _NC = build()
    umat = np.zeros((128, 128), np.float32)
    for s in range(128):
        for s2 in range(s + 1, (s // 64 + 1) * 64):
            umat[s2, s] = 1.0
    ind = np.zeros((128, 2), np.float32)
    ind[:64, 0] = 1.0
    ind[64:, 1] = 1.0
    common = {
        "w_ada": f(w_ada),
        "b_ada_col": np.ascontiguousarray(np.stack([_col(f(b_ada)[l]) for l in range(2)], axis=1)),
        "nmix_col": np.ascontiguousarray(np.stack([_col(f(norm_mix)[l]) for l in range(2)], axis=1)),
        "nmlp_col": np.ascontiguousarray(np.stack([_col(f(norm_mlp)[l]) for l in range(2)], axis=1)),
        "nfin_col": _col(f(norm_final)), "gn_col": _col(f(gla_g_norm)[0]),
        "w_glu": f(s5_w_glu)[0], "w_in": f(gla_w_in)[0], "w_g2": f(gla_w_gate2)[0], "b_gate": f(gla_b_gate),
        "w_out": f(gla_w_out)[0], "w_ff1": f(w_ff1), "w_ff2": f(w_ff2), "umat": umat, "ind": ind,
        "eye32": np.ascontiguousarray(np.tile(np.eye(32, dtype=np.float32), (4, 1))),
    }
    are, aim, ldt = f(s5_a_re)[0], f(s5_a_im)[0], f(s5_log_dt)[0]
    bre, bim, cre, cim, dsk = f(s5_b_re)[0], f(s5_b_im)[0], f(s5_c_re)[0], f(s5_c_im)[0], f(s5_d)[0]
    in_maps = []
    for core in range(8):
        b, hf = core // 2, core % 2
        g0 = 32 * hf
        xT = np.ascontiguousarray(x[b].T)
        order = list(range(4 * hf, 4 * hf + 4)) + list(range(4 * (1 - hf), 4 * (1 - hf) + 4))
        xfull = _tiles(xT)[:, order, :]
        st = lambda a: np.ascontiguousarray(a[g0:g0 + 32].reshape(16, 128).T)
        bexp = []
        for bsrc in (bre, bim):
            t = np.zeros((128, 4, 128), np.float32)
            for gl_ in range(32):
                i, g8 = gl_ // 8, gl_ % 8
                g2 = gl_ % 2
                t[g8 * 16:(g8 + 1) * 16, i, g2 * 64:(g2 + 1) * 64] = bsrc[g0 + gl_].T
            bexp.append(t)
        cexp = []
        for csrc in (cre, cim):
            t = np.zeros((128, 16, 32), np.float32)
            for gl_ in range(32):
                q, g2 = gl_ // 2, gl_ % 2
                t[g2 * 64:(g2 + 1) * 64, q, g2 * 16:(g2 + 1) * 16] = csrc[g0 + gl_].T
            cexp.append(t)
        sel = np.zeros((128, 2), np.float32)
        sel[:, hf] = 1.0
        m = dict(common)
        m.update({
            "xT_own": np.ascontiguousarray(_tiles(xT)[:, :, hf * NTOK:(hf + 1) * NTOK]),
            "xT_full": np.ascontiguousarray(xfull),
            "c_col": _col(c[b]),
            "a_re_st": st(are), "a_im_st": st(aim),
            "ldt_st": np.ascontiguousarray(np.repeat(ldt[g0:g0 + 32].reshape(16, 2).T, 64, axis=0)),
            "bexp_re": bexp[0], "bexp_im": bexp[1], "cexp_re": cexp[0], "cexp_im": cexp[1],
            "d_col": _col(dsk[512 * hf:512 * hf + 512]), "sel": sel,
        })
        in_maps.append(m)
    res = run_bass_kernel_spmd(_NC, in_maps, core_ids=list(range(8)))
    global _LAST
    _LAST = res
    out = np.empty((4, SEQ, 1024), np.float32)
    for core in range(8):
        b, hf = core // 2, core % 2
        o = np.asarray(res.results[core]["outT"])
        out[b, hf * NTOK:(hf + 1) * NTOK, :] = o.transpose(1, 0, 2).reshape(1024, NTOK).T
    return out
```

```python
import math
from contextlib import ExitStack
import numpy as np
import concourse.bass as bass
import concourse.mybir as mybir
from concourse.bass_utils import run_bass_kernel_spmd

F32 = mybir.dt.float32
BF = mybir.dt.bfloat16
ALU = mybir.AluOpType
AF = mybir.ActivationFunctionType
NTOK = 2048
SEQ = 4096
EPS = 1e-6
PAIRS = [[0, 1], [2, 3], [4, 5], [6, 7]]
_DBG_NOCC = 0
_DBG_STAGE = 0


class _Stop(Exception):
    pass


class Buf:
    def __init__(self, name=""):
        self.name = name
        self.w = None
        self.r = {}


class Prog:
    ENGS = ["pe", "act", "dve", "pool", "sync"]

    def __init__(self, ndma=48):
        self.ops = {e: [] for e in self.ENGS}
        self.cnt = {e: 0 for e in self.ENGS}
        self.waited = {}
        self.ndma = ndma
        self.dma_use = [0] * ndma
        self.pools = {"sync": list(range(0, ndma - 16)), "pool": list(range(ndma - 16, ndma))}
        self.rrq = {"sync": 0, "pool": 0}

    def _wait(self, eng, dep):
        key, val = dep
        if eng == "pe" and key == "pe":
            return
        if self.waited.get((eng, key), 0) >= val:
            return
        self.waited[(eng, key)] = val
        self.ops[eng].append(("wait", key, val))

    def _deps(self, eng, reads, writes):
        for b in reads:
            if b.w is not None:
                self._wait(eng, b.w)
        for b in writes:
            if b.w is not None:
                self._wait(eng, b.w)
            for k, v in b.r.items():
                self._wait(eng, (k, v))

    def _mark(self, me, reads, writes):
        for b in reads:
            b.r[me[0]] = max(b.r.get(me[0], 0), me[1])
        for b in writes:
            b.w = me
            b.r = {}

    def op(self, eng, fn, reads=(), writes=()):
        self._deps(eng, reads, writes)
        self.cnt[eng] += 1
        self.ops[eng].append(("op", fn))
        self._mark((eng, self.cnt[eng]), reads, writes)

    def dma(self, q, fn, reads=(), writes=()):
        pl = self.pools[q]
        i = pl[self.rrq[q] % len(pl)]
        self.rrq[q] += 1
        if self.dma_use[i] > 0:
            self._wait(q, (("d", i), 16 * self.dma_use[i]))
        self._deps(q, reads, writes)
        self.dma_use[i] += 1
        self.ops[q].append(("dma", fn, i))
        self._mark((("d", i), 16 * self.dma_use[i]), reads, writes)

    def cc(self, fn, reads=(), writes=()):
        if _DBG_NOCC:
            return
        self._deps("pool", reads, writes)
        self.cnt.setdefault("cc", 0)
        self.cnt["cc"] += 1
        self.ops["pool"].append(("cc", fn))
        self._mark(("cc", self.cnt["cc"]), reads, writes)

    def final_wait(self, eng, bufs):
        for b in bufs:
            if b.w is not None:
                self._wait(eng, b.w)

    def emit(self, nc, es):
        sems = {e: es.enter_context(nc.semaphore("s_" + e)) for e in ["pe", "act", "dve", "pool", "cc"]}
        dsem = [es.enter_context(nc.semaphore("d%d" % i)) for i in range(self.ndma)]

        def sem_of(key):
            return dsem[key[1]] if isinstance(key, tuple) else sems[key]

        block = es.enter_context(nc.Block())

        def run(name, eng):
            for o in self.ops[name]:
                if o[0] == "wait":
                    eng.wait_ge(sem_of(o[1]), o[2])
                elif o[0] == "op":
                    o[1](eng).then_inc(sems[name], 1)
                elif o[0] == "dma":
                    o[1](eng).then_inc(dsem[o[2]], 16)
                elif o[0] == "cc":
                    o[1](eng).then_inc(sems["cc"], 1)

        @block.tensor
        def _(e):
            run("pe", e)

        @block.scalar
        def _(e):
            run("act", e)

        @block.vector
        def _(e):
            run("dve", e)

        @block.gpsimd
        def _(e):
            run("pool", e)

        @block.sync
        def _(e):
            run("sync", e)


def build():
    nc = bass.Bass("TRN2", target_bir_lowering=False)
    es = ExitStack()
    P = Prog()

    def din(name, shape, dt=F32):
        return nc.dram_tensor(name, list(shape), dt, kind="ExternalInput").ap()

    xT_own = din("xT_own", [128, 8, NTOK])
    xT_full = din("xT_full", [128, 8, SEQ])
    c_col = din("c_col", [128, 8])
    w_ada = din("w_ada", [2, 1024, 6144])
    b_ada_col = din("b_ada_col", [128, 2, 48])
    nmix_col = din("nmix_col", [128, 2, 8])
    nmlp_col = din("nmlp_col", [128, 2, 8])
    nfin_col = din("nfin_col", [128, 8])
    gn_col = din("gn_col", [128, 8])
    a_re_st = din("a_re_st", [128, 16])
    a_im_st = din("a_im_st", [128, 16])
    ldt_st = din("ldt_st", [128, 16])
    bexp_re = din("bexp_re", [128, 4, 128])
    bexp_im = din("bexp_im", [128, 4, 128])
    cexp_re = din("cexp_re", [128, 16, 32])
    cexp_im = din("cexp_im", [128, 16, 32])
    d_col = din("d_col", [128, 4])
    w_glu = din("w_glu", [1024, 2048])
    w_in = din("w_in", [1024, 3088])
    w_g2 = din("w_g2", [16, 512])
    b_gate = din("b_gate", [1, 512])
    w_out = din("w_out", [1024, 1024])
    w_ff1 = din("w_ff1", [2, 1024, 4096])
    w_ff2 = din("w_ff2", [2, 4096, 1024])
    sel = din("sel", [128, 2])
    umat = din("umat", [128, 128])
    ind = din("ind", [128, 2])
    eye32 = din("eye32", [128, 32])
    outT = nc.dram_tensor("outT", [128, 8, NTOK], F32, kind="ExternalOutput").ap()
    z_ibs = [nc.dram_tensor("z_ib%d" % a, [256, SEQ], BF) for a in range(2)]
    z_obs = [nc.dram_tensor("z_ob%d" % a, [512, SEQ], BF) for a in range(2)]
    s_ib = nc.dram_tensor("s_ib", [128, 1024], F32)
    s_ob = nc.dram_tensor("s_ob", [256, 1024], F32)

    def sb(name, shape, dt=F32):
        return es.enter_context(nc.sbuf_tensor(name, list(shape), dt))

    def ps(name, shape, dt=F32):
        return es.enter_context(nc.psum_tensor(name, list(shape), dt))

    A_x = sb("A_x", [128, 8, NTOK])
    A_h = sb("A_h", [128, 8, NTOK], BF)
    A_g = sb("A_g", [128, 16384], BF)
    A_w = [sb("A_w0", [128, 8192], BF), sb("A_w1", [128, 8192], BF)]
    A_s = sb("A_s", [128, 4096])
    A_t = sb("A_t", [128, 4096], BF)
    psA = ps("psA", [128, 4, 512])
    psB = ps("psB", [128, 2, 512])
    psC = ps("psC", [128, 1024])
    bA = [Buf("psA%d" % i) for i in range(4)]
    bB = [Buf("psB%d" % i) for i in range(2)]
    bC = Buf("psC")
    b_x = [[Buf() for _ in range(4)] for _ in range(8)]
    b_h = [Buf() for _ in range(4)]
    b_w = [Buf("w0"), Buf("w1")]
    b_g = Buf("g")
    b_s = Buf("s")
    b_t = Buf("t")

    cst = sb("cst", [128, 16])
    b_c = Buf("cst")
    ones_bf = sb("ones_bf", [128, 128], BF)

    def memset(t, v):
        return lambda e: e.memset(t, v)

    CV = {"one": 1.0, "negpi": -math.pi, "eps": EPS, "zero": 0.0}
    cidx = {k: i for i, k in enumerate(CV)}
    for k, i in cidx.items():
        P.op("dve", memset(cst[:, i:i + 1], CV[k]), writes=[b_c])
    P.op("dve", memset(ones_bf[:], 1.0), writes=[b_c])

    def C(k):
        return cst[:, cidx[k]:cidx[k] + 1]

    small = {}
    b_sm = Buf("small")

    def load_small(name, ap, shape, dt=F32):
        t = sb("sm_" + name, shape, dt)
        P.dma("sync", lambda e, t=t, ap=ap: e.dma_start(out=t[:], in_=ap), writes=[b_sm])
        small[name] = t
        return t

    ccol = sb("sm_ccol", [128, 8])
    b_cc = Buf("ccol")
    P.dma("sync", lambda e: e.dma_start(out=ccol[:], in_=c_col), writes=[b_cc])
    bada = load_small("bada", b_ada_col, [128, 2, 48])
    nmix = load_small("nmix", nmix_col, [128, 2, 8])
    nmlp = load_small("nmlp", nmlp_col, [128, 2, 8])
    nfin = load_small("nfin", nfin_col, [128, 8])
    gnc = load_small("gnc", gn_col, [128, 8])
    arst = load_small("arst", a_re_st, [128, 16])
    aist = load_small("aist", a_im_st, [128, 16])
    ldst = load_small("ldst", ldt_st, [128, 16])
    dcol = load_small("dcol", d_col, [128, 4])
    selt = load_small("selt", sel, [128, 2])
    umt = load_small("umt", umat, [128, 128])
    indt = load_small("indt", ind, [128, 2])
    eyet = load_small("eyet", eye32, [128, 32])
    BT = sb("BT", [128, 2, 4, 128], BF)
    P.dma("pool", lambda e: e.dma_start(out=BT[:, 0], in_=bexp_re), writes=[b_sm])
    P.dma("pool", lambda e: e.dma_start(out=BT[:, 1], in_=bexp_im), writes=[b_sm])
    wg2 = sb("wg2", [16, 512], BF)
    bgb = sb("bgb", [1, 512], BF)
    P.dma("pool", lambda e: e.dma_start(out=wg2[:], in_=w_g2), writes=[b_sm])
    P.dma("pool", lambda e: e.dma_start(out=bgb[:], in_=b_gate), writes=[b_sm])

    def tt(eng, out, a, b, op, reads, writes):
        P.op(eng, lambda e: e.tensor_tensor(out=out, in0=a, in1=b, op=op), reads=reads, writes=writes)

    def ts(eng, out, a, s1, s2, op0, op1, reads, writes):
        if s2 is None:
            P.op(eng, lambda e: e.tensor_scalar(out=out, in0=a, scalar1=s1, scalar2=None, op0=op0),
                 reads=reads, writes=writes)
            return
        P.op(eng, lambda e: e.tensor_scalar(out=out, in0=a, scalar1=s1, scalar2=s2, op0=op0, op1=op1),
             reads=reads, writes=writes)

    def stt(out, a, s, b, op0, op1, reads, writes):
        P.op("dve", lambda e: e.scalar_tensor_tensor(out=out, in0=a, scalar=s, in1=b, op0=op0, op1=op1),
             reads=reads, writes=writes)

    def act(out, a, func, reads, writes, bias=None, scale=1.0):
        kw = {} if bias is None else {"bias": bias}
        P.op("act", lambda e: e.activation(out=out, in_=a, func=func, scale=scale, **kw),
             reads=reads, writes=writes)

    def mm(out, pairs, reads, writes, tp=None):
        def fn(e):
            n = len(pairs)
            ins = None
            for j, (l, r) in enumerate(pairs):
                kw = {} if tp is None else {"tile_position": tp}
                ins = e.matmul(out, l, r, start=(j == 0), stop=(j == n - 1), **kw)
            return ins
        P.op("pe", fn, reads=reads, writes=writes)

    def dump(stage, src, bufs, bf=False):
        if _DBG_STAGE != stage:
            return
        bo = Buf("dbg")
        if bf:
            v = A_x[:].rearrange("p k t -> p (k t)")
            P.op("act", lambda e: e.activation(out=v, in_=src, func=AF.Identity), reads=bufs, writes=[bo])
            P.dma("sync", lambda e: e.dma_start(out=outT, in_=A_x[:]), reads=[bo], writes=[bo])
        else:
            P.dma("sync", lambda e: e.dma_start(out=src[0], in_=src[1]), reads=bufs, writes=[bo])
        P.final_wait("sync", [bo])
        raise _Stop()

    def body():
        cs = sb("cs", [128, 8])
        b_cs = Buf("cs")
        act(cs[:], ccol[:], AF.Silu, [b_cc, b_c], [b_cs])
        modc = sb("modc", [128, 2, 48])
        b_mod = Buf("mod")
        pcount = [0]

        bCs = []
        ada_n = [0]
        b_ada2 = Buf("ada2")

        def ada_issue(l, j):
            wi = 0
            half = ada_n[0] % 2 if ada_n[0] < 8 else 0
            ada_n[0] += 1
            bw_ = b_w[0] if half == 0 else b_ada2
            wv = A_w[wi][:].bitcast(F32)[:, 2048 * half:2048 * (half + 1)].rearrange("p (k c) -> p k c", k=8)
            src = w_ada[l, :, j * 256:(j + 1) * 256].rearrange("(k p) c -> p k c", p=128)
            P.dma("sync", lambda e, wv=wv, src=src: e.dma_start(out=wv, in_=src), writes=[bw_])
            for t in range(2):
                col = l * 48 + 2 * j + t
                mm(psC[:, col:col + 1],
                   [(wv[:, k, t * 128:(t + 1) * 128], cs[:, k:k + 1]) for k in range(8)],
                   [bw_, b_cs], [bC])

        def ada_evac(c0, c1):
            tt("dve", modc[:].rearrange("p l c -> p (l c)")[:, c0:c1], psC[:, c0:c1],
               bada[:].rearrange("p l c -> p (l c)")[:, c0:c1], ALU.add, [bC, b_sm], [b_mod])

        def ada_evac_all():
            pass

        def ada_piece(l, j):
            ada_issue(l, j)

        ada_rest = [(0, j) for j in range(8, 24)] + [(1, j) for j in range(24)]
        for j in range(8):
            ada_piece(0, j)
        ada_evac(0, 16)
        if _DBG_STAGE == 1:
            for lj in ada_rest:
                ada_piece(*lj)
            ada_rest = []
            ada_evac(16, 96)
        dump(1, (outT[:, 0, 0:96], modc[:].rearrange("p l c -> p (l c)")), [b_mod])
        modA = sb("modA", [128, 2, 2, 8])

        def mk_modA(l, s_):
            nrm = nmix if s_ == 0 else nmlp
            stt(modA[:, l, s_, :], modc[:, l, 24 * s_ + 8:24 * s_ + 16], 1.0, nrm[:, l, :], ALU.add, ALU.mult,
                [b_mod, b_sm], [b_mod])

        mk_modA(0, 0)

        def shift(l, s_):
            return modc[:, l, 24 * s_:24 * s_ + 8]

        def gate(l, s_):
            return modc[:, l, 24 * s_ + 16:24 * s_ + 24]

        def rstd_block(xin, nt, div, xreads, tbw=512, br=None):
            br = b_s if br is None else br
            sq = A_t[:, 0:nt * tbw].rearrange("p (k c) -> p k c", k=nt)
            act(sq, xin, AF.Square, xreads, [b_t])
            pb = psB[:, 0, 0:tbw]
            mm(pb, [(ones_bf[:], sq[:, k, :]) for k in range(nt)], [b_t, b_c], [bB[0]])
            r = A_s[:, 3584:3584 + tbw]
            ts("dve", r, pb, 1.0 / div, EPS, ALU.mult, ALU.add, [bB[0]], [br])
            act(r, r, AF.Sqrt, [br], [br])
            P.op("dve", lambda e: e.reciprocal(out=r, in_=r), reads=[br], writes=[br])
            return r

        myA = sb("myA", [128, 4])
        myB = sb("myB", [128, 4])
        for dst, srcv in ((myA, modA[:, 0, 0, :]), (myB, shift(0, 0))):
            ts("dve", dst[:], srcv[:, 0:4], selt[:, 0:1], None, ALU.mult, ALU.bypass, [b_mod, b_sm], [b_mod])
            stt(dst[:], srcv[:, 4:8], selt[:, 1:2], dst[:], ALU.mult, ALU.add, [b_mod, b_sm], [b_mod])

        uT = A_h[:].rearrange("p k t -> p (k t)").rearrange("p (k t) -> p k t", k=4)
        b_u = Buf("uT")
        TB0 = 256
        xfs = [A_s[:, i_ * 2048:(i_ + 1) * 2048].rearrange("p (k c) -> p k c", k=8) for i_ in range(2)]
        b_xf = [Buf(), Buf()]
        sq0 = A_t[:, 0:2048].rearrange("p (k c) -> p k c", k=8)
        atf = A_t[:].bitcast(F32)
        r0 = [atf[:, 1024:1280], atf[:, 1280:1536]]
        tmp0 = [atf[:, 1536:1792], atf[:, 1792:2048]]
        b_sq, b_r0, b_tmp0 = Buf(), [Buf(), Buf()], [Buf(), Buf()]
        for tb in range(SEQ // TB0):
            xi_ = tb % 2
            xf = xfs[xi_]
            P.dma("pool", lambda e, tb=tb, xf=xf: e.dma_start(out=xf, in_=xT_full[:, :, tb * TB0:(tb + 1) * TB0]),
                  writes=[b_xf[xi_]])
            act(sq0, xf, AF.Square, [b_xf[xi_]], [b_sq])
            pb = psB[:, tb % 2, 0:TB0]
            mm(pb, [(ones_bf[:], sq0[:, k, :]) for k in range(8)], [b_sq, b_c], [bB[tb % 2]])
            r = r0[xi_]
            ts("dve", r, pb, 1.0 / 1024.0, EPS, ALU.mult, ALU.add, [bB[tb % 2]], [b_r0[xi_]])
            act(r, r, AF.Sqrt, [b_r0[xi_]], [b_r0[xi_]])
            P.op("dve", lambda e, r=r: e.reciprocal(out=r, in_=r), reads=[b_r0[xi_]], writes=[b_r0[xi_]])
            for k in range(4):
                tmp = tmp0[k % 2]
                tt("dve", tmp, xf[:, k, :], r, ALU.mult, [b_xf[xi_], b_r0[xi_]], [b_tmp0[k % 2]])
                act(uT[:, k, tb * TB0:(tb + 1) * TB0], tmp, AF.Identity, [b_tmp0[k % 2], b_mod], [b_u],
                    bias=myB[:, k:k + 1], scale=myA[:, k:k + 1])
        P.op("dve", memset(cst[:, 11:12], 0.0), reads=[], writes=[b_s, b_t] + b_xf + [b_sq] + b_r0 + b_tmp0)
        dump(2, uT.rearrange("p k t -> p (k t)"), [b_u], bf=True)
        lam = sb("lam", [128, 16, 16])
        b_l = Buf("lam")
        L_ = lambda i: lam[:, i, :]
        DT, MAG, PH, SN, CSN, LR, LI, DEN, NR, T1, T2, FR, FI = range(13)
        act(L_(DT), ldst[:], AF.Exp, [b_sm, b_c], [b_l])
        tt("dve", L_(MAG), arst[:], L_(DT), ALU.mult, [b_l, b_sm], [b_l])
        act(L_(MAG), L_(MAG), AF.Exp, [b_l], [b_l])
        tt("dve", L_(PH), aist[:], L_(DT), ALU.mult, [b_l, b_sm], [b_l])
        for dst, off in ((SN, 0.0), (CSN, 0.25)):
            ts("dve", L_(T1), L_(PH), 1.0 / (2 * math.pi), off, ALU.mult, ALU.add, [b_l], [b_l])
            for _ in range(5):
                ts("dve", L_(T2), L_(T1), 0.5, None, ALU.is_gt, None, [b_l], [b_l])
                tt("dve", L_(T1), L_(T1), L_(T2), ALU.subtract, [b_l], [b_l])
            act(L_(dst), L_(T1), AF.Sin, [b_l, b_c], [b_l], scale=2 * math.pi)
        tt("dve", L_(LR), L_(MAG), L_(CSN), ALU.mult, [b_l], [b_l])
        tt("dve", L_(LI), L_(MAG), L_(SN), ALU.mult, [b_l], [b_l])
        tt("dve", L_(DEN), arst[:], arst[:], ALU.mult, [b_sm], [b_l])
        tt("dve", L_(T1), aist[:], aist[:], ALU.mult, [b_sm], [b_l])
        tt("dve", L_(DEN), L_(DEN), L_(T1), ALU.add, [b_l], [b_l])
        P.op("dve", lambda e: e.reciprocal(out=L_(DEN), in_=L_(DEN)), reads=[b_l], writes=[b_l])
        ts("dve", L_(NR), L_(LR), -1.0, None, ALU.add, ALU.bypass, [b_l], [b_l])
        tt("dve", L_(T1), L_(NR), arst[:], ALU.mult, [b_l, b_sm], [b_l])
        tt("dve", L_(T2), L_(LI), aist[:], ALU.mult, [b_l, b_sm], [b_l])
        tt("dve", L_(T1), L_(T1), L_(T2), ALU.add, [b_l], [b_l])
        tt("dve", L_(FR), L_(T1), L_(DEN), ALU.mult, [b_l], [b_l])
        tt("dve", L_(T1), L_(LI), arst[:], ALU.mult, [b_l, b_sm], [b_l])
        tt("dve", L_(T2), L_(NR), aist[:], ALU.mult, [b_l, b_sm], [b_l])
        tt("dve", L_(T1), L_(T1), L_(T2), ALU.subtract, [b_l], [b_l])
        tt("dve", L_(FI), L_(T1), L_(DEN), ALU.mult, [b_l], [b_l])
        CT = sb("CT", [128, 2, 16, 32], BF)
        CTf = A_s[:, 1024:2048].rearrange("p (r q c) -> p r q c", r=2, q=16)
        cre = A_s[:, 0:512].rearrange("p (q c) -> p q c", q=16)
        cim = A_s[:, 512:1024].rearrange("p (q c) -> p q c", q=16)
        P.dma("sync", lambda e: e.dma_start(out=cre, in_=cexp_re), writes=[b_s])
        P.dma("sync", lambda e: e.dma_start(out=cim, in_=cexp_im), writes=[b_s])
        ctmp = sb("ctmp", [128, 2, 32])
        for q in range(16):
            fr, fi = lam[:, FR, q:q + 1], lam[:, FI, q:q + 1]
            ts("dve", ctmp[:, 0, :], cim[:, q, :], fi, None, ALU.mult, ALU.bypass, [b_l, b_s], [b_l])
            stt(CTf[:, 0, q, :], cre[:, q, :], fr, ctmp[:, 0, :], ALU.mult, ALU.subtract, [b_l, b_s], [b_s])
            P.op("dve", lambda e, q=q: e.tensor_copy(out=CT[:, 0, q, :], in_=CTf[:, 0, q, :]), reads=[b_s], writes=[b_l])
            ts("dve", ctmp[:, 1, :], cim[:, q, :], fr, -1.0, ALU.mult, ALU.mult, [b_l, b_s], [b_l])
            ts("dve", ctmp[:, 0, :], cre[:, q, :], fi, None, ALU.mult, ALU.bypass, [b_l, b_s], [b_l])
            tt("dve", CT[:, 1, q, :], ctmp[:, 1, :], ctmp[:, 0, :], ALU.subtract, [b_l], [b_l])
            tt("dve", CTf[:, 1, q, :], ctmp[:, 0, :], ctmp[:, 1, :], ALU.subtract, [b_l], [b_s])
        LP = sb("LP", [128, 12, 3, 16])
        P.op("dve", lambda e: e.tensor_copy(out=LP[:, 0, 0, :], in_=L_(LR)), reads=[b_l], writes=[b_l])
        P.op("dve", lambda e: e.tensor_copy(out=LP[:, 0, 1, :], in_=L_(LI)), reads=[b_l], writes=[b_l])
        for k in range(12):
            ts("dve", LP[:, k, 2, :], LP[:, k, 1, :], -1.0, None, ALU.mult, ALU.bypass, [b_l], [b_l])
            if k < 11:
                tt("dve", L_(T1), LP[:, k, 0, :], LP[:, k, 0, :], ALU.mult, [b_l], [b_l])
                tt("dve", L_(T2), LP[:, k, 1, :], LP[:, k, 1, :], ALU.mult, [b_l], [b_l])
                tt("dve", LP[:, k + 1, 0, :], L_(T1), L_(T2), ALU.subtract, [b_l], [b_l])
                tt("dve", L_(T1), LP[:, k, 0, :], LP[:, k, 1, :], ALU.mult, [b_l], [b_l])
                ts("dve", LP[:, k + 1, 1, :], L_(T1), 2.0, None, ALU.mult, ALU.bypass, [b_l], [b_l])

        L = 8
        NCH = SEQ // L
        LPS = sb("LPS", [128, L, 3, 16])
        for c3 in range(3):
            P.op("dve", lambda e, c3=c3: e.tensor_copy(out=LPS[:, 0, c3, :], in_=LP[:, 0, c3, :]),
                 reads=[b_l], writes=[b_l])
        for s1 in range(1, L):
            pr, pi_ = LPS[:, s1 - 1, 0, :], LPS[:, s1 - 1, 1, :]
            tt("dve", L_(T1), pr, LP[:, 0, 0, :], ALU.mult, [b_l], [b_l])
            tt("dve", L_(T2), pi_, LP[:, 0, 1, :], ALU.mult, [b_l], [b_l])
            tt("dve", LPS[:, s1, 0, :], L_(T1), L_(T2), ALU.subtract, [b_l], [b_l])
            tt("dve", L_(T1), pr, LP[:, 0, 1, :], ALU.mult, [b_l], [b_l])
            tt("dve", L_(T2), pi_, LP[:, 0, 0, :], ALU.mult, [b_l], [b_l])
            tt("dve", LPS[:, s1, 1, :], L_(T1), L_(T2), ALU.add, [b_l], [b_l])
            ts("dve", LPS[:, s1, 2, :], LPS[:, s1, 1, :], -1.0, None, ALU.mult, ALU.bypass, [b_l], [b_l])
        CH = A_w[0][:].bitcast(F32)[:, 2048:4096].rearrange("p (a r n) -> p a r n", a=2, r=2)
        bCH = [[Buf(), Buf()], [Buf(), Buf()]]
        P.op("dve", memset(cst[:, 13:14], 0.0), reads=[], writes=[b_ada2] + sum(bCH, []))
        X = A_x[:].rearrange("p k t -> p (k t)").rearrange("p (a r t) -> p a r t", a=2, r=2)
        bX = [[Buf(), Buf()], [Buf(), Buf()]]
        zT = A_g[:].rearrange("p (k t) -> p k t", k=4)
        b_z = Buf("zT")
        xb = A_s[:].bitcast(BF).rearrange("p (r t) -> p r t", r=2)
        ysc = A_t[:].bitcast(F32)[:, 0:1536].rearrange("p (a c) -> p a c", a=3)
        b_y = b_t
        Sb = A_t[:, 3072:4096].rearrange("p (r n) -> p r n", r=2)
        b_sb = Buf("Sb")
        P.op("dve", memset(Sb[:, :, 0:1], 0.0), reads=[], writes=[b_sb, b_t])
        CQ = A_w[1][:].rearrange("p (q r s c) -> p q r s c", q=16, r=2, s=L)
        b_cq = Buf("CQ")
        for s1 in range(L):
            lr_b = LPS[:, s1, 0, :].unsqueeze(2).broadcast_to([128, 16, 32])
            li_b = LPS[:, s1, 1, :].unsqueeze(2).broadcast_to([128, 16, 32])
            cfr, cfi = CTf[:, 0], CTf[:, 1]
            t1 = A_s[:, 2048:2560].rearrange("p (q c) -> p q c", q=16)
            t2 = A_s[:, 2560:3072].rearrange("p (q c) -> p q c", q=16)
            tt("dve", t1, cfr, lr_b, ALU.mult, [b_s, b_l], [b_s])
            tt("dve", t2, cfi, li_b, ALU.mult, [b_s, b_l], [b_s])
            tt("dve", CQ[:, :, 0, s1, :], t1, t2, ALU.subtract, [b_s], [b_cq, b_w[1]])
            tt("dve", t1, cfr, li_b, ALU.mult, [b_s, b_l], [b_s])
            tt("dve", t2, cfi, lr_b, ALU.mult, [b_s, b_l], [b_s])
            tt("dve", t1, t1, t2, ALU.add, [b_s], [b_s])
            ts("dve", CQ[:, :, 1, s1, :], t1, -1.0, None, ALU.mult, ALU.bypass, [b_s], [b_cq])
        pcnt = 0

        def cview(a_, ri):
            return X[:, a_, ri, :].rearrange("p (c j) -> p c j", j=L)

        def hs_level(src, dst, bs, bd, a, b, nb, d, n, sl):
            sr, si, dr, di = src[0], src[1], dst[0], dst[1]
            stt(sl(dr, d, n), sl(sr, 0, n - d), a, sl(sr, d, n), ALU.mult, ALU.add, [bs[0], b_l], [bd[0]])
            stt(sl(di, d, n), sl(si, 0, n - d), a, sl(si, d, n), ALU.mult, ALU.add, [bs[1], b_l], [bd[1]])
            h0 = d // 2 if d > 1 else 0
            bh = [Buf(), Buf()]
            P.op("dve", lambda e: e.tensor_copy(out=sl(dr, h0, d), in_=sl(sr, h0, d)), reads=[bs[0]], writes=[bh[0]])
            P.op("dve", lambda e: e.tensor_copy(out=sl(di, h0, d), in_=sl(si, h0, d)), reads=[bs[1]], writes=[bh[1]])
            stt(sl(dr, d, n), sl(si, 0, n - d), nb, sl(dr, d, n), ALU.mult, ALU.add, [bs[1], b_l, bh[0]], [bd[0]])
            stt(sl(di, d, n), sl(sr, 0, n - d), b, sl(di, d, n), ALU.mult, ALU.add, [bs[0], b_l, bh[1]], [bd[1]])

        def madd(vr, vi, dsel, ssel, k, q, bx):
            a, b, nb = LP[:, k, 0, q:q + 1], LP[:, k, 1, q:q + 1], LP[:, k, 2, q:q + 1]
            dr, di, sr, si = dsel(vr), dsel(vi), ssel(vr), ssel(vi)
            stt(dr, sr, a, dr, ALU.mult, ALU.add, [bx[0], b_l], [bx[0]])
            stt(di, si, a, di, ALU.mult, ALU.add, [bx[1], b_l], [bx[1]])
            stt(dr, si, nb, dr, ALU.mult, ALU.add, [bx[0], bx[1], b_l], [bx[0]])
            stt(di, sr, b, di, ALU.mult, ALU.add, [bx[0], bx[1], b_l], [bx[1]])

        pcb = [0, 0]
        b_g2 = [Buf(), Buf()]
        P.op("dve", memset(cst[:, 9:10], 0.0), reads=[], writes=[b_t] + b_g2)
        Ddiag = sb("Ddiag", [128, 4, 32], BF)
        for i_ in range(4):
            ts("dve", Ddiag[:, i_, :], eyet[:], dcol[:, i_:i_ + 1], None, ALU.mult, ALU.bypass, [b_sm], [b_l])

        def stA(q):
            i, q4, xp = q // 4, q % 4, q % 2
            rows = slice(32 * q4, 32 * q4 + 32)
            for ri in range(2):
                for tb in range(8):
                    pb = pcb[0] % 2
                    pcb[0] += 1
                    mm(psB[:, pb, :], [(BT[rows, ri, i, :], uT[rows, i, tb * 512:(tb + 1) * 512])],
                       [b_sm, b_u], [bB[pb]], tp=(32 * q4, 0))
                    act(X[:, xp, ri, tb * 512:(tb + 1) * 512], psB[:, pb, :], AF.Identity, [bB[pb], b_c],
                        [bX[xp][ri]])

        def stB1(q):
            xp = q % 2
            vr, vi = cview(xp, 0), cview(xp, 1)
            madd(vr, vi, lambda v: v[:, :, 1::2], lambda v: v[:, :, 0::2], 0, q, bX[xp])
            madd(vr, vi, lambda v: v[:, :, 3::4], lambda v: v[:, :, 1::4], 1, q, bX[xp])
            madd(vr, vi, lambda v: v[:, :, 7], lambda v: v[:, :, 3], 2, q, bX[xp])
            madd(vr, vi, lambda v: v[:, :, 5], lambda v: v[:, :, 3], 1, q, bX[xp])
            madd(vr, vi, lambda v: v[:, :, 2::2], lambda v: v[:, :, 1:6:2], 0, q, bX[xp])

        def stB2(q):
            xp = q % 2
            for ri in range(2):
                act(CH[:, 0, ri, :], cview(xp, ri)[:, :, L - 1], AF.Identity, [bX[xp][ri], b_c], [bCH[0][ri]])
            for k in range(9):
                s_, d_ = k % 2, 1 - (k % 2)
                hs_level([CH[:, s_, 0, :], CH[:, s_, 1, :]], [CH[:, d_, 0, :], CH[:, d_, 1, :]], bCH[s_], bCH[d_],
                         LP[:, k + 3, 0, q:q + 1], LP[:, k + 3, 1, q:q + 1], LP[:, k + 3, 2, q:q + 1], 1 << k, NCH,
                         lambda v, lo, hi: v[:, lo:hi])

        def stCconv(q):
            i, q4, xp = q // 4, q % 4, q % 2
            rows = slice(32 * q4, 32 * q4 + 32)
            for ri in range(2):
                act(Sb[:, ri, 1:NCH], CH[:, 1, ri, 0:NCH - 1], AF.Identity, [bCH[1][ri], b_c], [b_sb])
                act(xb[:, ri, :], X[:, xp, ri, :], AF.Identity, [bX[xp][ri], b_c], [b_s])
            for tb in range(4):
                ymm_emit(q, tb)

        def ymm_emit(q, tb):
            i, q4 = q // 4, q % 4
            rows = slice(32 * q4, 32 * q4 + 32)
            tsl = slice(tb * 512, (tb + 1) * 512)
            pb = (pcb[1] + tb) % 4
            pv = psA[rows, pb, :].rearrange("p (c j) -> p c j", j=L)

            def ymm(e):
                tp = (0, 32 * q4)
                e.matmul(psA[rows, pb, :], Ddiag[rows, i, :], uT[rows, i, tsl], start=True, stop=False,
                         tile_position=(32 * q4, 32 * q4))
                e.matmul(psA[rows, pb, :], CT[:, 0, q, :], xb[:, 0, tsl], start=False, stop=False, tile_position=tp)
                ins = e.matmul(psA[rows, pb, :], CT[:, 1, q, :], xb[:, 1, tsl], start=False, stop=False,
                               tile_position=tp)
                for s1 in range(L):
                    for ri in range(2):
                        ins = e.matmul(pv[:, :, s1], CQ[:, q, ri, s1, :], Sb[:, ri, tb * 64:(tb + 1) * 64],
                                       start=False, stop=(ri == 1), tile_position=tp,
                                       skip_group_check=True)
                return ins
            P.op("pe", ymm, reads=[b_l, b_s, b_cq, b_sb, b_u], writes=[bA[pb]])

        def stCevac(q):
            i, q4 = q // 4, q % 4
            rows = slice(32 * q4, 32 * q4 + 32)
            for tb in range(8):
                tsl = slice(tb * 512, (tb + 1) * 512)
                pb = (pcb[1] + tb) % 4
                act(zT[rows, i, tsl], psA[rows, pb, :], AF.Identity, [bA[pb], b_c], [b_z], bias=C("zero")[rows])
                if tb + 4 < 8:
                    ymm_emit(q, tb + 4)
            pcb[1] += 8
            if q4 == 3:
                def gelu_prep(tb):
                    tsl = slice(tb * 512, (tb + 1) * 512)
                    yb, t1, bt1 = zT[:, i, tsl], ysc[:, tb % 2, :], b_g2[tb % 2]
                    act(t1, yb, AF.Square, [b_z, b_c], [bt1], scale=math.sqrt(0.044715))
                    stt(t1, t1, 1.0, yb, ALU.add, ALU.mult, [bt1, b_z], [bt1])
                    act(t1, t1, AF.Sigmoid, [bt1, b_c], [bt1], scale=2.0 * math.sqrt(2.0 / math.pi))

                gelu_prep(0)
                for tb in range(8):
                    if tb + 1 < 8:
                        gelu_prep(tb + 1)
                    tsl = slice(tb * 512, (tb + 1) * 512)
                    yb = zT[:, i, tsl]
                    tt("dve", yb, yb, ysc[:, tb % 2, :], ALU.mult, [b_g2[tb % 2], b_z], [b_z])

        stA(0)
        stB1(0)
        stB2(0)
        for q in range(16):
            if q + 1 < 16:
                stA(q + 1)
            stCconv(q)
            for _ in range(2):
                if ada_rest:
                    ada_issue(*ada_rest.pop(0))
            if q + 1 < 16:
                stB1(q + 1)
            stCevac(q)
            if q + 1 < 16:
                stB2(q + 1)
            if q >= 12:
                for _ in range(2):
                    if ada_rest:
                        ada_issue(*ada_rest.pop(0))
        P.op("dve", memset(cst[:, 10:11], 0.0), reads=[], writes=[b_t, b_sb, b_cq, b_w[1], b_w[0], bC] + bCs + b_g2 + sum(bCH, []))

        dump(3, zT.rearrange("p k t -> p (k t)"), [b_z], bf=True)
        b_zo = Buf("z_ob")
        for a in range(2):
            P.dma("sync", lambda e, a=a: e.dma_start(out=z_ibs[a].ap().rearrange("(j p) t -> p j t", p=128),
                                                      in_=zT[:, 2 * a:2 * a + 2, :]), reads=[b_z], writes=[b_zo])
        for a in range(2):
            P.cc(lambda e, a=a: e.collective_compute("AllGather", ALU.bypass, replica_groups=PAIRS,
                                                     ins=[z_ibs[a][:, :]], outs=[z_obs[a][:, :]]),
                 reads=[b_zo], writes=[b_zo])
        while ada_rest:
            ada_piece(*ada_rest.pop(0))
        ada_evac(16, 96)
        mk_modA(0, 1)
        mk_modA(1, 0)
        mk_modA(1, 1)
        for k in range(8):
            P.dma("sync", lambda e, k=k: e.dma_start(out=A_x[:, k, :], in_=xT_own[:, k, :]),
                  reads=[], writes=[bX[0][0], bX[0][1], bX[1][0], bX[1][1]] + b_x[k])
        zc = A_t[:].rearrange("p (k c) -> p k c", k=8)
        zc2 = A_s[:].bitcast(BF)[:, 0:4096].rearrange("p (k c) -> p k c", k=8)
        for tb in range(4):
            for dstz, off, bz in ((zc, 0, b_t), (zc2, NTOK, b_s)):
                for a in range(2):
                    for r in range(2):
                        srcz = z_obs[a][r * 256:(r + 1) * 256, off + tb * 512:off + (tb + 1) * 512].rearrange(
                            "(j p) t -> p j t", p=128)
                        t0 = r * 4 + a * 2
                        P.dma("sync", lambda e, d_=dstz[:, t0:t0 + 2, :], s_=srcz: e.dma_start(out=d_, in_=s_),
                              reads=[b_zo], writes=[bz])
            hv = A_h[:, :, tb * 512:(tb + 1) * 512]
            ts("dve", hv, zc, selt[:, 0:1], None, ALU.mult, ALU.bypass, [b_t, b_sm, b_u], [b_h[tb], b_u])
            stt(hv, zc2, selt[:, 1:2], hv, ALU.mult, ALU.add, [b_s, b_sm], [b_h[tb]])

        wq_cnt = [0]

        def load_w(dst, src, bw):
            P.dma("pool", lambda e: e.dma_start(out=dst, in_=src), writes=[bw])

        for j in range(2):
            wi = wq_cnt[0] % 2
            wq_cnt[0] += 1
            W = A_w[wi][:].rearrange("p (k c) -> p k c", k=8)
            load_w(W[:, :, 0:512], w_glu[:, j * 512:(j + 1) * 512].rearrange("(k p) c -> p k c", p=128), b_w[wi])
            load_w(W[:, :, 512:1024], w_glu[:, 1024 + j * 512:1024 + (j + 1) * 512].rearrange("(k p) c -> p k c", p=128),
                   b_w[wi])
            for tb in range(4):
                tsl = slice(tb * 512, (tb + 1) * 512)
                for t in range(4):
                    n = 4 * j + t
                    pb = pcnt % 4
                    pcnt += 1
                    mm(psA[:, pb, :], [(W[:, k, 512 + t * 128:512 + (t + 1) * 128], A_h[:, k, tsl]) for k in range(8)],
                       [b_w[wi], b_h[tb]], [bA[pb]])
                    sg = ysc[:, 0, :]
                    act(sg, psA[:, pb, :], AF.Sigmoid, [bA[pb], b_c], [b_y])
                    pb2 = pcnt % 4
                    pcnt += 1
                    mm(psA[:, pb2, :], [(W[:, k, t * 128:(t + 1) * 128], A_h[:, k, tsl]) for k in range(8)],
                       [b_w[wi], b_h[tb]], [bA[pb2]])
                    y = ysc[:, 1, :]
                    tt("dve", y, psA[:, pb2, :], sg, ALU.mult, [bA[pb2], b_y], [b_y])
                    stt(A_x[:, n, tsl], y, gate(0, 0)[:, n:n + 1], A_x[:, n, tsl], ALU.mult, ALU.add,
                        [b_y, b_mod, b_x[n][tb]], [b_x[n][tb]])

        b_tm = [Buf(), Buf(), Buf(), Buf()]
        b_rr = Buf("rstd")

        def norm_mod(l, s_):
            P.op("dve", memset(cst[:, 15:16], 0.0), reads=[], writes=[b_s, b_rr] + b_tm)
            for tb in range(4):
                tsl = slice(tb * 512, (tb + 1) * 512)
                xr = [b_x[k][tb] for k in range(8)]
                r = rstd_block(A_x[:, :, tsl], 8, 1024.0, xr, br=b_rr)
                for k in range(8):
                    tmp = A_s[:, (k % 4) * 512:(k % 4 + 1) * 512]
                    tt("dve", tmp, A_x[:, k, tsl], r, ALU.mult, [b_rr, b_x[k][tb]], [b_tm[k % 4]])
                    act(A_h[:, k, tsl], tmp, AF.Identity, [b_tm[k % 4], b_mod], [b_h[tb]],
                        bias=shift(l, s_)[:, k:k + 1], scale=modA[:, l, s_, k:k + 1])
            P.op("dve", memset(cst[:, 15:16], 0.0), reads=[], writes=[b_s, b_rr] + b_tm)

        bAT = [Buf(), Buf()]

        b_ys = [Buf(), Buf(), Buf()]

        def mlp(l):
            nonlocal pcnt
            aT = A_g[:, 0:4096].rearrange("p (a f c) -> p a f c", a=2, f=4)
            first = [True]
            Wv_ = {}
            P.op("dve", memset(cst[:, 12:13], 0.0), reads=[], writes=[b_t] + b_ys)

            def s1(n):
                nonlocal pcnt
                e8, tb = n // 4, n % 4
                if tb == 0:
                    wi = wq_cnt[0] % 2
                    wq_cnt[0] += 1
                    W1 = A_w[wi][:, 0:4096].rearrange("p (k c) -> p k c", k=8)
                    W2 = A_w[wi][:, 4096:8192].rearrange("p (k c) -> p k c", k=4)
                    load_w(W1, w_ff1[l, :, e8 * 512:(e8 + 1) * 512].rearrange("(k p) c -> p k c", p=128), b_w[wi])
                    load_w(W2, w_ff2[l, e8 * 512:(e8 + 1) * 512, :].rearrange("(k p) c -> p k c", p=128), b_w[wi])
                    Wv_[e8] = (wi, W1, W2)
                wi, W1, W2 = Wv_[e8]
                tsl = slice(tb * 512, (tb + 1) * 512)
                ap_ = n % 2
                for f in range(4):
                    pb = pcnt % 4
                    pcnt += 1
                    mm(psA[:, pb, :], [(W1[:, k, f * 128:(f + 1) * 128], A_h[:, k, tsl]) for k in range(8)],
                       [b_w[wi], b_h[tb]], [bA[pb]])
                    sl3 = (4 * n + f) % 3
                    rl = ysc[:, sl3, :]
                    act(rl, psA[:, pb, :], AF.Relu, [bA[pb], b_c], [b_ys[sl3]])
                    extra = [b_g, b_z] if first[0] else []
                    first[0] = False
                    tt("dve", aT[:, ap_, f, :], rl, rl, ALU.mult, [b_ys[sl3]], [bAT[ap_]] + extra)

            def s2(n):
                nonlocal pcnt
                e8, tb = n // 4, n % 4
                wi, W1, W2 = Wv_[e8]
                tsl = slice(tb * 512, (tb + 1) * 512)
                ap_ = n % 2
                for nn in range(8):
                    pb = pcnt % 2
                    pcnt += 1
                    mm(psB[:, pb, :], [(W2[:, f, nn * 128:(nn + 1) * 128], aT[:, ap_, f, :]) for f in range(4)],
                       [b_w[wi], bAT[ap_]], [bB[pb]])
                    stt(A_x[:, nn, tsl], psB[:, pb, :], gate(l, 1)[:, nn:nn + 1], A_x[:, nn, tsl], ALU.mult, ALU.add,
                        [bB[pb], b_mod, b_x[nn][tb]], [b_x[nn][tb]])

            s1(0)
            for n in range(32):
                if n + 1 < 32:
                    s1(n + 1)
                s2(n)
            P.op("dve", memset(cst[:, 12:13], 0.0), reads=[], writes=[b_t] + b_ys)

        allx = [b_x[k][tb] for k in range(8) for tb in range(4)]
        dump(4, (outT, A_x[:]), allx)
        norm_mod(0, 1)
        mlp(0)
        dump(5, (outT, A_x[:]), allx)

        norm_mod(1, 0)
        Wq = A_w[0][:, 0:4096].rearrange("p (k c) -> p k c", k=8)
        Wk = A_w[0][:, 4096:8192].rearrange("p (k c) -> p k c", k=8)
        Wv = A_w[1][:].rearrange("p (k c) -> p k c", k=8)
        Wr = A_g[:, 0:8192].rearrange("p (k c) -> p k c", k=8)
        Wo = A_g[:, 8192:16384].rearrange("p (k c) -> p k c", k=8)
        Wg = sb("Wg", [128, 8, 16], BF)
        win = lambda a, b: w_in[:, a:b].rearrange("(k p) c -> p k c", p=128)
        load_w(Wq, win(0, 512), b_w[0])
        load_w(Wk, win(512, 1024), b_w[0])
        load_w(Wv, win(1024, 2048), b_w[1])
        load_w(Wg[:], win(2048, 2064), b_g)
        P.dma("pool", lambda e: e.dma_start(out=Wr, in_=win(2064, 3088)), writes=[b_g, bAT[0], bAT[1]])
        load_w(Wo, w_out.rearrange("(k p) c -> p k c", p=128), b_g)

        S = sb("S", [128, 4, 256])
        Sbf = sb("Sbf", [128, 1, 4, 256], BF)
        b_S = Buf("S")
        b_Sbf = [Buf(), Buf()]
        Sbfs = [Sbf[:, 0],
                LP[:].rearrange("p a b c -> p (a b c)").bitcast(BF)[:, 0:1024].rearrange("p (h e) -> p h e", h=4)]
        gl = sb("gl", [16, 128], BF)
        la = A_s[:, 0:512]
        Ef = A_s[:, 512:1024]
        gsc = A_s[:, 1024:2048].rearrange("p (k t) -> p k t", k=8)
        rsd = A_s[:, 2048:2560].rearrange("p (k t) -> p k t", k=4)
        asb = A_s[:].bitcast(BF)
        kdec = [A_t[:, 0:512], asb[:, 5376:5888]]
        vtok = [A_t[:, 512:1536], asb[:, 5888:6912]]
        qTt = [A_t[:, 3584:4096].rearrange("p (k t) -> p k t", k=4),
               asb[:, 6912:7424].rearrange("p (k t) -> p k t", k=4)]
        srs = [BT[:].rearrange("p a b c -> p (a b c)").rearrange("p (k t) -> p k t", k=8),
               CT[:].rearrange("p a b c -> p (a b c)").rearrange("p (k t) -> p k t", k=8)]
        osq = A_t[:, 1536:2560].rearrange("p (k t) -> p k t", k=8)
        gat = A_t[:, 2560:3584].rearrange("p (k t) -> p k t", k=8)
        dec = sb("dec", [128, 2, 4, 2])
        b_gl, b_la, b_gs, b_osq, b_rsd, b_gat = [Buf() for _ in range(6)]
        b_kd, b_v, b_dec, b_q, b_sr = [[Buf(), Buf()] for _ in range(5)]
        P.op("dve", memset(S[:], 0.0), writes=[b_S])

        def gla_front(tt_, final):
            st = tt_ % 2
            tk = slice(tt_ * 128, (tt_ + 1) * 128)
            tb = tt_ // 4
            hT = lambda k: A_h[:, k, tk]
            mm(psB[0:16, 0, 0:128], [(Wg[:, k, :], hT(k)) for k in range(8)], [b_g, b_h[tb]], [bB[0]])
            act(gl[:], psB[0:16, 0, 0:128], AF.Identity, [bB[0], b_c], [b_gl], bias=C("zero")[0:16])
            mm(psA[:, 0, :], [(ones_bf[0:1, :], bgb[:]), (gl[:], wg2[:])], [b_c, b_sm, b_gl], [bA[0]])
            act(Ef, psA[:, 0, :], AF.Exp, [bA[0], b_c], [b_la], scale=-1.0)
            act(la, Ef, AF.Ln, [b_la, b_c], [b_la], bias=C("one"))
            mm(psA[:, 1, :], [(umt[:], la)], [b_sm, b_la], [bA[1]])
            for h in range(4):
                mm(psB[:, 1, 2 * h:2 * h + 2], [(la[:, h * 128:(h + 1) * 128], indt[:])], [b_la, b_sm], [bB[1]])
            mm(psA[:, 2, :], [(hT(k), Wk[:, k, :]) for k in range(8)], [b_w[0], b_h[tb]], [bA[2]])
            act(Ef, psA[:, 1, :], AF.Exp, [bA[1], b_c], [b_la], scale=-1.0 / 16.0)
            act(dec[:, st].rearrange("p h c -> p (h c)"), psB[:, 1, 0:8], AF.Exp, [bB[1], b_c], [b_dec[st]],
                scale=-1.0 / 16.0)
            tt("dve", kdec[st], psA[:, 2, :], Ef, ALU.mult, [bA[2], b_la], [b_kd[st]])
            for hh in range(2):
                mm(psA[:, 3, :], [(hT(k), Wv[:, k, hh * 512:(hh + 1) * 512]) for k in range(8)],
                   [b_w[1], b_h[tb]], [bA[3]])
                act(vtok[st][:, hh * 512:(hh + 1) * 512], psA[:, 3, :], AF.Identity, [bA[3], b_c], [b_v[st]])
            if final:
                for h in range(4):
                    mm(psB[:, 0, h * 128:(h + 1) * 128], [(Wq[:, k, h * 128:(h + 1) * 128], hT(k)) for k in range(8)],
                       [b_w[0], b_h[tb]], [bB[0]])
                act(qTt[st].rearrange("p k t -> p (k t)"), psB[:, 0, :], AF.Identity, [bB[0], b_c], [b_q[st]],
                    scale=128.0 ** -0.5)
                pr = psA[:, 0:2, :].rearrange("p a c -> p (a c)")
                for t8 in range(8):
                    mm(pr[:, t8 * 128:(t8 + 1) * 128], [(Wr[:, k, t8 * 128:(t8 + 1) * 128], hT(k)) for k in range(8)],
                       [b_g, b_h[tb]], [bA[0], bA[1]])
                act(srs[st].rearrange("p k t -> p (k t)"), pr, AF.Silu, [bA[0], bA[1], b_c], [b_sr[st]])

        def gla_back(tt_, final):
            st = tt_ % 2
            tk = slice(tt_ * 128, (tt_ + 1) * 128)
            tb = tt_ // 4
            for cc in range(2):
                rws = slice(64 * cc, 64 * cc + 64)
                for h in range(4):
                    mm(psC[:, h * 256:(h + 1) * 256],
                       [(kdec[st][rws, h * 128:(h + 1) * 128], vtok[st][rws, h * 256:(h + 1) * 256])],
                       [b_kd[st], b_v[st]], [bC], tp=(64 * cc, 0))
                for h in range(4):
                    stt(S[:, h, :], S[:, h, :], dec[:, st, h, cc:cc + 1], psC[:, h * 256:(h + 1) * 256], ALU.mult,
                        ALU.add, [b_S, b_dec[st], bC], [b_S])
                if final:
                    act(Sbfs[cc].rearrange("p h e -> p (h e)"), S[:].rearrange("p h e -> p (h e)"), AF.Identity,
                        [b_S, b_c], [b_Sbf[cc]])
            if final:
                for cc in range(2):
                    for h in range(4):
                        for e2 in range(2):
                            t8 = 2 * h + e2
                            mm(psB[:, 1, t8 * 64:(t8 + 1) * 64],
                               [(Sbfs[cc][:, h, e2 * 128:(e2 + 1) * 128], qTt[st][:, h, 64 * cc:64 * cc + 64])],
                               [b_Sbf[cc], b_q[st]], [bB[1]])
                    act(gsc[:, :, 64 * cc:64 * cc + 64], psB[:, 1, :].rearrange("p (k t) -> p k t", k=8),
                        AF.Identity, [bB[1], b_c], [b_gs])
            if final:
                act(osq, gsc, AF.Square, [b_gs, b_c], [b_osq])
                for h in range(4):
                    mm(psB[:, 0, h * 128:(h + 1) * 128],
                       [(ones_bf[:], osq[:, 2 * h, :]), (ones_bf[:], osq[:, 2 * h + 1, :])], [b_osq, b_c], [bB[0]])
                ts("dve", A_s[:, 2048:2560], psB[:, 0, :], 1.0 / 256.0, EPS, ALU.mult, ALU.add, [bB[0]], [b_rsd])
                act(A_s[:, 2048:2560], A_s[:, 2048:2560], AF.Sqrt, [b_rsd], [b_rsd])
                P.op("dve", lambda e: e.reciprocal(out=A_s[:, 2048:2560], in_=A_s[:, 2048:2560]),
                     reads=[b_rsd], writes=[b_rsd])
                for t8 in range(8):
                    stt(gsc[:, t8, :], gsc[:, t8, :], gnc[:, t8:t8 + 1], rsd[:, t8 // 2, :], ALU.mult, ALU.mult,
                        [b_gs, b_sm, b_rsd], [b_gs])
                tt("dve", gat, gsc, srs[st], ALU.mult, [b_gs, b_sr[st]], [b_gat])
                po = psA[:, 2:4, :].rearrange("p a c -> p (a c)")
                for n in range(8):
                    mm(po[:, n * 128:(n + 1) * 128], [(Wo[:, k, n * 128:(n + 1) * 128], gat[:, k, :]) for k in range(8)],
                       [b_g, b_gat], [bA[2], bA[3]])
                for n in range(8):
                    stt(A_x[:, n, tk], po[:, n * 128:(n + 1) * 128], gate(1, 0)[:, n:n + 1], A_x[:, n, tk],
                        ALU.mult, ALU.add, [bA[2], bA[3], b_mod, b_x[n][tb]], [b_x[n][tb]])

        def gla_pass(final, first_front_done=False):
            if not first_front_done:
                gla_front(0, final)
            for tt_ in range(16):
                if tt_ + 1 < 16:
                    gla_front(tt_ + 1, final)
                gla_back(tt_, final)

        gla_tmp = [b_s, b_t, b_gl, b_la, b_gs, b_osq, b_rsd, b_gat, b_sm, b_l] + sum([b_kd, b_v, b_dec, b_q, b_sr], [])
        bar = sb("bar", [128, 1])

        def barrier(bufs):
            P.op("dve", memset(bar[:], 0.0), writes=bufs)

        barrier(gla_tmp)
        gla_pass(False)
        b_so = Buf("s_ob")
        P.dma("sync", lambda e: e.dma_start(out=s_ib[:, :], in_=S[:].rearrange("p h e -> p (h e)")), reads=[b_S], writes=[b_so])
        P.cc(lambda e: e.collective_compute("AllGather", ALU.bypass, replica_groups=PAIRS,
                                            ins=[s_ib[:, :]], outs=[s_ob[:, :]]), reads=[b_so], writes=[b_so])
        P.dma("sync", lambda e: e.dma_start(out=S[:].rearrange("p h e -> p (h e)"), in_=s_ob[0:128, :]),
              reads=[b_so], writes=[b_S])
        gla_front(0, True)
        ts("dve", S[:].rearrange("p h e -> p (h e)"), S[:].rearrange("p h e -> p (h e)"), selt[:, 1:2], None,
           ALU.mult, ALU.bypass, [b_S, b_sm], [b_S])
        gla_pass(True, first_front_done=True)
        barrier(gla_tmp)
        dump(6, (outT, A_x[:]), allx)

        norm_mod(1, 1)
        mlp(1)
        dump(7, (outT, A_x[:]), allx)

        b_out = Buf("out")
        for tb in range(4):
            tsl = slice(tb * 512, (tb + 1) * 512)
            xr = [b_x[k][tb] for k in range(8)]
            r = rstd_block(A_x[:, :, tsl], 8, 1024.0, xr)
            for k in range(8):
                stt(A_x[:, k, tsl], A_x[:, k, tsl], nfin[:, k:k + 1], r, ALU.mult, ALU.mult,
                    [b_s, b_sm, b_x[k][tb]], [b_x[k][tb]])
            P.dma("sync", lambda e, tsl=tsl: e.dma_start(out=outT[:, :, tsl], in_=A_x[:, :, tsl]),
                  reads=xr, writes=[b_out])
        P.final_wait("sync", [b_out])
    try:
        body()
    except _Stop:
        pass
    P.emit(nc, es)
    es.close()
    return nc


_NC = None
_LAST = None


def _tiles(a2d):
    C, T = a2d.shape
    return np.ascontiguousarray(a2d.reshape(C // 128, 128, T).transpose(1, 0, 2))


def _col(v):
    return np.ascontiguousarray(v.reshape(-1, 128).T)


def kernel(x, c, w_ada, b_ada, norm_mix, norm_mlp, s5_a_re, s5_a_im, s5_log_dt, s5_b_re, s5_b_im,
           s5_c_re, s5_c_im, s5_d, s5_w_glu, gla_w_in, gla_w_gate2, gla_b_gate, gla_g_norm, gla_w_out,
           w_ff1, w_ff2, norm_final):
    global _NC
    f = lambda a: np.ascontiguousarray(np.asarray(a, dtype=np.float32))
    x, c = f(x), f(c)
    if _NC is None:
        _NC = build()
    umat = np.zeros((128, 128), np.float32)
    for s in range(128):
        for s2 in range(s + 1, (s // 64 + 1) * 64):
            umat[s2, s] = 1.0
    ind = np.zeros((128, 2), np.float32)
    ind[:64, 0] = 1.0
    ind[64:, 1] = 1.0
    common = {
        "w_ada": f(w_ada),
        "b_ada_col": np.ascontiguousarray(np.stack([_col(f(b_ada)[l]) for l in range(2)], axis=1)),
        "nmix_col": np.ascontiguousarray(np.stack([_col(f(norm_mix)[l]) for l in range(2)], axis=1)),
        "nmlp_col": np.ascontiguousarray(np.stack([_col(f(norm_mlp)[l]) for l in range(2)], axis=1)),
        "nfin_col": _col(f(norm_final)), "gn_col": _col(f(gla_g_norm)[0]),
        "w_glu": f(s5_w_glu)[0], "w_in": f(gla_w_in)[0], "w_g2": f(gla_w_gate2)[0], "b_gate": f(gla_b_gate),
        "w_out": f(gla_w_out)[0], "w_ff1": f(w_ff1), "w_ff2": f(w_ff2), "umat": umat, "ind": ind,
        "eye32": np.ascontiguousarray(np.tile(np.eye(32, dtype=np.float32), (4, 1))),
    }
    are, aim, ldt = f(s5_a_re)[0], f(s5_a_im)[0], f(s5_log_dt)[0]
    bre, bim, cre, cim, dsk = f(s5_b_re)[0], f(s5_b_im)[0], f(s5_c_re)[0], f(s5_c_im)[0], f(s5_d)[0]
    in_maps = []
    for core in range(8):
        b, hf = core // 2, core % 2
        g0 = 32 * hf
        xT = np.ascontiguousarray(x[b].T)
        order = list(range(4 * hf, 4 * hf + 4)) + list(range(4 * (1 - hf), 4 * (1 - hf) + 4))
        xfull = _tiles(xT)[:, order, :]
        st = lambda a: np.ascontiguousarray(a[g0:g0 + 32].reshape(16, 128).T)
        bexp = []
        for bsrc in (bre, bim):
            t = np.zeros((128, 4, 128), np.float32)
            for gl_ in range(32):
                i, g8 = gl_ // 8, gl_ % 8
                g2 = gl_ % 2
                t[g8 * 16:(g8 + 1) * 16, i, g2 * 64:(g2 + 1) * 64] = bsrc[g0 + gl_].T
            bexp.append(t)
        cexp = []
        for csrc in (cre, cim):
            t = np.zeros((128, 16, 32), np.float32)
            for gl_ in range(32):
                q, g2 = gl_ // 2, gl_ % 2
                t[g2 * 64:(g2 + 1) * 64, q, g2 * 16:(g2 + 1) * 16] = csrc[g0 + gl_].T
            cexp.append(t)
        sel = np.zeros((128, 2), np.float32)
        sel[:, hf] = 1.0
        m = dict(common)
        m.update({
            "xT_own": np.ascontiguousarray(_tiles(xT)[:, :, hf * NTOK:(hf + 1) * NTOK]),
            "xT_full": np.ascontiguousarray(xfull),
            "c_col": _col(c[b]),
            "a_re_st": st(are), "a_im_st": st(aim),
            "ldt_st": np.ascontiguousarray(np.repeat(ldt[g0:g0 + 32].reshape(16, 2).T, 64, axis=0)),
            "bexp_re": bexp[0], "bexp_im": bexp[1], "cexp_re": cexp[0], "cexp_im": cexp[1],
            "d_col": _col(dsk[512 * hf:512 * hf + 512]), "sel": sel,
        })
        in_maps.append(m)
    res = run_bass_kernel_spmd(_NC, in_maps, core_ids=list(range(8)))
    global _LAST
    _LAST = res
    out = np.empty((4, SEQ, 1024), np.float32)
    for core in range(8):
        b, hf = core // 2, core % 2
        o = np.asarray(res.results[core]["outT"])
        out[b, hf * NTOK:(hf + 1) * NTOK, :] = o.transpose(1, 0, 2).reshape(1024, NTOK).T
    return out
```

```python
import math
from contextlib import ExitStack
import numpy as np
import concourse.bass as bass
import concourse.mybir as mybir
from concourse.bass_utils import run_bass_kernel_spmd

F32 = mybir.dt.float32
BF = mybir.dt.bfloat16
ALU = mybir.AluOpType
AF = mybir.ActivationFunctionType
NTOK = 2048
SEQ = 4096
EPS = 1e-6
PAIRS = [[0, 1], [2, 3], [4, 5], [6, 7]]
_DBG_NOCC = 0
_DBG_STAGE = 0


class _Stop(Exception):
    pass


class Buf:
    def __init__(self, name=""):
        self.name = name
        self.w = None
        self.r = {}


class Prog:
    ENGS = ["pe", "act", "dve", "pool", "sync"]

    def __init__(self, ndma=48):
        self.ops = {e: [] for e in self.ENGS}
        self.cnt = {e: 0 for e in self.ENGS}
        self.waited = {}
        self.ndma = ndma
        self.dma_use = [0] * ndma
        self.pools = {"sync": list(range(0, ndma - 16)), "pool": list(range(ndma - 16, ndma))}
        self.rrq = {"sync": 0, "pool": 0}

    def _wait(self, eng, dep):
        key, val = dep
        if eng == "pe" and key == "pe":
            return
        if self.waited.get((eng, key), 0) >= val:
            return
        self.waited[(eng, key)] = val
        self.ops[eng].append(("wait", key, val))

    def _deps(self, eng, reads, writes):
        for b in reads:
            if b.w is not None:
                self._wait(eng, b.w)
        for b in writes:
            if b.w is not None:
                self._wait(eng, b.w)
            for k, v in b.r.items():
                self._wait(eng, (k, v))

    def _mark(self, me, reads, writes):
        for b in reads:
            b.r[me[0]] = max(b.r.get(me[0], 0), me[1])
        for b in writes:
            b.w = me
            b.r = {}

    def op(self, eng, fn, reads=(), writes=()):
        self._deps(eng, reads, writes)
        self.cnt[eng] += 1
        self.ops[eng].append(("op", fn))
        self._mark((eng, self.cnt[eng]), reads, writes)

    def dma(self, q, fn, reads=(), writes=()):
        pl = self.pools[q]
        i = pl[self.rrq[q] % len(pl)]
        self.rrq[q] += 1
        if self.dma_use[i] > 0:
            self._wait(q, (("d", i), 16 * self.dma_use[i]))
        self._deps(q, reads, writes)
        self.dma_use[i] += 1
        self.ops[q].append(("dma", fn, i))
        self._mark((("d", i), 16 * self.dma_use[i]), reads, writes)

    def cc(self, fn, reads=(), writes=()):
        if _DBG_NOCC:
            return
        self._deps("pool", reads, writes)
        self.cnt.setdefault("cc", 0)
        self.cnt["cc"] += 1
        self.ops["pool"].append(("cc", fn))
        self._mark(("cc", self.cnt["cc"]), reads, writes)

    def final_wait(self, eng, bufs):
        for b in bufs:
            if b.w is not None:
                self._wait(eng, b.w)

    def emit(self, nc, es):
        sems = {e: es.enter_context(nc.semaphore("s_" + e)) for e in ["pe", "act", "dve", "pool", "cc"]}
        dsem = [es.enter_context(nc.semaphore("d%d" % i)) for i in range(self.ndma)]

        def sem_of(key):
            return dsem[key[1]] if isinstance(key, tuple) else sems[key]

        block = es.enter_context(nc.Block())

        def run(name, eng):
            for o in self.ops[name]:
                if o[0] == "wait":
                    eng.wait_ge(sem_of(o[1]), o[2])
                elif o[0] == "op":
                    o[1](eng).then_inc(sems[name], 1)
                elif o[0] == "dma":
                    o[1](eng).then_inc(dsem[o[2]], 16)
                elif o[0] == "cc":
                    o[1](eng).then_inc(sems["cc"], 1)

        @block.tensor
        def _(e):
            run("pe", e)

        @block.scalar
        def _(e):
            run("act", e)

        @block.vector
        def _(e):
            run("dve", e)

        @block.gpsimd
        def _(e):
            run("pool", e)

        @block.sync
        def _(e):
            run("sync", e)


def build():
    nc = bass.Bass("TRN2", target_bir_lowering=False)
    es = ExitStack()
    P = Prog()

    def din(name, shape, dt=F32):
        return nc.dram_tensor(name, list(shape), dt, kind="ExternalInput").ap()

    xT_own = din("xT_own", [128, 8, NTOK])
    xT_full = din("xT_full", [128, 8, SEQ])
    c_col = din("c_col", [128, 8])
    w_ada = din("w_ada", [2, 1024, 6144])
    b_ada_col = din("b_ada_col", [128, 2, 48])
    nmix_col = din("nmix_col", [128, 2, 8])
    nmlp_col = din("nmlp_col", [128, 2, 8])
    nfin_col = din("nfin_col", [128, 8])
    gn_col = din("gn_col", [128, 8])
    a_re_st = din("a_re_st", [128, 16])
    a_im_st = din("a_im_st", [128, 16])
    ldt_st = din("ldt_st", [128, 16])
    bexp_re = din("bexp_re", [128, 4, 128])
    bexp_im = din("bexp_im", [128, 4, 128])
    cexp_re = din("cexp_re", [128, 16, 32])
    cexp_im = din("cexp_im", [128, 16, 32])
    d_col = din("d_col", [128, 4])
    w_glu = din("w_glu", [1024, 2048])
    w_in = din("w_in", [1024, 3088])
    w_g2 = din("w_g2", [16, 512])
    b_gate = din("b_gate", [1, 512])
    w_out = din("w_out", [1024, 1024])
    w_ff1 = din("w_ff1", [2, 1024, 4096])
    w_ff2 = din("w_ff2", [2, 4096, 1024])
    sel = din("sel", [128, 2])
    umat = din("umat", [128, 128])
    ind = din("ind", [128, 2])
    eye32 = din("eye32", [128, 32])
    outT = nc.dram_tensor("outT", [128, 8, NTOK], F32, kind="ExternalOutput").ap()
    z_ibs = [nc.dram_tensor("z_ib%d" % a, [256, SEQ], BF) for a in range(2)]
    z_obs = [nc.dram_tensor("z_ob%d" % a, [512, SEQ], BF) for a in range(2)]
    s_ib = nc.dram_tensor("s_ib", [128, 1024], F32)
    s_ob = nc.dram_tensor("s_ob", [256, 1024], F32)

    def sb(name, shape, dt=F32):
        return es.enter_context(nc.sbuf_tensor(name, list(shape), dt))

    def ps(name, shape, dt=F32):
        return es.enter_context(nc.psum_tensor(name, list(shape), dt))

    A_x = sb("A_x", [128, 8, NTOK])
    A_h = sb("A_h", [128, 8, NTOK], BF)
    A_g = sb("A_g", [128, 16384], BF)
    A_w = [sb("A_w0", [128, 8192], BF), sb("A_w1", [128, 8192], BF)]
    A_s = sb("A_s", [128, 4096])
    A_t = sb("A_t", [128, 4096], BF)
    psA = ps("psA", [128, 4, 512])
    psB = ps("psB", [128, 2, 512])
    psC = ps("psC", [128, 1024])
    bA = [Buf("psA%d" % i) for i in range(4)]
    bB = [Buf("psB%d" % i) for i in range(2)]
    bC = Buf("psC")
    b_x = [[Buf() for _ in range(4)] for _ in range(8)]
    b_h = [Buf() for _ in range(4)]
    b_w = [Buf("w0"), Buf("w1")]
    b_g = Buf("g")
    b_s = Buf("s")
    b_t = Buf("t")

    cst = sb("cst", [128, 16])
    b_c = Buf("cst")
    ones_bf = sb("ones_bf", [128, 128], BF)

    def memset(t, v):
        return lambda e: e.memset(t, v)

    CV = {"one": 1.0, "negpi": -math.pi, "eps": EPS, "zero": 0.0}
    cidx = {k: i for i, k in enumerate(CV)}
    for k, i in cidx.items():
        P.op("dve", memset(cst[:, i:i + 1], CV[k]), writes=[b_c])
    P.op("dve", memset(ones_bf[:], 1.0), writes=[b_c])

    def C(k):
        return cst[:, cidx[k]:cidx[k] + 1]

    small = {}
    b_sm = Buf("small")

    def load_small(name, ap, shape, dt=F32):
        t = sb("sm_" + name, shape, dt)
        P.dma("sync", lambda e, t=t, ap=ap: e.dma_start(out=t[:], in_=ap), writes=[b_sm])
        small[name] = t
        return t

    ccol = sb("sm_ccol", [128, 8])
    b_cc = Buf("ccol")
    P.dma("sync", lambda e: e.dma_start(out=ccol[:], in_=c_col), writes=[b_cc])
    bada = load_small("bada", b_ada_col, [128, 2, 48])
    nmix = load_small("nmix", nmix_col, [128, 2, 8])
    nmlp = load_small("nmlp", nmlp_col, [128, 2, 8])
    nfin = load_small("nfin", nfin_col, [128, 8])
    gnc = load_small("gnc", gn_col, [128, 8])
    arst = load_small("arst", a_re_st, [128, 16])
    aist = load_small("aist", a_im_st, [128, 16])
    ldst = load_small("ldst", ldt_st, [128, 16])
    dcol = load_small("dcol", d_col, [128, 4])
    selt = load_small("selt", sel, [128, 2])
    umt = load_small("umt", umat, [128, 128])
    indt = load_small("indt", ind, [128, 2])
    eyet = load_small("eyet", eye32, [128, 32])
    BT = sb("BT", [128, 2, 4, 128], BF)
    P.dma("pool", lambda e: e.dma_start(out=BT[:, 0], in_=bexp_re), writes=[b_sm])
    P.dma("pool", lambda e: e.dma_start(out=BT[:, 1], in_=bexp_im), writes=[b_sm])
    wg2 = sb("wg2", [16, 512], BF)
    bgb = sb("bgb", [1, 512], BF)
    P.dma("pool", lambda e: e.dma_start(out=wg2[:], in_=w_g2), writes=[b_sm])
    P.dma("pool", lambda e: e.dma_start(out=bgb[:], in_=b_gate), writes=[b_sm])

    def tt(eng, out, a, b, op, reads, writes):
        P.op(eng, lambda e: e.tensor_tensor(out=out, in0=a, in1=b, op=op), reads=reads, writes=writes)

    def ts(eng, out, a, s1, s2, op0, op1, reads, writes):
        if s2 is None:
            P.op(eng, lambda e: e.tensor_scalar(out=out, in0=a, scalar1=s1, scalar2=None, op0=op0),
                 reads=reads, writes=writes)
            return
        P.op(eng, lambda e: e.tensor_scalar(out=out, in0=a, scalar1=s1, scalar2=s2, op0=op0, op1=op1),
             reads=reads, writes=writes)

    def stt(out, a, s, b, op0, op1, reads, writes):
        P.op("dve", lambda e: e.scalar_tensor_tensor(out=out, in0=a, scalar=s, in1=b, op0=op0, op1=op1),
             reads=reads, writes=writes)

    def act(out, a, func, reads, writes, bias=None, scale=1.0):
        kw = {} if bias is None else {"bias": bias}
        P.op("act", lambda e: e.activation(out=out, in_=a, func=func, scale=scale, **kw),
             reads=reads, writes=writes)

    def mm(out, pairs, reads, writes, tp=None):
        def fn(e):
            n = len(pairs)
            ins = None
            for j, (l, r) in enumerate(pairs):
                kw = {} if tp is None else {"tile_position": tp}
                ins = e.matmul(out, l, r, start=(j == 0), stop=(j == n - 1), **kw)
            return ins
        P.op("pe", fn, reads=reads, writes=writes)

    def dump(stage, src, bufs, bf=False):
        if _DBG_STAGE != stage:
            return
        bo = Buf("dbg")
        if bf:
            v = A_x[:].rearrange("p k t -> p (k t)")
            P.op("act", lambda e: e.activation(out=v, in_=src, func=AF.Identity), reads=bufs, writes=[bo])
            P.dma("sync", lambda e: e.dma_start(out=outT, in_=A_x[:]), reads=[bo], writes=[bo])
        else:
            P.dma("sync", lambda e: e.dma_start(out=src[0], in_=src[1]), reads=bufs, writes=[bo])
        P.final_wait("sync", [bo])
        raise _Stop()

    def body():
        cs = sb("cs", [128, 8])
        b_cs = Buf("cs")
        act(cs[:], ccol[:], AF.Silu, [b_cc, b_c], [b_cs])
        modc = sb("modc", [128, 2, 48])
        b_mod = Buf("mod")
        pcount = [0]

        bCs = []
        ada_n = [0]
        b_ada2 = Buf("ada2")

        def ada_issue(l, j):
            wi = 0
            half = ada_n[0] % 2 if ada_n[0] < 8 else 0
            ada_n[0] += 1
            bw_ = b_w[0] if half == 0 else b_ada2
            wv = A_w[wi][:].bitcast(F32)[:, 2048 * half:2048 * (half + 1)].rearrange("p (k c) -> p k c", k=8)
            src = w_ada[l, :, j * 256:(j + 1) * 256].rearrange("(k p) c -> p k c", p=128)
            P.dma("sync", lambda e, wv=wv, src=src: e.dma_start(out=wv, in_=src), writes=[bw_])
            for t in range(2):
                col = l * 48 + 2 * j + t
                mm(psC[:, col:col + 1],
                   [(wv[:, k, t * 128:(t + 1) * 128], cs[:, k:k + 1]) for k in range(8)],
                   [bw_, b_cs], [bC])

        def ada_evac(c0, c1):
            tt("dve", modc[:].rearrange("p l c -> p (l c)")[:, c0:c1], psC[:, c0:c1],
               bada[:].rearrange("p l c -> p (l c)")[:, c0:c1], ALU.add, [bC, b_sm], [b_mod])

        def ada_evac_all():
            pass

        def ada_piece(l, j):
            ada_issue(l, j)

        ada_rest = [(0, j) for j in range(8, 24)] + [(1, j) for j in range(24)]
        for j in range(8):
            ada_piece(0, j)
        ada_evac(0, 16)
        if _DBG_STAGE == 1:
            for lj in ada_rest:
                ada_piece(*lj)
            ada_rest = []
            ada_evac(16, 96)
        dump(1, (outT[:, 0, 0:96], modc[:].rearrange("p l c -> p (l c)")), [b_mod])
        modA = sb("modA", [128, 2, 2, 8])

        def mk_modA(l, s_):
            nrm = nmix if s_ == 0 else nmlp
            stt(modA[:, l, s_, :], modc[:, l, 24 * s_ + 8:24 * s_ + 16], 1.0, nrm[:, l, :], ALU.add, ALU.mult,
                [b_mod, b_sm], [b_mod])

        mk_modA(0, 0)

        def shift(l, s_):
            return modc[:, l, 24 * s_:24 * s_ + 8]

        def gate(l, s_):
            return modc[:, l, 24 * s_ + 16:24 * s_ + 24]

        def rstd_block(xin, nt, div, xreads, tbw=512, br=None):
            br = b_s if br is None else br
            sq = A_t[:, 0:nt * tbw].rearrange("p (k c) -> p k c", k=nt)
            act(sq, xin, AF.Square, xreads, [b_t])
            pb = psB[:, 0, 0:tbw]
            mm(pb, [(ones_bf[:], sq[:, k, :]) for k in range(nt)], [b_t, b_c], [bB[0]])
            r = A_s[:, 3584:3584 + tbw]
            ts("dve", r, pb, 1.0 / div, EPS, ALU.mult, ALU.add, [bB[0]], [br])
            act(r, r, AF.Sqrt, [br], [br])
            P.op("dve", lambda e: e.reciprocal(out=r, in_=r), reads=[br], writes=[br])
            return r

        myA = sb("myA", [128, 4])
        myB = sb("myB", [128, 4])
        for dst, srcv in ((myA, modA[:, 0, 0, :]), (myB, shift(0, 0))):
            ts("dve", dst[:], srcv[:, 0:4], selt[:, 0:1], None, ALU.mult, ALU.bypass, [b_mod, b_sm], [b_mod])
            stt(dst[:], srcv[:, 4:8], selt[:, 1:2], dst[:], ALU.mult, ALU.add, [b_mod, b_sm], [b_mod])

        uT = A_h[:].rearrange("p k t -> p (k t)").rearrange("p (k t) -> p k t", k=4)
        b_u = Buf("uT")
        TB0 = 256
        xfs = [A_s[:, i_ * 2048:(i_ + 1) * 2048].rearrange("p (k c) -> p k c", k=8) for i_ in range(2)]
        b_xf = [Buf(), Buf()]
        sq0 = A_t[:, 0:2048].rearrange("p (k c) -> p k c", k=8)
        atf = A_t[:].bitcast(F32)
        r0 = [atf[:, 1024:1280], atf[:, 1280:1536]]
        tmp0 = [atf[:, 1536:1792], atf[:, 1792:2048]]
        b_sq, b_r0, b_tmp0 = Buf(), [Buf(), Buf()], [Buf(), Buf()]
        for tb in range(SEQ // TB0):
            xi_ = tb % 2
            xf = xfs[xi_]
            P.dma("pool", lambda e, tb=tb, xf=xf: e.dma_start(out=xf, in_=xT_full[:, :, tb * TB0:(tb + 1) * TB0]),
                  writes=[b_xf[xi_]])
            act(sq0, xf, AF.Square, [b_xf[xi_]], [b_sq])
            pb = psB[:, tb % 2, 0:TB0]
            mm(pb, [(ones_bf[:], sq0[:, k, :]) for k in range(8)], [b_sq, b_c], [bB[tb % 2]])
            r = r0[xi_]
            ts("dve", r, pb, 1.0 / 1024.0, EPS, ALU.mult, ALU.add, [bB[tb % 2]], [b_r0[xi_]])
            act(r, r, AF.Sqrt, [b_r0[xi_]], [b_r0[xi_]])
            P.op("dve", lambda e, r=r: e.reciprocal(out=r, in_=r), reads=[b_r0[xi_]], writes=[b_r0[xi_]])
            for k in range(4):
                tmp = tmp0[k % 2]
                tt("dve", tmp, xf[:, k, :], r, ALU.mult, [b_xf[xi_], b_r0[xi_]], [b_tmp0[k % 2]])
                act(uT[:, k, tb * TB0:(tb + 1) * TB0], tmp, AF.Identity, [b_tmp0[k % 2], b_mod], [b_u],
                    bias=myB[:, k:k + 1], scale=myA[:, k:k + 1])
        P.op("dve", memset(cst[:, 11:12], 0.0), reads=[], writes=[b_s, b_t] + b_xf + [b_sq] + b_r0 + b_tmp0)
        dump(2, uT.rearrange("p k t -> p (k t)"), [b_u], bf=True)
        lam = sb("lam", [128, 16, 16])
        b_l = Buf("lam")
        L_ = lambda i: lam[:, i, :]
        DT, MAG, PH, SN, CSN, LR, LI, DEN, NR, T1, T2, FR, FI = range(13)
        act(L_(DT), ldst[:], AF.Exp, [b_sm, b_c], [b_l])
        tt("dve", L_(MAG), arst[:], L_(DT), ALU.mult, [b_l, b_sm], [b_l])
        act(L_(MAG), L_(MAG), AF.Exp, [b_l], [b_l])
        tt("dve", L_(PH), aist[:], L_(DT), ALU.mult, [b_l, b_sm], [b_l])
        for dst, off in ((SN, 0.0), (CSN, 0.25)):
            ts("dve", L_(T1), L_(PH), 1.0 / (2 * math.pi), off, ALU.mult, ALU.add, [b_l], [b_l])
            for _ in range(5):
                ts("dve", L_(T2), L_(T1), 0.5, None, ALU.is_gt, None, [b_l], [b_l])
                tt("dve", L_(T1), L_(T1), L_(T2), ALU.subtract, [b_l], [b_l])
            act(L_(dst), L_(T1), AF.Sin, [b_l, b_c], [b_l], scale=2 * math.pi)
        tt("dve", L_(LR), L_(MAG), L_(CSN), ALU.mult, [b_l], [b_l])
        tt("dve", L_(LI), L_(MAG), L_(SN), ALU.mult, [b_l], [b_l])
        tt("dve", L_(DEN), arst[:], arst[:], ALU.mult, [b_sm], [b_l])
        tt("dve", L_(T1), aist[:], aist[:], ALU.mult, [b_sm], [b_l])
        tt("dve", L_(DEN), L_(DEN), L_(T1), ALU.add, [b_l], [b_l])
        P.op("dve", lambda e: e.reciprocal(out=L_(DEN), in_=L_(DEN)), reads=[b_l], writes=[b_l])
        ts("dve", L_(NR), L_(LR), -1.0, None, ALU.add, ALU.bypass, [b_l], [b_l])
        tt("dve", L_(T1), L_(NR), arst[:], ALU.mult, [b_l, b_sm], [b_l])
        tt("dve", L_(T2), L_(LI), aist[:], ALU.mult, [b_l, b_sm], [b_l])
        tt("dve", L_(T1), L_(T1), L_(T2), ALU.add, [b_l], [b_l])
        tt("dve", L_(FR), L_(T1), L_(DEN), ALU.mult, [b_l], [b_l])
        tt("dve", L_(T1), L_(LI), arst[:], ALU.mult, [b_l, b_sm], [b_l])
        tt("dve", L_(T2), L_(NR), aist[:], ALU.mult, [b_l, b_sm], [b_l])
        tt("dve", L_(T1), L_(T1), L_(T2), ALU.subtract, [b_l], [b_l])
        tt("dve", L_(FI), L_(T1), L_(DEN), ALU.mult, [b_l], [b_l])
        CT = sb("CT", [128, 2, 16, 32], BF)
        CTf = A_s[:, 1024:2048].rearrange("p (r q c) -> p r q c", r=2, q=16)
        cre = A_s[:, 0:512].rearrange("p (q c) -> p q c", q=16)
        cim = A_s[:, 512:1024].rearrange("p (q c) -> p q c", q=16)
        P.dma("sync", lambda e: e.dma_start(out=cre, in_=cexp_re), writes=[b_s])
        P.dma("sync", lambda e: e.dma_start(out=cim, in_=cexp_im), writes=[b_s])
        ctmp = sb("ctmp", [128, 2, 32])
        for q in range(16):
            fr, fi = lam[:, FR, q:q + 1], lam[:, FI, q:q + 1]
            ts("dve", ctmp[:, 0, :], cim[:, q, :], fi, None, ALU.mult, ALU.bypass, [b_l, b_s], [b_l])
            stt(CTf[:, 0, q, :], cre[:, q, :], fr, ctmp[:, 0, :], ALU.mult, ALU.subtract, [b_l, b_s], [b_s])
            P.op("dve", lambda e, q=q: e.tensor_copy(out=CT[:, 0, q, :], in_=CTf[:, 0, q, :]), reads=[b_s], writes=[b_l])
            ts("dve", ctmp[:, 1, :], cim[:, q, :], fr, -1.0, ALU.mult, ALU.mult, [b_l, b_s], [b_l])
            ts("dve", ctmp[:, 0, :], cre[:, q, :], fi, None, ALU.mult, ALU.bypass, [b_l, b_s], [b_l])
            tt("dve", CT[:, 1, q, :], ctmp[:, 1, :], ctmp[:, 0, :], ALU.subtract, [b_l], [b_l])
            tt("dve", CTf[:, 1, q, :], ctmp[:, 0, :], ctmp[:, 1, :], ALU.subtract, [b_l], [b_s])
        LP = sb("LP", [128, 12, 3, 16])
        P.op("dve", lambda e: e.tensor_copy(out=LP[:, 0, 0, :], in_=L_(LR)), reads=[b_l], writes=[b_l])
        P.op("dve", lambda e: e.tensor_copy(out=LP[:, 0, 1, :], in_=L_(LI)), reads=[b_l], writes=[b_l])
        for k in range(12):
            ts("dve", LP[:, k, 2, :], LP[:, k, 1, :], -1.0, None, ALU.mult, ALU.bypass, [b_l], [b_l])
            if k < 11:
                tt("dve", L_(T1), LP[:, k, 0, :], LP[:, k, 0, :], ALU.mult, [b_l], [b_l])
                tt("dve", L_(T2), LP[:, k, 1, :], LP[:, k, 1, :], ALU.mult, [b_l], [b_l])
                tt("dve", LP[:, k + 1, 0, :], L_(T1), L_(T2), ALU.subtract, [b_l], [b_l])
                tt("dve", L_(T1), LP[:, k, 0, :], LP[:, k, 1, :], ALU.mult, [b_l], [b_l])
                ts("dve", LP[:, k + 1, 1, :], L_(T1), 2.0, None, ALU.mult, ALU.bypass, [b_l], [b_l])

        L = 8
        NCH = SEQ // L
        LPS = sb("LPS", [128, L, 3, 16])
        for c3 in range(3):
            P.op("dve", lambda e, c3=c3: e.tensor_copy(out=LPS[:, 0, c3, :], in_=LP[:, 0, c3, :]),
                 reads=[b_l], writes=[b_l])
        for s1 in range(1, L):
            pr, pi_ = LPS[:, s1 - 1, 0, :], LPS[:, s1 - 1, 1, :]
            tt("dve", L_(T1), pr, LP[:, 0, 0, :], ALU.mult, [b_l], [b_l])
            tt("dve", L_(T2), pi_, LP[:, 0, 1, :], ALU.mult, [b_l], [b_l])
            tt("dve", LPS[:, s1, 0, :], L_(T1), L_(T2), ALU.subtract, [b_l], [b_l])
            tt("dve", L_(T1), pr, LP[:, 0, 1, :], ALU.mult, [b_l], [b_l])
            tt("dve", L_(T2), pi_, LP[:, 0, 0, :], ALU.mult, [b_l], [b_l])
            tt("dve", LPS[:, s1, 1, :], L_(T1), L_(T2), ALU.add, [b_l], [b_l])
            ts("dve", LPS[:, s1, 2, :], LPS[:, s1, 1, :], -1.0, None, ALU.mult, ALU.bypass, [b_l], [b_l])
        CH = A_w[0][:].bitcast(F32)[:, 2048:4096].rearrange("p (a r n) -> p a r n", a=2, r=2)
        bCH = [[Buf(), Buf()], [Buf(), Buf()]]
        P.op("dve", memset(cst[:, 13:14], 0.0), reads=[], writes=[b_ada2] + sum(bCH, []))
        X = A_x[:].rearrange("p k t -> p (k t)").rearrange("p (a r t) -> p a r t", a=2, r=2)
        bX = [[Buf(), Buf()], [Buf(), Buf()]]
        zT = A_g[:].rearrange("p (k t) -> p k t", k=4)
        b_z = Buf("zT")
        xb = A_s[:].bitcast(BF).rearrange("p (r t) -> p r t", r=2)
        ysc = A_t[:].bitcast(F32)[:, 0:1536].rearrange("p (a c) -> p a c", a=3)
        b_y = b_t
        Sb = A_t[:, 3072:4096].rearrange("p (r n) -> p r n", r=2)
        b_sb = Buf("Sb")
        P.op("dve", memset(Sb[:, :, 0:1], 0.0), reads=[], writes=[b_sb, b_t])
        CQ = A_w[1][:].rearrange("p (q r s c) -> p q r s c", q=16, r=2, s=L)
        b_cq = Buf("CQ")
        for s1 in range(L):
            lr_b = LPS[:, s1, 0, :].unsqueeze(2).broadcast_to([128, 16, 32])
            li_b = LPS[:, s1, 1, :].unsqueeze(2).broadcast_to([128, 16, 32])
            cfr, cfi = CTf[:, 0], CTf[:, 1]
            t1 = A_s[:, 2048:2560].rearrange("p (q c) -> p q c", q=16)
            t2 = A_s[:, 2560:3072].rearrange("p (q c) -> p q c", q=16)
            tt("dve", t1, cfr, lr_b, ALU.mult, [b_s, b_l], [b_s])
            tt("dve", t2, cfi, li_b, ALU.mult, [b_s, b_l], [b_s])
            tt("dve", CQ[:, :, 0, s1, :], t1, t2, ALU.subtract, [b_s], [b_cq, b_w[1]])
            tt("dve", t1, cfr, li_b, ALU.mult, [b_s, b_l], [b_s])
            tt("dve", t2, cfi, lr_b, ALU.mult, [b_s, b_l], [b_s])
            tt("dve", t1, t1, t2, ALU.add, [b_s], [b_s])
            ts("dve", CQ[:, :, 1, s1, :], t1, -1.0, None, ALU.mult, ALU.bypass, [b_s], [b_cq])
        pcnt = 0

        def cview(a_, ri):
            return X[:, a_, ri, :].rearrange("p (c j) -> p c j", j=L)

        def hs_level(src, dst, bs, bd, a, b, nb, d, n, sl):
            sr, si, dr, di = src[0], src[1], dst[0], dst[1]
            stt(sl(dr, d, n), sl(sr, 0, n - d), a, sl(sr, d, n), ALU.mult, ALU.add, [bs[0], b_l], [bd[0]])
            stt(sl(di, d, n), sl(si, 0, n - d), a, sl(si, d, n), ALU.mult, ALU.add, [bs[1], b_l], [bd[1]])
            h0 = d // 2 if d > 1 else 0
            bh = [Buf(), Buf()]
            P.op("dve", lambda e: e.tensor_copy(out=sl(dr, h0, d), in_=sl(sr, h0, d)), reads=[bs[0]], writes=[bh[0]])
            P.op("dve", lambda e: e.tensor_copy(out=sl(di, h0, d), in_=sl(si, h0, d)), reads=[bs[1]], writes=[bh[1]])
            stt(sl(dr, d, n), sl(si, 0, n - d), nb, sl(dr, d, n), ALU.mult, ALU.add, [bs[1], b_l, bh[0]], [bd[0]])
            stt(sl(di, d, n), sl(sr, 0, n - d), b, sl(di, d, n), ALU.mult, ALU.add, [bs[0], b_l, bh[1]], [bd[1]])

        def madd(vr, vi, dsel, ssel, k, q, bx):
            a, b, nb = LP[:, k, 0, q:q + 1], LP[:, k, 1, q:q + 1], LP[:, k, 2, q:q + 1]
            dr, di, sr, si = dsel(vr), dsel(vi), ssel(vr), ssel(vi)
            stt(dr, sr, a, dr, ALU.mult, ALU.add, [bx[0], b_l], [bx[0]])
            stt(di, si, a, di, ALU.mult, ALU.add, [bx[1], b_l], [bx[1]])
            stt(dr, si, nb, dr, ALU.mult, ALU.add, [bx[0], bx[1], b_l], [bx[0]])
            stt(di, sr, b, di, ALU.mult, ALU.add, [bx[0], bx[1], b_l], [bx[1]])

        pcb = [0, 0]
        b_g2 = [Buf(), Buf()]
        P.op("dve", memset(cst[:, 9:10], 0.0), reads=[], writes=[b_t] + b_g2)
        Ddiag = sb("Ddiag", [128, 4, 32], BF)
        for i_ in range(4):
            ts("dve", Ddiag[:, i_, :], eyet[:], dcol[:, i_:i_ + 1], None, ALU.mult, ALU.bypass, [b_sm], [b_l])

        def stA(q):
            i, q4, xp = q // 4, q % 4, q % 2
            rows = slice(32 * q4, 32 * q4 + 32)
            for ri in range(2):
                for tb in range(8):
                    pb = pcb[0] % 2
                    pcb[0] += 1
                    mm(psB[:, pb, :], [(BT[rows, ri, i, :], uT[rows, i, tb * 512:(tb + 1) * 512])],
                       [b_sm, b_u], [bB[pb]], tp=(32 * q4, 0))
                    act(X[:, xp, ri, tb * 512:(tb + 1) * 512], psB[:, pb, :], AF.Identity, [bB[pb], b_c],
                        [bX[xp][ri]])

        def stB1(q):
            xp = q % 2
            vr, vi = cview(xp, 0), cview(xp, 1)
            madd(vr, vi, lambda v: v[:, :, 1::2], lambda v: v[:, :, 0::2], 0, q, bX[xp])
            madd(vr, vi, lambda v: v[:, :, 3::4], lambda v: v[:, :, 1::4], 1, q, bX[xp])
            madd(vr, vi, lambda v: v[:, :, 7], lambda v: v[:, :, 3], 2, q, bX[xp])
            madd(vr, vi, lambda v: v[:, :, 5], lambda v: v[:, :, 3], 1, q, bX[xp])
            madd(vr, vi, lambda v: v[:, :, 2::2], lambda v: v[:, :, 1:6:2], 0, q, bX[xp])

        def stB2(q):
            xp = q % 2
            for ri in range(2):
                act(CH[:, 0, ri, :], cview(xp, ri)[:, :, L - 1], AF.Identity, [bX[xp][ri], b_c], [bCH[0][ri]])
            for k in range(9):
                s_, d_ = k % 2, 1 - (k % 2)
                hs_level([CH[:, s_, 0, :], CH[:, s_, 1, :]], [CH[:, d_, 0, :], CH[:, d_, 1, :]], bCH[s_], bCH[d_],
                         LP[:, k + 3, 0, q:q + 1], LP[:, k + 3, 1, q:q + 1], LP[:, k + 3, 2, q:q + 1], 1 << k, NCH,
                         lambda v, lo, hi: v[:, lo:hi])

        def stCconv(q):
            i, q4, xp = q // 4, q % 4, q % 2
            rows = slice(32 * q4, 32 * q4 + 32)
            for ri in range(2):
                act(Sb[:, ri, 1:NCH], CH[:, 1, ri, 0:NCH - 1], AF.Identity, [bCH[1][ri], b_c], [b_sb])
                act(xb[:, ri, :], X[:, xp, ri, :], AF.Identity, [bX[xp][ri], b_c], [b_s])
            for tb in range(4):
                ymm_emit(q, tb)

        def ymm_emit(q, tb):
            i, q4 = q // 4, q % 4
            rows = slice(32 * q4, 32 * q4 + 32)
            tsl = slice(tb * 512, (tb + 1) * 512)
            pb = (pcb[1] + tb) % 4
            pv = psA[rows, pb, :].rearrange("p (c j) -> p c j", j=L)

            def ymm(e):
                tp = (0, 32 * q4)
                e.matmul(psA[rows, pb, :], Ddiag[rows, i, :], uT[rows, i, tsl], start=True, stop=False,
                         tile_position=(32 * q4, 32 * q4))
                e.matmul(psA[rows, pb, :], CT[:, 0, q, :], xb[:, 0, tsl], start=False, stop=False, tile_position=tp)
                ins = e.matmul(psA[rows, pb, :], CT[:, 1, q, :], xb[:, 1, tsl], start=False, stop=False,
                               tile_position=tp)
                for s1 in range(L):
                    for ri in range(2):
                        ins = e.matmul(pv[:, :, s1], CQ[:, q, ri, s1, :], Sb[:, ri, tb * 64:(tb + 1) * 64],
                                       start=False, stop=(ri == 1), tile_position=tp,
                                       skip_group_check=True)
                return ins
            P.op("pe", ymm, reads=[b_l, b_s, b_cq, b_sb, b_u], writes=[bA[pb]])

        def stCevac(q):
            i, q4 = q // 4, q % 4
            rows = slice(32 * q4, 32 * q4 + 32)
            for tb in range(8):
                tsl = slice(tb * 512, (tb + 1) * 512)
                pb = (pcb[1] + tb) % 4
                act(zT[rows, i, tsl], psA[rows, pb, :], AF.Identity, [bA[pb], b_c], [b_z], bias=C("zero")[rows])
                if tb + 4 < 8:
                    ymm_emit(q, tb + 4)
            pcb[1] += 8
            if q4 == 3:
                def gelu_prep(tb):
                    tsl = slice(tb * 512, (tb + 1) * 512)
                    yb, t1, bt1 = zT[:, i, tsl], ysc[:, tb % 2, :], b_g2[tb % 2]
                    act(t1, yb, AF.Square, [b_z, b_c], [bt1], scale=math.sqrt(0.044715))
                    stt(t1, t1, 1.0, yb, ALU.add, ALU.mult, [bt1, b_z], [bt1])
                    act(t1, t1, AF.Sigmoid, [bt1, b_c], [bt1], scale=2.0 * math.sqrt(2.0 / math.pi))

                gelu_prep(0)
                for tb in range(8):
                    if tb + 1 < 8:
                        gelu_prep(tb + 1)
                    tsl = slice(tb * 512, (tb + 1) * 512)
                    yb = zT[:, i, tsl]
                    tt("dve", yb, yb, ysc[:, tb % 2, :], ALU.mult, [b_g2[tb % 2], b_z], [b_z])

        stA(0)
        stB1(0)
        stB2(0)
        for q in range(16):
            if q + 1 < 16:
                stA(q + 1)
            stCconv(q)
            for _ in range(2):
                if ada_rest:
                    ada_issue(*ada_rest.pop(0))
            if q + 1 < 16:
                stB1(q + 1)
            stCevac(q)
            if q + 1 < 16:
                stB2(q + 1)
            if q >= 12:
                for _ in range(2):
                    if ada_rest:
                        ada_issue(*ada_rest.pop(0))
        P.op("dve", memset(cst[:, 10:11], 0.0), reads=[], writes=[b_t, b_sb, b_cq, b_w[1], b_w[0], bC] + bCs + b_g2 + sum(bCH, []))

        dump(3, zT.rearrange("p k t -> p (k t)"), [b_z], bf=True)
        b_zo = Buf("z_ob")
        for a in range(2):
            P.dma("sync", lambda e, a=a: e.dma_start(out=z_ibs[a].ap().rearrange("(j p) t -> p j t", p=128),
                                                      in_=zT[:, 2 * a:2 * a + 2, :]), reads=[b_z], writes=[b_zo])
        for a in range(2):
            P.cc(lambda e, a=a: e.collective_compute("AllGather", ALU.bypass, replica_groups=PAIRS,
                                                     ins=[z_ibs[a][:, :]], outs=[z_obs[a][:, :]]),
                 reads=[b_zo], writes=[b_zo])
        while ada_rest:
            ada_piece(*ada_rest.pop(0))
        ada_evac(16, 96)
        mk_modA(0, 1)
        mk_modA(1, 0)
        mk_modA(1, 1)
        for k in range(8):
            P.dma("sync", lambda e, k=k: e.dma_start(out=A_x[:, k, :], in_=xT_own[:, k, :]),
                  reads=[], writes=[bX[0][0], bX[0][1], bX[1][0], bX[1][1]] + b_x[k])
        zc = A_t[:].rearrange("p (k c) -> p k c", k=8)
        zc2 = A_s[:].bitcast(BF)[:, 0:4096].rearrange("p (k c) -> p k c", k=8)
        for tb in range(4):
            for dstz, off, bz in ((zc, 0, b_t), (zc2, NTOK, b_s)):
                for a in range(2):
                    for r in range(2):
                        srcz = z_obs[a][r * 256:(r + 1) * 256, off + tb * 512:off + (tb + 1) * 512].rearrange(
                            "(j p) t -> p j t", p=128)
                        t0 = r * 4 + a * 2
                        P.dma("sync", lambda e, d_=dstz[:, t0:t0 + 2, :], s_=srcz: e.dma_start(out=d_, in_=s_),
                              reads=[b_zo], writes=[bz])
            hv = A_h[:, :, tb * 512:(tb + 1) * 512]
            ts("dve", hv, zc, selt[:, 0:1], None, ALU.mult, ALU.bypass, [b_t, b_sm, b_u], [b_h[tb], b_u])
            stt(hv, zc2, selt[:, 1:2], hv, ALU.mult, ALU.add, [b_s, b_sm], [b_h[tb]])

        wq_cnt = [0]

        def load_w(dst, src, bw):
            P.dma("pool", lambda e: e.dma_start(out=dst, in_=src), writes=[bw])

        for j in range(2):
            wi = wq_cnt[0] % 2
            wq_cnt[0] += 1
            W = A_w[wi][:].rearrange("p (k c) -> p k c", k=8)
            load_w(W[:, :, 0:512], w_glu[:, j * 512:(j + 1) * 512].rearrange("(k p) c -> p k c", p=128), b_w[wi])
            load_w(W[:, :, 512:1024], w_glu[:, 1024 + j * 512:1024 + (j + 1) * 512].rearrange("(k p) c -> p k c", p=128),
                   b_w[wi])
            for tb in range(4):
                tsl = slice(tb * 512, (tb + 1) * 512)
                for t in range(4):
                    n = 4 * j + t
                    pb = pcnt % 4
                    pcnt += 1
                    mm(psA[:, pb, :], [(W[:, k, 512 + t * 128:512 + (t + 1) * 128], A_h[:, k, tsl]) for k in range(8)],
                       [b_w[wi], b_h[tb]], [bA[pb]])
                    sg = ysc[:, 0, :]
                    act(sg, psA[:, pb, :], AF.Sigmoid, [bA[pb], b_c], [b_y])
                    pb2 = pcnt % 4
                    pcnt += 1
                    mm(psA[:, pb2, :], [(W[:, k, t * 128:(t + 1) * 128], A_h[:, k, tsl]) for k in range(8)],
                       [b_w[wi], b_h[tb]], [bA[pb2]])
                    y = ysc[:, 1, :]
                    tt("dve", y, psA[:, pb2, :], sg, ALU.mult, [bA[pb2], b_y], [b_y])
                    stt(A_x[:, n, tsl], y, gate(0, 0)[:, n:n + 1], A_x[:, n, tsl], ALU.mult, ALU.add,
                        [b_y, b_mod, b_x[n][tb]], [b_x[n][tb]])

        b_tm = [Buf(), Buf(), Buf(), Buf()]
        b_rr = Buf("rstd")

        def norm_mod(l, s_):
            P.op("dve", memset(cst[:, 15:16], 0.0), reads=[], writes=[b_s, b_rr] + b_tm)
            for tb in range(4):
                tsl = slice(tb * 512, (tb + 1) * 512)
                xr = [b_x[k][tb] for k in range(8)]
                r = rstd_block(A_x[:, :, tsl], 8, 1024.0, xr, br=b_rr)
                for k in range(8):
                    tmp = A_s[:, (k % 4) * 512:(k % 4 + 1) * 512]
                    tt("dve", tmp, A_x[:, k, tsl], r, ALU.mult, [b_rr, b_x[k][tb]], [b_tm[k % 4]])
                    act(A_h[:, k, tsl], tmp, AF.Identity, [b_tm[k % 4], b_mod], [b_h[tb]],
                        bias=shift(l, s_)[:, k:k + 1], scale=modA[:, l, s_, k:k + 1])
            P.op("dve", memset(cst[:, 15:16], 0.0), reads=[], writes=[b_s, b_rr] + b_tm)

        bAT = [Buf(), Buf()]

        b_ys = [Buf(), Buf(), Buf()]

        def mlp(l):
            nonlocal pcnt
            aT = A_g[:, 0:4096].rearrange("p (a f c) -> p a f c", a=2, f=4)
            first = [True]
            Wv_ = {}
            P.op("dve", memset(cst[:, 12:13], 0.0), reads=[], writes=[b_t] + b_ys)

            def s1(n):
                nonlocal pcnt
                e8, tb = n // 4, n % 4
                if tb == 0:
                    wi = wq_cnt[0] % 2
                    wq_cnt[0] += 1
                    W1 = A_w[wi][:, 0:4096].rearrange("p (k c) -> p k c", k=8)
                    W2 = A_w[wi][:, 4096:8192].rearrange("p (k c) -> p k c", k=4)
                    load_w(W1, w_ff1[l, :, e8 * 512:(e8 + 1) * 512].rearrange("(k p) c -> p k c", p=128), b_w[wi])
                    load_w(W2, w_ff2[l, e8 * 512:(e8 + 1) * 512, :].rearrange("(k p) c -> p k c", p=128), b_w[wi])
                    Wv_[e8] = (wi, W1, W2)
                wi, W1, W2 = Wv_[e8]
                tsl = slice(tb * 512, (tb + 1) * 512)
                ap_ = n % 2
                for f in range(4):
                    pb = pcnt % 4
                    pcnt += 1
                    mm(psA[:, pb, :], [(W1[:, k, f * 128:(f + 1) * 128], A_h[:, k, tsl]) for k in range(8)],
                       [b_w[wi], b_h[tb]], [bA[pb]])
                    sl3 = (4 * n + f) % 3
                    rl = ysc[:, sl3, :]
                    act(rl, psA[:, pb, :], AF.Relu, [bA[pb], b_c], [b_ys[sl3]])
                    extra = [b_g, b_z] if first[0] else []
                    first[0] = False
                    act(aT[:, ap_, f, :], rl, AF.Square, [b_ys[sl3], b_c], [bAT[ap_]] + extra)

            def s2(n):
                nonlocal pcnt
                e8, tb = n // 4, n % 4
                wi, W1, W2 = Wv_[e8]
                tsl = slice(tb * 512, (tb + 1) * 512)
                ap_ = n % 2
                for nn in range(8):
                    pb = pcnt % 2
                    pcnt += 1
                    mm(psB[:, pb, :], [(W2[:, f, nn * 128:(nn + 1) * 128], aT[:, ap_, f, :]) for f in range(4)],
                       [b_w[wi], bAT[ap_]], [bB[pb]])
                    stt(A_x[:, nn, tsl], psB[:, pb, :], gate(l, 1)[:, nn:nn + 1], A_x[:, nn, tsl], ALU.mult, ALU.add,
                        [bB[pb], b_mod, b_x[nn][tb]], [b_x[nn][tb]])

            s1(0)
            for n in range(32):
                if n + 1 < 32:
                    s1(n + 1)
                s2(n)
            P.op("dve", memset(cst[:, 12:13], 0.0), reads=[], writes=[b_t] + b_ys)

        allx = [b_x[k][tb] for k in range(8) for tb in range(4)]
        dump(4, (outT, A_x[:]), allx)
        norm_mod(0, 1)
        mlp(0)
        dump(5, (outT, A_x[:]), allx)

        norm_mod(1, 0)
        Wq = A_w[0][:, 0:4096].rearrange("p (k c) -> p k c", k=8)
        Wk = A_w[0][:, 4096:8192].rearrange("p (k c) -> p k c", k=8)
        Wv = A_w[1][:].rearrange("p (k c) -> p k c", k=8)
        Wr = A_g[:, 0:8192].rearrange("p (k c) -> p k c", k=8)
        Wo = A_g[:, 8192:16384].rearrange("p (k c) -> p k c", k=8)
        Wg = sb("Wg", [128, 8, 16], BF)
        win = lambda a, b: w_in[:, a:b].rearrange("(k p) c -> p k c", p=128)
        load_w(Wq, win(0, 512), b_w[0])
        load_w(Wk, win(512, 1024), b_w[0])
        load_w(Wv, win(1024, 2048), b_w[1])
        load_w(Wg[:], win(2048, 2064), b_g)
        P.dma("pool", lambda e: e.dma_start(out=Wr, in_=win(2064, 3088)), writes=[b_g, bAT[0], bAT[1]])
        load_w(Wo, w_out.rearrange("(k p) c -> p k c", p=128), b_g)

        S = sb("S", [128, 4, 256])
        Sbf = sb("Sbf", [128, 1, 4, 256], BF)
        b_S = Buf("S")
        b_Sbf = [Buf(), Buf()]
        Sbfs = [Sbf[:, 0],
                LP[:].rearrange("p a b c -> p (a b c)").bitcast(BF)[:, 0:1024].rearrange("p (h e) -> p h e", h=4)]
        gl = sb("gl", [16, 128], BF)
        la = A_s[:, 0:512]
        Ef = A_s[:, 512:1024]
        gsc = A_s[:, 1024:2048].rearrange("p (k t) -> p k t", k=8)
        rsd = A_s[:, 2048:2560].rearrange("p (k t) -> p k t", k=4)
        asb = A_s[:].bitcast(BF)
        kdec = [A_t[:, 0:512], asb[:, 5376:5888]]
        vtok = [A_t[:, 512:1536], asb[:, 5888:6912]]
        qTt = [A_t[:, 3584:4096].rearrange("p (k t) -> p k t", k=4),
               asb[:, 6912:7424].rearrange("p (k t) -> p k t", k=4)]
        srs = [BT[:].rearrange("p a b c -> p (a b c)").rearrange("p (k t) -> p k t", k=8),
               CT[:].rearrange("p a b c -> p (a b c)").rearrange("p (k t) -> p k t", k=8)]
        osq = A_t[:, 1536:2560].rearrange("p (k t) -> p k t", k=8)
        gat = A_t[:, 2560:3584].rearrange("p (k t) -> p k t", k=8)
        dec = sb("dec", [128, 2, 4, 2])
        b_gl, b_la, b_gs, b_osq, b_rsd, b_gat = [Buf() for _ in range(6)]
        b_kd, b_v, b_dec, b_q, b_sr = [[Buf(), Buf()] for _ in range(5)]
        P.op("dve", memset(S[:], 0.0), writes=[b_S])

        def gla_front(tt_, final):
            st = tt_ % 2
            tk = slice(tt_ * 128, (tt_ + 1) * 128)
            tb = tt_ // 4
            hT = lambda k: A_h[:, k, tk]
            mm(psB[0:16, 0, 0:128], [(Wg[:, k, :], hT(k)) for k in range(8)], [b_g, b_h[tb]], [bB[0]])
            act(gl[:], psB[0:16, 0, 0:128], AF.Identity, [bB[0], b_c], [b_gl], bias=C("zero")[0:16])
            mm(psA[:, 0, :], [(ones_bf[0:1, :], bgb[:]), (gl[:], wg2[:])], [b_c, b_sm, b_gl], [bA[0]])
            act(Ef, psA[:, 0, :], AF.Exp, [bA[0], b_c], [b_la], scale=-1.0)
            act(la, Ef, AF.Ln, [b_la, b_c], [b_la], bias=C("one"))
            mm(psA[:, 1, :], [(umt[:], la)], [b_sm, b_la], [bA[1]])
            for h in range(4):
                mm(psB[:, 1, 2 * h:2 * h + 2], [(la[:, h * 128:(h + 1) * 128], indt[:])], [b_la, b_sm], [bB[1]])
            mm(psA[:, 2, :], [(hT(k), Wk[:, k, :]) for k in range(8)], [b_w[0], b_h[tb]], [bA[2]])
            act(Ef, psA[:, 1, :], AF.Exp, [bA[1], b_c], [b_la], scale=-1.0 / 16.0)
            act(dec[:, st].rearrange("p h c -> p (h c)"), psB[:, 1, 0:8], AF.Exp, [bB[1], b_c], [b_dec[st]],
                scale=-1.0 / 16.0)
            tt("dve", kdec[st], psA[:, 2, :], Ef, ALU.mult, [bA[2], b_la], [b_kd[st]])
            for hh in range(2):
                mm(psA[:, 3, :], [(hT(k), Wv[:, k, hh * 512:(hh + 1) * 512]) for k in range(8)],
                   [b_w[1], b_h[tb]], [bA[3]])
                act(vtok[st][:, hh * 512:(hh + 1) * 512], psA[:, 3, :], AF.Identity, [bA[3], b_c], [b_v[st]])
            if final:
                for h in range(4):
                    mm(psB[:, 0, h * 128:(h + 1) * 128], [(Wq[:, k, h * 128:(h + 1) * 128], hT(k)) for k in range(8)],
                       [b_w[0], b_h[tb]], [bB[0]])
                act(qTt[st].rearrange("p k t -> p (k t)"), psB[:, 0, :], AF.Identity, [bB[0], b_c], [b_q[st]],
                    scale=128.0 ** -0.5)
                pr = psA[:, 0:2, :].rearrange("p a c -> p (a c)")
                for t8 in range(8):
                    mm(pr[:, t8 * 128:(t8 + 1) * 128], [(Wr[:, k, t8 * 128:(t8 + 1) * 128], hT(k)) for k in range(8)],
                       [b_g, b_h[tb]], [bA[0], bA[1]])
                act(srs[st].rearrange("p k t -> p (k t)"), pr, AF.Silu, [bA[0], bA[1], b_c], [b_sr[st]])

        def gla_back(tt_, final):
            st = tt_ % 2
            tk = slice(tt_ * 128, (tt_ + 1) * 128)
            tb = tt_ // 4
            for cc in range(2):
                rws = slice(64 * cc, 64 * cc + 64)
                for h in range(4):
                    mm(psC[:, h * 256:(h + 1) * 256],
                       [(kdec[st][rws, h * 128:(h + 1) * 128], vtok[st][rws, h * 256:(h + 1) * 256])],
                       [b_kd[st], b_v[st]], [bC], tp=(64 * cc, 0))
                for h in range(4):
                    stt(S[:, h, :], S[:, h, :], dec[:, st, h, cc:cc + 1], psC[:, h * 256:(h + 1) * 256], ALU.mult,
                        ALU.add, [b_S, b_dec[st], bC], [b_S])
                if final:
                    act(Sbfs[cc].rearrange("p h e -> p (h e)"), S[:].rearrange("p h e -> p (h e)"), AF.Identity,
                        [b_S, b_c], [b_Sbf[cc]])
            if final:
                for cc in range(2):
                    for h in range(4):
                        for e2 in range(2):
                            t8 = 2 * h + e2
                            mm(psB[:, 1, t8 * 64:(t8 + 1) * 64],
                               [(Sbfs[cc][:, h, e2 * 128:(e2 + 1) * 128], qTt[st][:, h, 64 * cc:64 * cc + 64])],
                               [b_Sbf[cc], b_q[st]], [bB[1]])
                    act(gsc[:, :, 64 * cc:64 * cc + 64], psB[:, 1, :].rearrange("p (k t) -> p k t", k=8),
                        AF.Identity, [bB[1], b_c], [b_gs])
            if final:
                act(osq, gsc, AF.Square, [b_gs, b_c], [b_osq])
                for h in range(4):
                    mm(psB[:, 0, h * 128:(h + 1) * 128],
                       [(ones_bf[:], osq[:, 2 * h, :]), (ones_bf[:], osq[:, 2 * h + 1, :])], [b_osq, b_c], [bB[0]])
                ts("dve", A_s[:, 2048:2560], psB[:, 0, :], 1.0 / 256.0, EPS, ALU.mult, ALU.add, [bB[0]], [b_rsd])
                act(A_s[:, 2048:2560], A_s[:, 2048:2560], AF.Sqrt, [b_rsd], [b_rsd])
                P.op("dve", lambda e: e.reciprocal(out=A_s[:, 2048:2560], in_=A_s[:, 2048:2560]),
                     reads=[b_rsd], writes=[b_rsd])
                for t8 in range(8):
                    stt(gsc[:, t8, :], gsc[:, t8, :], gnc[:, t8:t8 + 1], rsd[:, t8 // 2, :], ALU.mult, ALU.mult,
                        [b_gs, b_sm, b_rsd], [b_gs])
                tt("dve", gat, gsc, srs[st], ALU.mult, [b_gs, b_sr[st]], [b_gat])
                po = psA[:, 2:4, :].rearrange("p a c -> p (a c)")
                for n in range(8):
                    mm(po[:, n * 128:(n + 1) * 128], [(Wo[:, k, n * 128:(n + 1) * 128], gat[:, k, :]) for k in range(8)],
                       [b_g, b_gat], [bA[2], bA[3]])
                for n in range(8):
                    stt(A_x[:, n, tk], po[:, n * 128:(n + 1) * 128], gate(1, 0)[:, n:n + 1], A_x[:, n, tk],
                        ALU.mult, ALU.add, [bA[2], bA[3], b_mod, b_x[n][tb]], [b_x[n][tb]])

        def gla_pass(final, first_front_done=False):
            if not first_front_done:
                gla_front(0, final)
            for tt_ in range(16):
                if tt_ + 1 < 16:
                    gla_front(tt_ + 1, final)
                gla_back(tt_, final)

        gla_tmp = [b_s, b_t, b_gl, b_la, b_gs, b_osq, b_rsd, b_gat, b_sm, b_l] + sum([b_kd, b_v, b_dec, b_q, b_sr], [])
        bar = sb("bar", [128, 1])

        def barrier(bufs):
            P.op("dve", memset(bar[:], 0.0), writes=bufs)

        barrier(gla_tmp)
        gla_pass(False)
        b_so = Buf("s_ob")
        P.dma("sync", lambda e: e.dma_start(out=s_ib[:, :], in_=S[:].rearrange("p h e -> p (h e)")), reads=[b_S], writes=[b_so])
        P.cc(lambda e: e.collective_compute("AllGather", ALU.bypass, replica_groups=PAIRS,
                                            ins=[s_ib[:, :]], outs=[s_ob[:, :]]), reads=[b_so], writes=[b_so])
        P.dma("sync", lambda e: e.dma_start(out=S[:].rearrange("p h e -> p (h e)"), in_=s_ob[0:128, :]),
              reads=[b_so], writes=[b_S])
        gla_front(0, True)
        ts("dve", S[:].rearrange("p h e -> p (h e)"), S[:].rearrange("p h e -> p (h e)"), selt[:, 1:2], None,
           ALU.mult, ALU.bypass, [b_S, b_sm], [b_S])
        gla_pass(True, first_front_done=True)
        barrier(gla_tmp)
        dump(6, (outT, A_x[:]), allx)

        norm_mod(1, 1)
        mlp(1)
        dump(7, (outT, A_x[:]), allx)

        b_out = Buf("out")
        for tb in range(4):
            tsl = slice(tb * 512, (tb + 1) * 512)
            xr = [b_x[k][tb] for k in range(8)]
            r = rstd_block(A_x[:, :, tsl], 8, 1024.0, xr)
            for k in range(8):
                stt(A_x[:, k, tsl], A_x[:, k, tsl], nfin[:, k:k + 1], r, ALU.mult, ALU.mult,
                    [b_s, b_sm, b_x[k][tb]], [b_x[k][tb]])
            P.dma("sync", lambda e, tsl=tsl: e.dma_start(out=outT[:, :, tsl], in_=A_x[:, :, tsl]),
                  reads=xr, writes=[b_out])
        P.final_wait("sync", [b_out])
    try:
        body()
    except _Stop:
        pass
    P.emit(nc, es)
    es.close()
    return nc


_NC = None
_LAST = None


def _tiles(a2d):
    C, T = a2d.shape
    return np.ascontiguousarray(a2d.reshape(C // 128, 128, T).transpose(1, 0, 2))


def _col(v):
    return np.ascontiguousarray(v.reshape(-1, 128).T)


def kernel(x, c, w_ada, b_ada, norm_mix, norm_mlp, s5_a_re, s5_a_im, s5_log_dt, s5_b_re, s5_b_im,
           s5_c_re, s5_c_im, s5_d, s5_w_glu, gla_w_in, gla_w_gate2, gla_b_gate, gla_g_norm, gla_w_out,
           w_ff1, w_ff2, norm_final):
    global _NC
    f = lambda a: np.ascontiguousarray(np.asarray(a, dtype=np.float32))
    x, c = f(x), f(c)
    if _NC is None:
        _NC = build()
    umat = np.zeros((128, 128), np.float32)
    for s in range(128):
        for s2 in range(s + 1, (s // 64 + 1) * 64):
            umat[s2, s] = 1.0
    ind = np.zeros((128, 2), np.float32)
    ind[:64, 0] = 1.0
    ind[64:, 1] = 1.0
    common = {
        "w_ada": f(w_ada),
        "b_ada_col": np.ascontiguousarray(np.stack([_col(f(b_ada)[l]) for l in range(2)], axis=1)),
        "nmix_col": np.ascontiguousarray(np.stack([_col(f(norm_mix)[l]) for l in range(2)], axis=1)),
        "nmlp_col": np.ascontiguousarray(np.stack([_col(f(norm_mlp)[l]) for l in range(2)], axis=1)),
        "nfin_col": _col(f(norm_final)), "gn_col": _col(f(gla_g_norm)[0]),
        "w_glu": f(s5_w_glu)[0], "w_in": f(gla_w_in)[0], "w_g2": f(gla_w_gate2)[0], "b_gate": f(gla_b_gate),
        "w_out": f(gla_w_out)[0], "w_ff1": f(w_ff1), "w_ff2": f(w_ff2), "umat": umat, "ind": ind,
        "eye32": np.ascontiguousarray(np.tile(np.eye(32, dtype=np.float32), (4, 1))),
    }
    are, aim, ldt = f(s5_a_re)[0], f(s5_a_im)[0], f(s5_log_dt)[0]
    bre, bim, cre, cim, dsk = f(s5_b_re)[0], f(s5_b_im)[0], f(s5_c_re)[0], f(s5_c_im)[0], f(s5_d)[0]
    in_maps = []
    for core in range(8):
        b, hf = core // 2, core % 2
        g0 = 32 * hf
        xT = np.ascontiguousarray(x[b].T)
        order = list(range(4 * hf, 4 * hf + 4)) + list(range(4 * (1 - hf), 4 * (1 - hf) + 4))
        xfull = _tiles(xT)[:, order, :]
        st = lambda a: np.ascontiguousarray(a[g0:g0 + 32].reshape(16, 128).T)
        bexp = []
        for bsrc in (bre, bim):
            t = np.zeros((128, 4, 128), np.float32)
            for gl_ in range(32):
                i, g8 = gl_ // 8, gl_ % 8
                g2 = gl_ % 2
                t[g8 * 16:(g8 + 1) * 16, i, g2 * 64:(g2 + 1) * 64] = bsrc[g0 + gl_].T
            bexp.append(t)
        cexp = []
        for csrc in (cre, cim):
            t = np.zeros((128, 16, 32), np.float32)
            for gl_ in range(32):
                q, g2 = gl_ // 2, gl_ % 2
                t[g2 * 64:(g2 + 1) * 64, q, g2 * 16:(g2 + 1) * 16] = csrc[g0 + gl_].T
            cexp.append(t)
        sel = np.zeros((128, 2), np.float32)
        sel[:, hf] = 1.0
        m = dict(common)
        m.update({
            "xT_own": np.ascontiguousarray(_tiles(xT)[:, :, hf * NTOK:(hf + 1) * NTOK]),
            "xT_full": np.ascontiguousarray(xfull),
            "c_col": _col(c[b]),
            "a_re_st": st(are), "a_im_st": st(aim),
            "ldt_st": np.ascontiguousarray(np.repeat(ldt[g0:g0 + 32].reshape(16, 2).T, 64, axis=0)),
            "bexp_re": bexp[0], "bexp_im": bexp[1], "cexp_re": cexp[0], "cexp_im": cexp[1],
            "d_col": _col(dsk[512 * hf:512 * hf + 512]), "sel": sel,
        })
        in_maps.append(m)
    res = run_bass_kernel_spmd(_NC, in_maps, core_ids=list(range(8)))
    global _LAST
    _LAST = res
    out = np.empty((4, SEQ, 1024), np.float32)
    for core in range(8):
        b, hf = core // 2, core % 2
        o = np.asarray(res.results[core]["outT"])
        out[b, hf * NTOK:(hf + 1) * NTOK, :] = o.transpose(1, 0, 2).reshape(1024, NTOK).T
    return out
```

```python
import math
from contextlib import ExitStack
import numpy as np
import concourse.bass as bass
import concourse.mybir as mybir
from concourse.bass_utils import run_bass_kernel_spmd

F32 = mybir.dt.float32
BF = mybir.dt.bfloat16
ALU = mybir.AluOpType
AF = mybir.ActivationFunctionType
NTOK = 2048
SEQ = 4096
EPS = 1e-6
PAIRS = [[0, 1], [2, 3], [4, 5], [6, 7]]
_DBG_NOCC = 0
_DBG_STAGE = 0


class _Stop(Exception):
    pass


class Buf:
    def __init__(self, name=""):
        self.name = name
        self.w = None
        self.r = {}


class Prog:
    ENGS = ["pe", "act", "dve", "pool", "sync"]

    def __init__(self, ndma=48):
        self.ops = {e: [] for e in self.ENGS}
        self.cnt = {e: 0 for e in self.ENGS}
        self.waited = {}
        self.ndma = ndma
        self.dma_use = [0] * ndma
        self.pools = {"sync": list(range(0, ndma - 16)), "pool": list(range(ndma - 16, ndma))}
        self.rrq = {"sync": 0, "pool": 0}

    def _wait(self, eng, dep):
        key, val = dep
        if eng == "pe" and key == "pe":
            return
        if self.waited.get((eng, key), 0) >= val:
            return
        self.waited[(eng, key)] = val
        self.ops[eng].append(("wait", key, val))

    def _deps(self, eng, reads, writes):
        for b in reads:
            if b.w is not None:
                self._wait(eng, b.w)
        for b in writes:
            if b.w is not None:
                self._wait(eng, b.w)
            for k, v in b.r.items():
                self._wait(eng, (k, v))

    def _mark(self, me, reads, writes):
        for b in reads:
            b.r[me[0]] = max(b.r.get(me[0], 0), me[1])
        for b in writes:
            b.w = me
            b.r = {}

    def op(self, eng, fn, reads=(), writes=()):
        self._deps(eng, reads, writes)
        self.cnt[eng] += 1
        self.ops[eng].append(("op", fn))
        self._mark((eng, self.cnt[eng]), reads, writes)

    def dma(self, q, fn, reads=(), writes=()):
        pl = self.pools[q]
        i = pl[self.rrq[q] % len(pl)]
        self.rrq[q] += 1
        if self.dma_use[i] > 0:
            self._wait(q, (("d", i), 16 * self.dma_use[i]))
        self._deps(q, reads, writes)
        self.dma_use[i] += 1
        self.ops[q].append(("dma", fn, i))
        self._mark((("d", i), 16 * self.dma_use[i]), reads, writes)

    def cc(self, fn, reads=(), writes=()):
        if _DBG_NOCC:
            return
        self._deps("pool", reads, writes)
        self.cnt.setdefault("cc", 0)
        self.cnt["cc"] += 1
        self.ops["pool"].append(("cc", fn))
        self._mark(("cc", self.cnt["cc"]), reads, writes)

    def final_wait(self, eng, bufs):
        for b in bufs:
            if b.w is not None:
                self._wait(eng, b.w)

    def emit(self, nc, es):
        sems = {e: es.enter_context(nc.semaphore("s_" + e)) for e in ["pe", "act", "dve", "pool", "cc"]}
        dsem = [es.enter_context(nc.semaphore("d%d" % i)) for i in range(self.ndma)]

        def sem_of(key):
            return dsem[key[1]] if isinstance(key, tuple) else sems[key]

        block = es.enter_context(nc.Block())

        def run(name, eng):
            for o in self.ops[name]:
                if o[0] == "wait":
                    eng.wait_ge(sem_of(o[1]), o[2])
                elif o[0] == "op":
                    o[1](eng).then_inc(sems[name], 1)
                elif o[0] == "dma":
                    o[1](eng).then_inc(dsem[o[2]], 16)
                elif o[0] == "cc":
                    o[1](eng).then_inc(sems["cc"], 1)

        @block.tensor
        def _(e):
            run("pe", e)

        @block.scalar
        def _(e):
            run("act", e)

        @block.vector
        def _(e):
            run("dve", e)

        @block.gpsimd
        def _(e):
            run("pool", e)

        @block.sync
        def _(e):
            run("sync", e)


def build():
    nc = bass.Bass("TRN2", target_bir_lowering=False)
    es = ExitStack()
    P = Prog()

    def din(name, shape, dt=F32):
        return nc.dram_tensor(name, list(shape), dt, kind="ExternalInput").ap()

    xT_own = din("xT_own", [128, 8, NTOK])
    xT_full = din("xT_full", [128, 8, SEQ])
    c_col = din("c_col", [128, 8])
    w_ada = din("w_ada", [2, 1024, 6144])
    b_ada_col = din("b_ada_col", [128, 2, 48])
    nmix_col = din("nmix_col", [128, 2, 8])
    nmlp_col = din("nmlp_col", [128, 2, 8])
    nfin_col = din("nfin_col", [128, 8])
    gn_col = din("gn_col", [128, 8])
    a_re_st = din("a_re_st", [128, 16])
    a_im_st = din("a_im_st", [128, 16])
    ldt_st = din("ldt_st", [128, 16])
    bexp_re = din("bexp_re", [128, 4, 128])
    bexp_im = din("bexp_im", [128, 4, 128])
    cexp_re = din("cexp_re", [128, 16, 32])
    cexp_im = din("cexp_im", [128, 16, 32])
    d_col = din("d_col", [128, 4])
    w_glu = din("w_glu", [1024, 2048])
    w_in = din("w_in", [1024, 3088])
    w_g2 = din("w_g2", [16, 512])
    b_gate = din("b_gate", [1, 512])
    w_out = din("w_out", [1024, 1024])
    w_ff1 = din("w_ff1", [2, 1024, 4096])
    w_ff2 = din("w_ff2", [2, 4096, 1024])
    sel = din("sel", [128, 2])
    umat = din("umat", [128, 128])
    ind = din("ind", [128, 2])
    eye32 = din("eye32", [128, 32])
    outT = nc.dram_tensor("outT", [128, 8, NTOK], F32, kind="ExternalOutput").ap()
    z_ibs = [nc.dram_tensor("z_ib%d" % a, [256, SEQ], BF) for a in range(2)]
    z_obs = [nc.dram_tensor("z_ob%d" % a, [512, SEQ], BF) for a in range(2)]
    s_ib = nc.dram_tensor("s_ib", [128, 1024], F32)
    s_ob = nc.dram_tensor("s_ob", [256, 1024], F32)

    def sb(name, shape, dt=F32):
        return es.enter_context(nc.sbuf_tensor(name, list(shape), dt))

    def ps(name, shape, dt=F32):
        return es.enter_context(nc.psum_tensor(name, list(shape), dt))

    A_x = sb("A_x", [128, 8, NTOK])
    A_h = sb("A_h", [128, 8, NTOK], BF)
    A_g = sb("A_g", [128, 16384], BF)
    A_w = [sb("A_w0", [128, 8192], BF), sb("A_w1", [128, 8192], BF)]
    A_s = sb("A_s", [128, 4096])
    A_t = sb("A_t", [128, 4096], BF)
    psA = ps("psA", [128, 4, 512])
    psB = ps("psB", [128, 2, 512])
    psC = ps("psC", [128, 1024])
    bA = [Buf("psA%d" % i) for i in range(4)]
    bB = [Buf("psB%d" % i) for i in range(2)]
    bC = Buf("psC")
    b_x = [[Buf() for _ in range(4)] for _ in range(8)]
    b_h = [Buf() for _ in range(4)]
    b_w = [Buf("w0"), Buf("w1")]
    b_g = Buf("g")
    b_s = Buf("s")
    b_t = Buf("t")

    cst = sb("cst", [128, 16])
    b_c = Buf("cst")
    ones_bf = sb("ones_bf", [128, 128], BF)

    def memset(t, v):
        return lambda e: e.memset(t, v)

    CV = {"one": 1.0, "negpi": -math.pi, "eps": EPS, "zero": 0.0}
    cidx = {k: i for i, k in enumerate(CV)}
    for k, i in cidx.items():
        P.op("dve", memset(cst[:, i:i + 1], CV[k]), writes=[b_c])
    P.op("dve", memset(ones_bf[:], 1.0), writes=[b_c])

    def C(k):
        return cst[:, cidx[k]:cidx[k] + 1]

    small = {}
    b_sm = Buf("small")

    def load_small(name, ap, shape, dt=F32):
        t = sb("sm_" + name, shape, dt)
        P.dma("sync", lambda e, t=t, ap=ap: e.dma_start(out=t[:], in_=ap), writes=[b_sm])
        small[name] = t
        return t

    ccol = sb("sm_ccol", [128, 8])
    b_cc = Buf("ccol")
    P.dma("sync", lambda e: e.dma_start(out=ccol[:], in_=c_col), writes=[b_cc])
    bada = load_small("bada", b_ada_col, [128, 2, 48])
    nmix = load_small("nmix", nmix_col, [128, 2, 8])
    nmlp = load_small("nmlp", nmlp_col, [128, 2, 8])
    nfin = load_small("nfin", nfin_col, [128, 8])
    gnc = load_small("gnc", gn_col, [128, 8])
    arst = load_small("arst", a_re_st, [128, 16])
    aist = load_small("aist", a_im_st, [128, 16])
    ldst = load_small("ldst", ldt_st, [128, 16])
    dcol = load_small("dcol", d_col, [128, 4])
    selt = load_small("selt", sel, [128, 2])
    umt = load_small("umt", umat, [128, 128])
    indt = load_small("indt", ind, [128, 2])
    eyet = load_small("eyet", eye32, [128, 32])
    BT = sb("BT", [128, 2, 4, 128], BF)
    P.dma("pool", lambda e: e.dma_start(out=BT[:, 0], in_=bexp_re), writes=[b_sm])
    P.dma("pool", lambda e: e.dma_start(out=BT[:, 1], in_=bexp_im), writes=[b_sm])
    wg2 = sb("wg2", [16, 512], BF)
    bgb = sb("bgb", [1, 512], BF)
    P.dma("pool", lambda e: e.dma_start(out=wg2[:], in_=w_g2), writes=[b_sm])
    P.dma("pool", lambda e: e.dma_start(out=bgb[:], in_=b_gate), writes=[b_sm])

    def tt(eng, out, a, b, op, reads, writes):
        P.op(eng, lambda e: e.tensor_tensor(out=out, in0=a, in1=b, op=op), reads=reads, writes=writes)

    def ts(eng, out, a, s1, s2, op0, op1, reads, writes):
        if s2 is None:
            P.op(eng, lambda e: e.tensor_scalar(out=out, in0=a, scalar1=s1, scalar2=None, op0=op0),
                 reads=reads, writes=writes)
            return
        P.op(eng, lambda e: e.tensor_scalar(out=out, in0=a, scalar1=s1, scalar2=s2, op0=op0, op1=op1),
             reads=reads, writes=writes)

    def stt(out, a, s, b, op0, op1, reads, writes):
        P.op("dve", lambda e: e.scalar_tensor_tensor(out=out, in0=a, scalar=s, in1=b, op0=op0, op1=op1),
             reads=reads, writes=writes)

    def act(out, a, func, reads, writes, bias=None, scale=1.0):
        kw = {} if bias is None else {"bias": bias}
        P.op("act", lambda e: e.activation(out=out, in_=a, func=func, scale=scale, **kw),
             reads=reads, writes=writes)

    def mm(out, pairs, reads, writes, tp=None):
        def fn(e):
            n = len(pairs)
            ins = None
            for j, (l, r) in enumerate(pairs):
                kw = {} if tp is None else {"tile_position": tp}
                ins = e.matmul(out, l, r, start=(j == 0), stop=(j == n - 1), **kw)
            return ins
        P.op("pe", fn, reads=reads, writes=writes)

    def dump(stage, src, bufs, bf=False):
        if _DBG_STAGE != stage:
            return
        bo = Buf("dbg")
        if bf:
            v = A_x[:].rearrange("p k t -> p (k t)")
            P.op("act", lambda e: e.activation(out=v, in_=src, func=AF.Identity), reads=bufs, writes=[bo])
            P.dma("sync", lambda e: e.dma_start(out=outT, in_=A_x[:]), reads=[bo], writes=[bo])
        else:
            P.dma("sync", lambda e: e.dma_start(out=src[0], in_=src[1]), reads=bufs, writes=[bo])
        P.final_wait("sync", [bo])
        raise _Stop()

    def body():
        cs = sb("cs", [128, 8])
        b_cs = Buf("cs")
        act(cs[:], ccol[:], AF.Silu, [b_cc, b_c], [b_cs])
        modc = sb("modc", [128, 2, 48])
        b_mod = Buf("mod")
        pcount = [0]

        bCs = []
        ada_n = [0]
        b_ada2 = Buf("ada2")

        def ada_issue(l, j):
            wi = 0
            half = ada_n[0] % 2 if ada_n[0] < 8 else 0
            ada_n[0] += 1
            bw_ = b_w[0] if half == 0 else b_ada2
            wv = A_w[wi][:].bitcast(F32)[:, 2048 * half:2048 * (half + 1)].rearrange("p (k c) -> p k c", k=8)
            src = w_ada[l, :, j * 256:(j + 1) * 256].rearrange("(k p) c -> p k c", p=128)
            P.dma("sync", lambda e, wv=wv, src=src: e.dma_start(out=wv, in_=src), writes=[bw_])
            for t in range(2):
                col = l * 48 + 2 * j + t
                mm(psC[:, col:col + 1],
                   [(wv[:, k, t * 128:(t + 1) * 128], cs[:, k:k + 1]) for k in range(8)],
                   [bw_, b_cs], [bC])

        def ada_evac(c0, c1):
            tt("dve", modc[:].rearrange("p l c -> p (l c)")[:, c0:c1], psC[:, c0:c1],
               bada[:].rearrange("p l c -> p (l c)")[:, c0:c1], ALU.add, [bC, b_sm], [b_mod])

        def ada_evac_all():
            pass

        def ada_piece(l, j):
            ada_issue(l, j)

        ada_rest = [(0, j) for j in range(8, 24)] + [(1, j) for j in range(24)]
        for j in range(8):
            ada_piece(0, j)
        ada_evac(0, 16)
        if _DBG_STAGE == 1:
            for lj in ada_rest:
                ada_piece(*lj)
            ada_rest = []
            ada_evac(16, 96)
        dump(1, (outT[:, 0, 0:96], modc[:].rearrange("p l c -> p (l c)")), [b_mod])
        modA = sb("modA", [128, 2, 2, 8])

        def mk_modA(l, s_):
            nrm = nmix if s_ == 0 else nmlp
            stt(modA[:, l, s_, :], modc[:, l, 24 * s_ + 8:24 * s_ + 16], 1.0, nrm[:, l, :], ALU.add, ALU.mult,
                [b_mod, b_sm], [b_mod])

        mk_modA(0, 0)

        def shift(l, s_):
            return modc[:, l, 24 * s_:24 * s_ + 8]

        def gate(l, s_):
            return modc[:, l, 24 * s_ + 16:24 * s_ + 24]

        def rstd_block(xin, nt, div, xreads, tbw=512, br=None):
            br = b_s if br is None else br
            sq = A_t[:, 0:nt * tbw].rearrange("p (k c) -> p k c", k=nt)
            act(sq, xin, AF.Square, xreads, [b_t])
            pb = psB[:, 0, 0:tbw]
            mm(pb, [(ones_bf[:], sq[:, k, :]) for k in range(nt)], [b_t, b_c], [bB[0]])
            r = A_s[:, 3584:3584 + tbw]
            ts("dve", r, pb, 1.0 / div, EPS, ALU.mult, ALU.add, [bB[0]], [br])
            act(r, r, AF.Sqrt, [br], [br])
            P.op("dve", lambda e: e.reciprocal(out=r, in_=r), reads=[br], writes=[br])
            return r

        myA = sb("myA", [128, 4])
        myB = sb("myB", [128, 4])
        for dst, srcv in ((myA, modA[:, 0, 0, :]), (myB, shift(0, 0))):
            ts("dve", dst[:], srcv[:, 0:4], selt[:, 0:1], None, ALU.mult, ALU.bypass, [b_mod, b_sm], [b_mod])
            stt(dst[:], srcv[:, 4:8], selt[:, 1:2], dst[:], ALU.mult, ALU.add, [b_mod, b_sm], [b_mod])

        uT = A_h[:].rearrange("p k t -> p (k t)").rearrange("p (k t) -> p k t", k=4)
        b_u = Buf("uT")
        TB0 = 256
        xfs = [A_s[:, i_ * 2048:(i_ + 1) * 2048].rearrange("p (k c) -> p k c", k=8) for i_ in range(2)]
        b_xf = [Buf(), Buf()]
        sq0 = A_t[:, 0:2048].rearrange("p (k c) -> p k c", k=8)
        atf = A_t[:].bitcast(F32)
        r0 = [atf[:, 1024:1280], atf[:, 1280:1536]]
        tmp0 = [atf[:, 1536:1792], atf[:, 1792:2048]]
        b_sq, b_r0, b_tmp0 = Buf(), [Buf(), Buf()], [Buf(), Buf()]
        for tb in range(SEQ // TB0):
            xi_ = tb % 2
            xf = xfs[xi_]
            P.dma("pool", lambda e, tb=tb, xf=xf: e.dma_start(out=xf, in_=xT_full[:, :, tb * TB0:(tb + 1) * TB0]),
                  writes=[b_xf[xi_]])
            act(sq0, xf, AF.Square, [b_xf[xi_]], [b_sq])
            pb = psB[:, tb % 2, 0:TB0]
            mm(pb, [(ones_bf[:], sq0[:, k, :]) for k in range(8)], [b_sq, b_c], [bB[tb % 2]])
            r = r0[xi_]
            ts("dve", r, pb, 1.0 / 1024.0, EPS, ALU.mult, ALU.add, [bB[tb % 2]], [b_r0[xi_]])
            act(r, r, AF.Sqrt, [b_r0[xi_]], [b_r0[xi_]])
            P.op("dve", lambda e, r=r: e.reciprocal(out=r, in_=r), reads=[b_r0[xi_]], writes=[b_r0[xi_]])
            for k in range(4):
                tmp = tmp0[k % 2]
                tt("dve", tmp, xf[:, k, :], r, ALU.mult, [b_xf[xi_], b_r0[xi_]], [b_tmp0[k % 2]])
                act(uT[:, k, tb * TB0:(tb + 1) * TB0], tmp, AF.Identity, [b_tmp0[k % 2], b_mod], [b_u],
                    bias=myB[:, k:k + 1], scale=myA[:, k:k + 1])
        P.op("dve", memset(cst[:, 11:12], 0.0), reads=[], writes=[b_s, b_t] + b_xf + [b_sq] + b_r0 + b_tmp0)
        dump(2, uT.rearrange("p k t -> p (k t)"), [b_u], bf=True)
        lam = sb("lam", [128, 16, 16])
        b_l = Buf("lam")
        L_ = lambda i: lam[:, i, :]
        DT, MAG, PH, SN, CSN, LR, LI, DEN, NR, T1, T2, FR, FI = range(13)
        act(L_(DT), ldst[:], AF.Exp, [b_sm, b_c], [b_l])
        tt("dve", L_(MAG), arst[:], L_(DT), ALU.mult, [b_l, b_sm], [b_l])
        act(L_(MAG), L_(MAG), AF.Exp, [b_l], [b_l])
        tt("dve", L_(PH), aist[:], L_(DT), ALU.mult, [b_l, b_sm], [b_l])
        for dst, off in ((SN, 0.0), (CSN, 0.25)):
            ts("dve", L_(T1), L_(PH), 1.0 / (2 * math.pi), off, ALU.mult, ALU.add, [b_l], [b_l])
            for _ in range(5):
                ts("dve", L_(T2), L_(T1), 0.5, None, ALU.is_gt, None, [b_l], [b_l])
                tt("dve", L_(T1), L_(T1), L_(T2), ALU.subtract, [b_l], [b_l])
            act(L_(dst), L_(T1), AF.Sin, [b_l, b_c], [b_l], scale=2 * math.pi)
        tt("dve", L_(LR), L_(MAG), L_(CSN), ALU.mult, [b_l], [b_l])
        tt("dve", L_(LI), L_(MAG), L_(SN), ALU.mult, [b_l], [b_l])
        tt("dve", L_(DEN), arst[:], arst[:], ALU.mult, [b_sm], [b_l])
        tt("dve", L_(T1), aist[:], aist[:], ALU.mult, [b_sm], [b_l])
        tt("dve", L_(DEN), L_(DEN), L_(T1), ALU.add, [b_l], [b_l])
        P.op("dve", lambda e: e.reciprocal(out=L_(DEN), in_=L_(DEN)), reads=[b_l], writes=[b_l])
        ts("dve", L_(NR), L_(LR), -1.0, None, ALU.add, ALU.bypass, [b_l], [b_l])
        tt("dve", L_(T1), L_(NR), arst[:], ALU.mult, [b_l, b_sm], [b_l])
        tt("dve", L_(T2), L_(LI), aist[:], ALU.mult, [b_l, b_sm], [b_l])
        tt("dve", L_(T1), L_(T1), L_(T2), ALU.add, [b_l], [b_l])
        tt("dve", L_(FR), L_(T1), L_(DEN), ALU.mult, [b_l], [b_l])
        tt("dve", L_(T1), L_(LI), arst[:], ALU.mult, [b_l, b_sm], [b_l])
        tt("dve", L_(T2), L_(NR), aist[:], ALU.mult, [b_l, b_sm], [b_l])
        tt("dve", L_(T1), L_(T1), L_(T2), ALU.subtract, [b_l], [b_l])
        tt("dve", L_(FI), L_(T1), L_(DEN), ALU.mult, [b_l], [b_l])
        CT = sb("CT", [128, 2, 16, 32], BF)
        CTf = A_s[:, 1024:2048].rearrange("p (r q c) -> p r q c", r=2, q=16)
        cre = A_s[:, 0:512].rearrange("p (q c) -> p q c", q=16)
        cim = A_s[:, 512:1024].rearrange("p (q c) -> p q c", q=16)
        P.dma("sync", lambda e: e.dma_start(out=cre, in_=cexp_re), writes=[b_s])
        P.dma("sync", lambda e: e.dma_start(out=cim, in_=cexp_im), writes=[b_s])
        ctmp = sb("ctmp", [128, 2, 32])
        for q in range(16):
            fr, fi = lam[:, FR, q:q + 1], lam[:, FI, q:q + 1]
            ts("dve", ctmp[:, 0, :], cim[:, q, :], fi, None, ALU.mult, ALU.bypass, [b_l, b_s], [b_l])
            stt(CTf[:, 0, q, :], cre[:, q, :], fr, ctmp[:, 0, :], ALU.mult, ALU.subtract, [b_l, b_s], [b_s])
            P.op("dve", lambda e, q=q: e.tensor_copy(out=CT[:, 0, q, :], in_=CTf[:, 0, q, :]), reads=[b_s], writes=[b_l])
            ts("dve", ctmp[:, 1, :], cim[:, q, :], fr, -1.0, ALU.mult, ALU.mult, [b_l, b_s], [b_l])
            ts("dve", ctmp[:, 0, :], cre[:, q, :], fi, None, ALU.mult, ALU.bypass, [b_l, b_s], [b_l])
            tt("dve", CT[:, 1, q, :], ctmp[:, 1, :], ctmp[:, 0, :], ALU.subtract, [b_l], [b_l])
            tt("dve", CTf[:, 1, q, :], ctmp[:, 0, :], ctmp[:, 1, :], ALU.subtract, [b_l], [b_s])
        LP = sb("LP", [128, 12, 3, 16])
        P.op("dve", lambda e: e.tensor_copy(out=LP[:, 0, 0, :], in_=L_(LR)), reads=[b_l], writes=[b_l])
        P.op("dve", lambda e: e.tensor_copy(out=LP[:, 0, 1, :], in_=L_(LI)), reads=[b_l], writes=[b_l])
        for k in range(12):
            ts("dve", LP[:, k, 2, :], LP[:, k, 1, :], -1.0, None, ALU.mult, ALU.bypass, [b_l], [b_l])
            if k < 11:
                tt("dve", L_(T1), LP[:, k, 0, :], LP[:, k, 0, :], ALU.mult, [b_l], [b_l])
                tt("dve", L_(T2), LP[:, k, 1, :], LP[:, k, 1, :], ALU.mult, [b_l], [b_l])
                tt("dve", LP[:, k + 1, 0, :], L_(T1), L_(T2), ALU.subtract, [b_l], [b_l])
                tt("dve", L_(T1), LP[:, k, 0, :], LP[:, k, 1, :], ALU.mult, [b_l], [b_l])
                ts("dve", LP[:, k + 1, 1, :], L_(T1), 2.0, None, ALU.mult, ALU.bypass, [b_l], [b_l])

        L = 8
        NCH = SEQ // L
        LPS = sb("LPS", [128, L, 3, 16])
        for c3 in range(3):
            P.op("dve", lambda e, c3=c3: e.tensor_copy(out=LPS[:, 0, c3, :], in_=LP[:, 0, c3, :]),
                 reads=[b_l], writes=[b_l])
        for s1 in range(1, L):
            pr, pi_ = LPS[:, s1 - 1, 0, :], LPS[:, s1 - 1, 1, :]
            tt("dve", L_(T1), pr, LP[:, 0, 0, :], ALU.mult, [b_l], [b_l])
            tt("dve", L_(T2), pi_, LP[:, 0, 1, :], ALU.mult, [b_l], [b_l])
            tt("dve", LPS[:, s1, 0, :], L_(T1), L_(T2), ALU.subtract, [b_l], [b_l])
            tt("dve", L_(T1), pr, LP[:, 0, 1, :], ALU.mult, [b_l], [b_l])
            tt("dve", L_(T2), pi_, LP[:, 0, 0, :], ALU.mult, [b_l], [b_l])
            tt("dve", LPS[:, s1, 1, :], L_(T1), L_(T2), ALU.add, [b_l], [b_l])
            ts("dve", LPS[:, s1, 2, :], LPS[:, s1, 1, :], -1.0, None, ALU.mult, ALU.bypass, [b_l], [b_l])
        CH = A_w[0][:].bitcast(F32)[:, 2048:4096].rearrange("p (a r n) -> p a r n", a=2, r=2)
        bCH = [[Buf(), Buf()], [Buf(), Buf()]]
        P.op("dve", memset(cst[:, 13:14], 0.0), reads=[], writes=[b_ada2] + sum(bCH, []))
        X = A_x[:].rearrange("p k t -> p (k t)").rearrange("p (a r t) -> p a r t", a=2, r=2)
        bX = [[Buf(), Buf()], [Buf(), Buf()]]
        zT = A_g[:].rearrange("p (k t) -> p k t", k=4)
        b_z = Buf("zT")
        xb = A_s[:].bitcast(BF).rearrange("p (r t) -> p r t", r=2)
        ysc = A_t[:].bitcast(F32)[:, 0:1536].rearrange("p (a c) -> p a c", a=3)
        b_y = b_t
        Sb = A_t[:, 3072:4096].rearrange("p (r n) -> p r n", r=2)
        b_sb = Buf("Sb")
        P.op("dve", memset(Sb[:, :, 0:1], 0.0), reads=[], writes=[b_sb, b_t])
        CQ = A_w[1][:].rearrange("p (q r s c) -> p q r s c", q=16, r=2, s=L)
        b_cq = Buf("CQ")
        for s1 in range(L):
            lr_b = LPS[:, s1, 0, :].unsqueeze(2).broadcast_to([128, 16, 32])
            li_b = LPS[:, s1, 1, :].unsqueeze(2).broadcast_to([128, 16, 32])
            cfr, cfi = CTf[:, 0], CTf[:, 1]
            t1 = A_s[:, 2048:2560].rearrange("p (q c) -> p q c", q=16)
            t2 = A_s[:, 2560:3072].rearrange("p (q c) -> p q c", q=16)
            tt("dve", t1, cfr, lr_b, ALU.mult, [b_s, b_l], [b_s])
            tt("dve", t2, cfi, li_b, ALU.mult, [b_s, b_l], [b_s])
            tt("dve", CQ[:, :, 0, s1, :], t1, t2, ALU.subtract, [b_s], [b_cq, b_w[1]])
            tt("dve", t1, cfr, li_b, ALU.mult, [b_s, b_l], [b_s])
            tt("dve", t2, cfi, lr_b, ALU.mult, [b_s, b_l], [b_s])
            tt("dve", t1, t1, t2, ALU.add, [b_s], [b_s])
            ts("dve", CQ[:, :, 1, s1, :], t1, -1.0, None, ALU.mult, ALU.bypass, [b_s], [b_cq])
        pcnt = 0

        def cview(a_, ri):
            return X[:, a_, ri, :].rearrange("p (c j) -> p c j", j=L)

        def hs_level(src, dst, bs, bd, a, b, nb, d, n, sl):
            sr, si, dr, di = src[0], src[1], dst[0], dst[1]
            stt(sl(dr, d, n), sl(sr, 0, n - d), a, sl(sr, d, n), ALU.mult, ALU.add, [bs[0], b_l], [bd[0]])
            stt(sl(di, d, n), sl(si, 0, n - d), a, sl(si, d, n), ALU.mult, ALU.add, [bs[1], b_l], [bd[1]])
            h0 = d // 2 if d > 1 else 0
            bh = [Buf(), Buf()]
            P.op("dve", lambda e: e.tensor_copy(out=sl(dr, h0, d), in_=sl(sr, h0, d)), reads=[bs[0]], writes=[bh[0]])
            P.op("dve", lambda e: e.tensor_copy(out=sl(di, h0, d), in_=sl(si, h0, d)), reads=[bs[1]], writes=[bh[1]])
            stt(sl(dr, d, n), sl(si, 0, n - d), nb, sl(dr, d, n), ALU.mult, ALU.add, [bs[1], b_l, bh[0]], [bd[0]])
            stt(sl(di, d, n), sl(sr, 0, n - d), b, sl(di, d, n), ALU.mult, ALU.add, [bs[0], b_l, bh[1]], [bd[1]])

        def madd(vr, vi, dsel, ssel, k, q, bx):
            a, b, nb = LP[:, k, 0, q:q + 1], LP[:, k, 1, q:q + 1], LP[:, k, 2, q:q + 1]
            dr, di, sr, si = dsel(vr), dsel(vi), ssel(vr), ssel(vi)
            stt(dr, sr, a, dr, ALU.mult, ALU.add, [bx[0], b_l], [bx[0]])
            stt(di, si, a, di, ALU.mult, ALU.add, [bx[1], b_l], [bx[1]])
            stt(dr, si, nb, dr, ALU.mult, ALU.add, [bx[0], bx[1], b_l], [bx[0]])
            stt(di, sr, b, di, ALU.mult, ALU.add, [bx[0], bx[1], b_l], [bx[1]])

        pcb = [0, 0]
        b_g2 = [Buf(), Buf()]
        P.op("dve", memset(cst[:, 9:10], 0.0), reads=[], writes=[b_t] + b_g2)
        Ddiag = sb("Ddiag", [128, 4, 32], BF)
        for i_ in range(4):
            ts("dve", Ddiag[:, i_, :], eyet[:], dcol[:, i_:i_ + 1], None, ALU.mult, ALU.bypass, [b_sm], [b_l])

        def stA(q):
            i, q4, xp = q // 4, q % 4, q % 2
            rows = slice(32 * q4, 32 * q4 + 32)
            for ri in range(2):
                for tb in range(8):
                    pb = pcb[0] % 2
                    pcb[0] += 1
                    mm(psB[:, pb, :], [(BT[rows, ri, i, :], uT[rows, i, tb * 512:(tb + 1) * 512])],
                       [b_sm, b_u], [bB[pb]], tp=(32 * q4, 0))
                    act(X[:, xp, ri, tb * 512:(tb + 1) * 512], psB[:, pb, :], AF.Identity, [bB[pb], b_c],
                        [bX[xp][ri]])

        def stB1(q):
            xp = q % 2
            vr, vi = cview(xp, 0), cview(xp, 1)
            madd(vr, vi, lambda v: v[:, :, 1::2], lambda v: v[:, :, 0::2], 0, q, bX[xp])
            madd(vr, vi, lambda v: v[:, :, 3::4], lambda v: v[:, :, 1::4], 1, q, bX[xp])
            madd(vr, vi, lambda v: v[:, :, 7], lambda v: v[:, :, 3], 2, q, bX[xp])
            madd(vr, vi, lambda v: v[:, :, 5], lambda v: v[:, :, 3], 1, q, bX[xp])
            madd(vr, vi, lambda v: v[:, :, 2::2], lambda v: v[:, :, 1:6:2], 0, q, bX[xp])

        def stB2(q):
            xp = q % 2
            for ri in range(2):
                act(CH[:, 0, ri, :], cview(xp, ri)[:, :, L - 1], AF.Identity, [bX[xp][ri], b_c], [bCH[0][ri]])
            for k in range(9):
                s_, d_ = k % 2, 1 - (k % 2)
                hs_level([CH[:, s_, 0, :], CH[:, s_, 1, :]], [CH[:, d_, 0, :], CH[:, d_, 1, :]], bCH[s_], bCH[d_],
                         LP[:, k + 3, 0, q:q + 1], LP[:, k + 3, 1, q:q + 1], LP[:, k + 3, 2, q:q + 1], 1 << k, NCH,
                         lambda v, lo, hi: v[:, lo:hi])

        def stCconv(q):
            i, q4, xp = q // 4, q % 4, q % 2
            rows = slice(32 * q4, 32 * q4 + 32)
            for ri in range(2):
                act(Sb[:, ri, 1:NCH], CH[:, 1, ri, 0:NCH - 1], AF.Identity, [bCH[1][ri], b_c], [b_sb])
                act(xb[:, ri, :], X[:, xp, ri, :], AF.Identity, [bX[xp][ri], b_c], [b_s])
            for tb in range(4):
                ymm_emit(q, tb)

        def ymm_emit(q, tb):
            i, q4 = q // 4, q % 4
            rows = slice(32 * q4, 32 * q4 + 32)
            tsl = slice(tb * 512, (tb + 1) * 512)
            pb = (pcb[1] + tb) % 4
            pv = psA[rows, pb, :].rearrange("p (c j) -> p c j", j=L)

            def ymm(e):
                tp = (0, 32 * q4)
                e.matmul(psA[rows, pb, :], Ddiag[rows, i, :], uT[rows, i, tsl], start=True, stop=False,
                         tile_position=(32 * q4, 32 * q4))
                e.matmul(psA[rows, pb, :], CT[:, 0, q, :], xb[:, 0, tsl], start=False, stop=False, tile_position=tp)
                ins = e.matmul(psA[rows, pb, :], CT[:, 1, q, :], xb[:, 1, tsl], start=False, stop=False,
                               tile_position=tp)
                for s1 in range(L):
                    for ri in range(2):
                        ins = e.matmul(pv[:, :, s1], CQ[:, q, ri, s1, :], Sb[:, ri, tb * 64:(tb + 1) * 64],
                                       start=False, stop=(ri == 1), tile_position=tp,
                                       skip_group_check=True)
                return ins
            P.op("pe", ymm, reads=[b_l, b_s, b_cq, b_sb, b_u], writes=[bA[pb]])

        def stCevac(q):
            i, q4 = q // 4, q % 4
            rows = slice(32 * q4, 32 * q4 + 32)
            for tb in range(8):
                tsl = slice(tb * 512, (tb + 1) * 512)
                pb = (pcb[1] + tb) % 4
                act(zT[rows, i, tsl], psA[rows, pb, :], AF.Identity, [bA[pb], b_c], [b_z], bias=C("zero")[rows])
                if tb + 4 < 8:
                    ymm_emit(q, tb + 4)
            pcb[1] += 8
            if q4 == 3:
                def gelu_prep(tb):
                    tsl = slice(tb * 512, (tb + 1) * 512)
                    yb, t1, bt1 = zT[:, i, tsl], ysc[:, tb % 2, :], b_g2[tb % 2]
                    act(t1, yb, AF.Square, [b_z, b_c], [bt1], scale=math.sqrt(0.044715))
                    stt(t1, t1, 1.0, yb, ALU.add, ALU.mult, [bt1, b_z], [bt1])
                    act(t1, t1, AF.Sigmoid, [bt1, b_c], [bt1], scale=2.0 * math.sqrt(2.0 / math.pi))

                gelu_prep(0)
                for tb in range(8):
                    if tb + 1 < 8:
                        gelu_prep(tb + 1)
                    tsl = slice(tb * 512, (tb + 1) * 512)
                    yb = zT[:, i, tsl]
                    tt("dve", yb, yb, ysc[:, tb % 2, :], ALU.mult, [b_g2[tb % 2], b_z], [b_z])

        stA(0)
        stB1(0)
        stB2(0)
        for q in range(16):
            if q + 1 < 16:
                stA(q + 1)
            stCconv(q)
            for _ in range(2):
                if ada_rest:
                    ada_issue(*ada_rest.pop(0))
            if q + 1 < 16:
                stB1(q + 1)
            stCevac(q)
            if q + 1 < 16:
                stB2(q + 1)
            if q >= 12:
                for _ in range(2):
                    if ada_rest:
                        ada_issue(*ada_rest.pop(0))
        P.op("dve", memset(cst[:, 10:11], 0.0), reads=[], writes=[b_t, b_sb, b_cq, b_w[1], b_w[0], bC] + bCs + b_g2 + sum(bCH, []))

        dump(3, zT.rearrange("p k t -> p (k t)"), [b_z], bf=True)
        b_zo = Buf("z_ob")
        for a in range(2):
            P.dma("sync", lambda e, a=a: e.dma_start(out=z_ibs[a].ap().rearrange("(j p) t -> p j t", p=128),
                                                      in_=zT[:, 2 * a:2 * a + 2, :]), reads=[b_z], writes=[b_zo])
        for a in range(2):
            P.cc(lambda e, a=a: e.collective_compute("AllGather", ALU.bypass, replica_groups=PAIRS,
                                                     ins=[z_ibs[a][:, :]], outs=[z_obs[a][:, :]]),
                 reads=[b_zo], writes=[b_zo])
        while ada_rest:
            ada_piece(*ada_rest.pop(0))
        ada_evac(16, 96)
        mk_modA(0, 1)
        mk_modA(1, 0)
        mk_modA(1, 1)
        for k in range(8):
            P.dma("sync", lambda e, k=k: e.dma_start(out=A_x[:, k, :], in_=xT_own[:, k, :]),
                  reads=[], writes=[bX[0][0], bX[0][1], bX[1][0], bX[1][1]] + b_x[k])
        zc = A_t[:].rearrange("p (k c) -> p k c", k=8)
        zc2 = A_s[:].bitcast(BF)[:, 0:4096].rearrange("p (k c) -> p k c", k=8)
        for tb in range(4):
            for dstz, off, bz in ((zc, 0, b_t), (zc2, NTOK, b_s)):
                for a in range(2):
                    for r in range(2):
                        srcz = z_obs[a][r * 256:(r + 1) * 256, off + tb * 512:off + (tb + 1) * 512].rearrange(
                            "(j p) t -> p j t", p=128)
                        t0 = r * 4 + a * 2
                        P.dma("sync", lambda e, d_=dstz[:, t0:t0 + 2, :], s_=srcz: e.dma_start(out=d_, in_=s_),
                              reads=[b_zo], writes=[bz])
            hv = A_h[:, :, tb * 512:(tb + 1) * 512]
            ts("dve", hv, zc, selt[:, 0:1], None, ALU.mult, ALU.bypass, [b_t, b_sm, b_u], [b_h[tb], b_u])
            stt(hv, zc2, selt[:, 1:2], hv, ALU.mult, ALU.add, [b_s, b_sm], [b_h[tb]])

        wq_cnt = [0]

        def load_w(dst, src, bw):
            P.dma("pool", lambda e: e.dma_start(out=dst, in_=src), writes=[bw])

        for j in range(2):
            wi = wq_cnt[0] % 2
            wq_cnt[0] += 1
            W = A_w[wi][:].rearrange("p (k c) -> p k c", k=8)
            load_w(W[:, :, 0:512], w_glu[:, j * 512:(j + 1) * 512].rearrange("(k p) c -> p k c", p=128), b_w[wi])
            load_w(W[:, :, 512:1024], w_glu[:, 1024 + j * 512:1024 + (j + 1) * 512].rearrange("(k p) c -> p k c", p=128),
                   b_w[wi])
            for tb in range(4):
                tsl = slice(tb * 512, (tb + 1) * 512)
                for t in range(4):
                    n = 4 * j + t
                    pb = pcnt % 4
                    pcnt += 1
                    mm(psA[:, pb, :], [(W[:, k, 512 + t * 128:512 + (t + 1) * 128], A_h[:, k, tsl]) for k in range(8)],
                       [b_w[wi], b_h[tb]], [bA[pb]])
                    sg = ysc[:, 0, :]
                    act(sg, psA[:, pb, :], AF.Sigmoid, [bA[pb], b_c], [b_y])
                    pb2 = pcnt % 4
                    pcnt += 1
                    mm(psA[:, pb2, :], [(W[:, k, t * 128:(t + 1) * 128], A_h[:, k, tsl]) for k in range(8)],
                       [b_w[wi], b_h[tb]], [bA[pb2]])
                    y = ysc[:, 1, :]
                    tt("dve", y, psA[:, pb2, :], sg, ALU.mult, [bA[pb2], b_y], [b_y])
                    stt(A_x[:, n, tsl], y, gate(0, 0)[:, n:n + 1], A_x[:, n, tsl], ALU.mult, ALU.add,
                        [b_y, b_mod, b_x[n][tb]], [b_x[n][tb]])

        b_tm = [Buf(), Buf(), Buf(), Buf()]
        b_rr = Buf("rstd")

        def norm_mod(l, s_):
            P.op("dve", memset(cst[:, 15:16], 0.0), reads=[], writes=[b_s, b_rr] + b_tm)
            for tb in range(4):
                tsl = slice(tb * 512, (tb + 1) * 512)
                xr = [b_x[k][tb] for k in range(8)]
                r = rstd_block(A_x[:, :, tsl], 8, 1024.0, xr, br=b_rr)
                for k in range(8):
                    tmp = A_s[:, (k % 4) * 512:(k % 4 + 1) * 512]
                    tt("dve", tmp, A_x[:, k, tsl], r, ALU.mult, [b_rr, b_x[k][tb]], [b_tm[k % 4]])
                    act(A_h[:, k, tsl], tmp, AF.Identity, [b_tm[k % 4], b_mod], [b_h[tb]],
                        bias=shift(l, s_)[:, k:k + 1], scale=modA[:, l, s_, k:k + 1])
            P.op("dve", memset(cst[:, 15:16], 0.0), reads=[], writes=[b_s, b_rr] + b_tm)

        bAT = [Buf(), Buf()]

        b_ys = [Buf(), Buf(), Buf()]

        def mlp(l):
            nonlocal pcnt
            aT = A_g[:, 0:4096].rearrange("p (a f c) -> p a f c", a=2, f=4)
            first = [True]
            Wv_ = {}
            bCh = [Buf(), Buf()]
            s2_banks = [(psB[:, 0, :], bB[0]), (psB[:, 1, :], bB[1]), (psC[:, 0:512], bCh[0]), (psC[:, 512:1024], bCh[1])]
            P.op("dve", memset(cst[:, 12:13], 0.0), reads=[], writes=[b_t, bC] + b_ys + bCh)

            def s1(n):
                nonlocal pcnt
                e8, tb = n // 4, n % 4
                if tb == 0:
                    wi = wq_cnt[0] % 2
                    wq_cnt[0] += 1
                    W1 = A_w[wi][:, 0:4096].rearrange("p (k c) -> p k c", k=8)
                    W2 = A_w[wi][:, 4096:8192].rearrange("p (k c) -> p k c", k=4)
                    load_w(W1, w_ff1[l, :, e8 * 512:(e8 + 1) * 512].rearrange("(k p) c -> p k c", p=128), b_w[wi])
                    load_w(W2, w_ff2[l, e8 * 512:(e8 + 1) * 512, :].rearrange("(k p) c -> p k c", p=128), b_w[wi])
                    Wv_[e8] = (wi, W1, W2)
                wi, W1, W2 = Wv_[e8]
                tsl = slice(tb * 512, (tb + 1) * 512)
                ap_ = n % 2
                for f in range(4):
                    pb = pcnt % 4
                    pcnt += 1
                    mm(psA[:, pb, :], [(W1[:, k, f * 128:(f + 1) * 128], A_h[:, k, tsl]) for k in range(8)],
                       [b_w[wi], b_h[tb]], [bA[pb]])
                    sl3 = (4 * n + f) % 3
                    rl = ysc[:, sl3, :]
                    act(rl, psA[:, pb, :], AF.Relu, [bA[pb], b_c], [b_ys[sl3]])
                    extra = [b_g, b_z] if first[0] else []
                    first[0] = False
                    act(aT[:, ap_, f, :], rl, AF.Square, [b_ys[sl3], b_c], [bAT[ap_]] + extra)

            def s2(n):
                nonlocal pcnt
                e8, tb = n // 4, n % 4
                wi, W1, W2 = Wv_[e8]
                tsl = slice(tb * 512, (tb + 1) * 512)
                ap_ = n % 2
                for nn in range(8):
                    pb = pcnt % 4
                    pcnt += 1
                    pbank, pbuf = s2_banks[pb]
                    mm(pbank, [(W2[:, f, nn * 128:(nn + 1) * 128], aT[:, ap_, f, :]) for f in range(4)],
                       [b_w[wi], bAT[ap_]], [pbuf])
                    stt(A_x[:, nn, tsl], pbank, gate(l, 1)[:, nn:nn + 1], A_x[:, nn, tsl], ALU.mult, ALU.add,
                        [pbuf, b_mod, b_x[nn][tb]], [b_x[nn][tb]])

            s1(0)
            for n in range(32):
                if n + 1 < 32:
                    s1(n + 1)
                s2(n)
            P.op("dve", memset(cst[:, 12:13], 0.0), reads=[], writes=[b_t, bC] + b_ys + bCh)

        allx = [b_x[k][tb] for k in range(8) for tb in range(4)]
        dump(4, (outT, A_x[:]), allx)
        norm_mod(0, 1)
        mlp(0)
        dump(5, (outT, A_x[:]), allx)

        norm_mod(1, 0)
        Wq = A_w[0][:, 0:4096].rearrange("p (k c) -> p k c", k=8)
        Wk = A_w[0][:, 4096:8192].rearrange("p (k c) -> p k c", k=8)
        Wv = A_w[1][:].rearrange("p (k c) -> p k c", k=8)
        Wr = A_g[:, 0:8192].rearrange("p (k c) -> p k c", k=8)
        Wo = A_g[:, 8192:16384].rearrange("p (k c) -> p k c", k=8)
        Wg = sb("Wg", [128, 8, 16], BF)
        win = lambda a, b: w_in[:, a:b].rearrange("(k p) c -> p k c", p=128)
        load_w(Wq, win(0, 512), b_w[0])
        load_w(Wk, win(512, 1024), b_w[0])
        load_w(Wv, win(1024, 2048), b_w[1])
        load_w(Wg[:], win(2048, 2064), b_g)
        P.dma("pool", lambda e: e.dma_start(out=Wr, in_=win(2064, 3088)), writes=[b_g, bAT[0], bAT[1]])
        load_w(Wo, w_out.rearrange("(k p) c -> p k c", p=128), b_g)

        S = sb("S", [128, 4, 256])
        Sbf = sb("Sbf", [128, 1, 4, 256], BF)
        b_S = Buf("S")
        b_Sbf = [Buf(), Buf()]
        Sbfs = [Sbf[:, 0],
                LP[:].rearrange("p a b c -> p (a b c)").bitcast(BF)[:, 0:1024].rearrange("p (h e) -> p h e", h=4)]
        gl = sb("gl", [16, 128], BF)
        la = A_s[:, 0:512]
        Ef = A_s[:, 512:1024]
        gsc = A_s[:, 1024:2048].rearrange("p (k t) -> p k t", k=8)
        rsd = A_s[:, 2048:2560].rearrange("p (k t) -> p k t", k=4)
        asb = A_s[:].bitcast(BF)
        kdec = [A_t[:, 0:512], asb[:, 5376:5888]]
        vtok = [A_t[:, 512:1536], asb[:, 5888:6912]]
        qTt = [A_t[:, 3584:4096].rearrange("p (k t) -> p k t", k=4),
               asb[:, 6912:7424].rearrange("p (k t) -> p k t", k=4)]
        srs = [BT[:].rearrange("p a b c -> p (a b c)").rearrange("p (k t) -> p k t", k=8),
               CT[:].rearrange("p a b c -> p (a b c)").rearrange("p (k t) -> p k t", k=8)]
        osq = A_t[:, 1536:2560].rearrange("p (k t) -> p k t", k=8)
        gat = A_t[:, 2560:3584].rearrange("p (k t) -> p k t", k=8)
        dec = sb("dec", [128, 2, 4, 2])
        b_gl, b_la, b_gs, b_osq, b_rsd, b_gat = [Buf() for _ in range(6)]
        b_kd, b_v, b_dec, b_q, b_sr = [[Buf(), Buf()] for _ in range(5)]
        P.op("dve", memset(S[:], 0.0), writes=[b_S])

        def gla_front(tt_, final):
            st = tt_ % 2
            tk = slice(tt_ * 128, (tt_ + 1) * 128)
            tb = tt_ // 4
            hT = lambda k: A_h[:, k, tk]
            mm(psB[0:16, 0, 0:128], [(Wg[:, k, :], hT(k)) for k in range(8)], [b_g, b_h[tb]], [bB[0]])
            act(gl[:], psB[0:16, 0, 0:128], AF.Identity, [bB[0], b_c], [b_gl], bias=C("zero")[0:16])
            mm(psA[:, 0, :], [(ones_bf[0:1, :], bgb[:]), (gl[:], wg2[:])], [b_c, b_sm, b_gl], [bA[0]])
            act(Ef, psA[:, 0, :], AF.Exp, [bA[0], b_c], [b_la], scale=-1.0)
            act(la, Ef, AF.Ln, [b_la, b_c], [b_la], bias=C("one"))
            mm(psA[:, 1, :], [(umt[:], la)], [b_sm, b_la], [bA[1]])
            for h in range(4):
                mm(psB[:, 1, 2 * h:2 * h + 2], [(la[:, h * 128:(h + 1) * 128], indt[:])], [b_la, b_sm], [bB[1]])
            mm(psA[:, 2, :], [(hT(k), Wk[:, k, :]) for k in range(8)], [b_w[0], b_h[tb]], [bA[2]])
            act(Ef, psA[:, 1, :], AF.Exp, [bA[1], b_c], [b_la], scale=-1.0 / 16.0)
            act(dec[:, st].rearrange("p h c -> p (h c)"), psB[:, 1, 0:8], AF.Exp, [bB[1], b_c], [b_dec[st]],
                scale=-1.0 / 16.0)
            tt("dve", kdec[st], psA[:, 2, :], Ef, ALU.mult, [bA[2], b_la], [b_kd[st]])
            for hh in range(2):
                mm(psA[:, 3, :], [(hT(k), Wv[:, k, hh * 512:(hh + 1) * 512]) for k in range(8)],
                   [b_w[1], b_h[tb]], [bA[3]])
                act(vtok[st][:, hh * 512:(hh + 1) * 512], psA[:, 3, :], AF.Identity, [bA[3], b_c], [b_v[st]])
            if final:
                for h in range(4):
                    mm(psB[:, 0, h * 128:(h + 1) * 128], [(Wq[:, k, h * 128:(h + 1) * 128], hT(k)) for k in range(8)],
                       [b_w[0], b_h[tb]], [bB[0]])
                act(qTt[st].rearrange("p k t -> p (k t)"), psB[:, 0, :], AF.Identity, [bB[0], b_c], [b_q[st]],
                    scale=128.0 ** -0.5)
                pr = psA[:, 0:2, :].rearrange("p a c -> p (a c)")
                for t8 in range(8):
                    mm(pr[:, t8 * 128:(t8 + 1) * 128], [(Wr[:, k, t8 * 128:(t8 + 1) * 128], hT(k)) for k in range(8)],
                       [b_g, b_h[tb]], [bA[0], bA[1]])
                act(srs[st].rearrange("p k t -> p (k t)"), pr, AF.Silu, [bA[0], bA[1], b_c], [b_sr[st]])

        def gla_back(tt_, final):
            st = tt_ % 2
            tk = slice(tt_ * 128, (tt_ + 1) * 128)
            tb = tt_ // 4
            for cc in range(2):
                rws = slice(64 * cc, 64 * cc + 64)
                for h in range(4):
                    mm(psC[:, h * 256:(h + 1) * 256],
                       [(kdec[st][rws, h * 128:(h + 1) * 128], vtok[st][rws, h * 256:(h + 1) * 256])],
                       [b_kd[st], b_v[st]], [bC], tp=(64 * cc, 0))
                for h in range(4):
                    stt(S[:, h, :], S[:, h, :], dec[:, st, h, cc:cc + 1], psC[:, h * 256:(h + 1) * 256], ALU.mult,
                        ALU.add, [b_S, b_dec[st], bC], [b_S])
                if final:
                    act(Sbfs[cc].rearrange("p h e -> p (h e)"), S[:].rearrange("p h e -> p (h e)"), AF.Identity,
                        [b_S, b_c], [b_Sbf[cc]])
            if final:
                for cc in range(2):
                    for h in range(4):
                        for e2 in range(2):
                            t8 = 2 * h + e2
                            mm(psB[:, 1, t8 * 64:(t8 + 1) * 64],
                               [(Sbfs[cc][:, h, e2 * 128:(e2 + 1) * 128], qTt[st][:, h, 64 * cc:64 * cc + 64])],
                               [b_Sbf[cc], b_q[st]], [bB[1]])
                    act(gsc[:, :, 64 * cc:64 * cc + 64], psB[:, 1, :].rearrange("p (k t) -> p k t", k=8),
                        AF.Identity, [bB[1], b_c], [b_gs])
            if final:
                act(osq, gsc, AF.Square, [b_gs, b_c], [b_osq])
                for h in range(4):
                    mm(psB[:, 0, h * 128:(h + 1) * 128],
                       [(ones_bf[:], osq[:, 2 * h, :]), (ones_bf[:], osq[:, 2 * h + 1, :])], [b_osq, b_c], [bB[0]])
                ts("dve", A_s[:, 2048:2560], psB[:, 0, :], 1.0 / 256.0, EPS, ALU.mult, ALU.add, [bB[0]], [b_rsd])
                act(A_s[:, 2048:2560], A_s[:, 2048:2560], AF.Sqrt, [b_rsd], [b_rsd])
                P.op("dve", lambda e: e.reciprocal(out=A_s[:, 2048:2560], in_=A_s[:, 2048:2560]),
                     reads=[b_rsd], writes=[b_rsd])
                for t8 in range(8):
                    stt(gsc[:, t8, :], gsc[:, t8, :], gnc[:, t8:t8 + 1], rsd[:, t8 // 2, :], ALU.mult, ALU.mult,
                        [b_gs, b_sm, b_rsd], [b_gs])
                tt("dve", gat, gsc, srs[st], ALU.mult, [b_gs, b_sr[st]], [b_gat])
                po = psA[:, 2:4, :].rearrange("p a c -> p (a c)")
                for n in range(8):
                    mm(po[:, n * 128:(n + 1) * 128], [(Wo[:, k, n * 128:(n + 1) * 128], gat[:, k, :]) for k in range(8)],
                       [b_g, b_gat], [bA[2], bA[3]])
                for n in range(8):
                    stt(A_x[:, n, tk], po[:, n * 128:(n + 1) * 128], gate(1, 0)[:, n:n + 1], A_x[:, n, tk],
                        ALU.mult, ALU.add, [bA[2], bA[3], b_mod, b_x[n][tb]], [b_x[n][tb]])

        def gla_pass(final, first_front_done=False):
            if not first_front_done:
                gla_front(0, final)
            for tt_ in range(16):
                if tt_ + 1 < 16:
                    gla_front(tt_ + 1, final)
                gla_back(tt_, final)

        gla_tmp = [b_s, b_t, b_gl, b_la, b_gs, b_osq, b_rsd, b_gat, b_sm, b_l] + sum([b_kd, b_v, b_dec, b_q, b_sr], [])
        bar = sb("bar", [128, 1])

        def barrier(bufs):
            P.op("dve", memset(bar[:], 0.0), writes=bufs)

        barrier(gla_tmp)
        gla_pass(False)
        b_so = Buf("s_ob")
        P.dma("sync", lambda e: e.dma_start(out=s_ib[:, :], in_=S[:].rearrange("p h e -> p (h e)")), reads=[b_S], writes=[b_so])
        P.cc(lambda e: e.collective_compute("AllGather", ALU.bypass, replica_groups=PAIRS,
                                            ins=[s_ib[:, :]], outs=[s_ob[:, :]]), reads=[b_so], writes=[b_so])
        P.dma("sync", lambda e: e.dma_start(out=S[:].rearrange("p h e -> p (h e)"), in_=s_ob[0:128, :]),
              reads=[b_so], writes=[b_S])
        gla_front(0, True)
        ts("dve", S[:].rearrange("p h e -> p (h e)"), S[:].rearrange("p h e -> p (h e)"), selt[:, 1:2], None,
           ALU.mult, ALU.bypass, [b_S, b_sm], [b_S])
        gla_pass(True, first_front_done=True)
        barrier(gla_tmp)
        dump(6, (outT, A_x[:]), allx)

        norm_mod(1, 1)
        mlp(1)
        dump(7, (outT, A_x[:]), allx)

        b_out = Buf("out")
        for tb in range(4):
            tsl = slice(tb * 512, (tb + 1) * 512)
            xr = [b_x[k][tb] for k in range(8)]
            r = rstd_block(A_x[:, :, tsl], 8, 1024.0, xr)
            for k in range(8):
                stt(A_x[:, k, tsl], A_x[:, k, tsl], nfin[:, k:k + 1], r, ALU.mult, ALU.mult,
                    [b_s, b_sm, b_x[k][tb]], [b_x[k][tb]])
            P.dma("sync", lambda e, tsl=tsl: e.dma_start(out=outT[:, :, tsl], in_=A_x[:, :, tsl]),
                  reads=xr, writes=[b_out])
        P.final_wait("sync", [b_out])
    try:
        body()
    except _Stop:
        pass
    P.emit(nc, es)
    es.close()
    return nc


_NC = None
_LAST = None


def _tiles(a2d):
    C, T = a2d.shape
    return np.ascontiguousarray(a2d.reshape(C // 128, 128, T).transpose(1, 0, 2))


def _col(v):
    return np.ascontiguousarray(v.reshape(-1, 128).T)


def kernel(x, c, w_ada, b_ada, norm_mix, norm_mlp, s5_a_re, s5_a_im, s5_log_dt, s5_b_re, s5_b_im,
           s5_c_re, s5_c_im, s5_d, s5_w_glu, gla_w_in, gla_w_gate2, gla_b_gate, gla_g_norm, gla_w_out,
           w_ff1, w_ff2, norm_final):
    global _NC
    f = lambda a: np.ascontiguousarray(np.asarray(a, dtype=np.float32))
    x, c = f(x), f(c)
    if _NC is None:
        _NC = build()
    umat = np.zeros((128, 128), np.float32)
    for s in range(128):
        for s2 in range(s + 1, (s // 64 + 1) * 64):
            umat[s2, s] = 1.0
    ind = np.zeros((128, 2), np.float32)
    ind[:64, 0] = 1.0
    ind[64:, 1] = 1.0
    common = {
        "w_ada": f(w_ada),
        "b_ada_col": np.ascontiguousarray(np.stack([_col(f(b_ada)[l]) for l in range(2)], axis=1)),
        "nmix_col": np.ascontiguousarray(np.stack([_col(f(norm_mix)[l]) for l in range(2)], axis=1)),
        "nmlp_col": np.ascontiguousarray(np.stack([_col(f(norm_mlp)[l]) for l in range(2)], axis=1)),
        "nfin_col": _col(f(norm_final)), "gn_col": _col(f(gla_g_norm)[0]),
        "w_glu": f(s5_w_glu)[0], "w_in": f(gla_w_in)[0], "w_g2": f(gla_w_gate2)[0], "b_gate": f(gla_b_gate),
        "w_out": f(gla_w_out)[0], "w_ff1": f(w_ff1), "w_ff2": f(w_ff2), "umat": umat, "ind": ind,
        "eye32": np.ascontiguousarray(np.tile(np.eye(32, dtype=np.float32), (4, 1))),
    }
    are, aim, ldt = f(s5_a_re)[0], f(s5_a_im)[0], f(s5_log_dt)[0]
    bre, bim, cre, cim, dsk = f(s5_b_re)[0], f(s5_b_im)[0], f(s5_c_re)[0], f(s5_c_im)[0], f(s5_d)[0]
    in_maps = []
    for core in range(8):
        b, hf = core // 2, core % 2
        g0 = 32 * hf
        xT = np.ascontiguousarray(x[b].T)
        order = list(range(4 * hf, 4 * hf + 4)) + list(range(4 * (1 - hf), 4 * (1 - hf) + 4))
        xfull = _tiles(xT)[:, order, :]
        st = lambda a: np.ascontiguousarray(a[g0:g0 + 32].reshape(16, 128).T)
        bexp = []
        for bsrc in (bre, bim):
            t = np.zeros((128, 4, 128), np.float32)
            for gl_ in range(32):
                i, g8 = gl_ // 8, gl_ % 8
                g2 = gl_ % 2
                t[g8 * 16:(g8 + 1) * 16, i, g2 * 64:(g2 + 1) * 64] = bsrc[g0 + gl_].T
            bexp.append(t)
        cexp = []
        for csrc in (cre, cim):
            t = np.zeros((128, 16, 32), np.float32)
            for gl_ in range(32):
                q, g2 = gl_ // 2, gl_ % 2
                t[g2 * 64:(g2 + 1) * 64, q, g2 * 16:(g2 + 1) * 16] = csrc[g0 + gl_].T
            cexp.append(t)
        sel = np.zeros((128, 2), np.float32)
        sel[:, hf] = 1.0
        m = dict(common)
        m.update({
            "xT_own": np.ascontiguousarray(_tiles(xT)[:, :, hf * NTOK:(hf + 1) * NTOK]),
            "xT_full": np.ascontiguousarray(xfull),
            "c_col": _col(c[b]),
            "a_re_st": st(are), "a_im_st": st(aim),
            "ldt_st": np.ascontiguousarray(np.repeat(ldt[g0:g0 + 32].reshape(16, 2).T, 64, axis=0)),
            "bexp_re": bexp[0], "bexp_im": bexp[1], "cexp_re": cexp[0], "cexp_im": cexp[1],
            "d_col": _col(dsk[512 * hf:512 * hf + 512]), "sel": sel,
        })
        in_maps.append(m)
    res = run_bass_kernel_spmd(_NC, in_maps, core_ids=list(range(8)))
    global _LAST
    _LAST = res
    out = np.empty((4, SEQ, 1024), np.float32)
    for core in range(8):
        b, hf = core // 2, core % 2
        o = np.asarray(res.results[core]["outT"])
        out[b, hf * NTOK:(hf + 1) * NTOK, :] = o.transpose(1, 0, 2).reshape(1024, NTOK).T
    return out
```

```python
import math
from contextlib import ExitStack
import numpy as np
import concourse.bass as bass
import concourse.mybir as mybir
from concourse.bass_utils import run_bass_kernel_spmd

F32 = mybir.dt.float32
BF = mybir.dt.bfloat16
ALU = mybir.AluOpType
AF = mybir.ActivationFunctionType
NTOK = 2048
SEQ = 4096
EPS = 1e-6
PAIRS = [[0, 1], [2, 3], [4, 5], [6, 7]]
_DBG_NOCC = 0
_DBG_STAGE = 0


class _Stop(Exception):
    pass


class Buf:
    def __init__(self, name=""):
        self.name = name
        self.w = None
        self.r = {}


class Prog:
    ENGS = ["pe", "act", "dve", "pool", "sync"]

    def __init__(self, ndma=48):
        self.ops = {e: [] for e in self.ENGS}
        self.cnt = {e: 0 for e in self.ENGS}
        self.waited = {}
        self.ndma = ndma
        self.dma_use = [0] * ndma
        self.pools = {"sync": list(range(0, ndma - 16)), "pool": list(range(ndma - 16, ndma))}
        self.rrq = {"sync": 0, "pool": 0}

    def _wait(self, eng, dep):
        key, val = dep
        if eng == "pe" and key == "pe":
            return
        if self.waited.get((eng, key), 0) >= val:
            return
        self.waited[(eng, key)] = val
        self.ops[eng].append(("wait", key, val))

    def _deps(self, eng, reads, writes):
        for b in reads:
            if b.w is not None:
                self._wait(eng, b.w)
        for b in writes:
            if b.w is not None:
                self._wait(eng, b.w)
            for k, v in b.r.items():
                self._wait(eng, (k, v))

    def _mark(self, me, reads, writes):
        for b in reads:
            b.r[me[0]] = max(b.r.get(me[0], 0), me[1])
        for b in writes:
            b.w = me
            b.r = {}

    def op(self, eng, fn, reads=(), writes=()):
        self._deps(eng, reads, writes)
        self.cnt[eng] += 1
        self.ops[eng].append(("op", fn))
        self._mark((eng, self.cnt[eng]), reads, writes)

    def dma(self, q, fn, reads=(), writes=()):
        pl = self.pools[q]
        i = pl[self.rrq[q] % len(pl)]
        self.rrq[q] += 1
        if self.dma_use[i] > 0:
            self._wait(q, (("d", i), 16 * self.dma_use[i]))
        self._deps(q, reads, writes)
        self.dma_use[i] += 1
        self.ops[q].append(("dma", fn, i))
        self._mark((("d", i), 16 * self.dma_use[i]), reads, writes)

    def cc(self, fn, reads=(), writes=()):
        if _DBG_NOCC:
            return
        self._deps("pool", reads, writes)
        self.cnt.setdefault("cc", 0)
        self.cnt["cc"] += 1
        self.ops["pool"].append(("cc", fn))
        self._mark(("cc", self.cnt["cc"]), reads, writes)

    def final_wait(self, eng, bufs):
        for b in bufs:
            if b.w is not None:
                self._wait(eng, b.w)

    def emit(self, nc, es):
        sems = {e: es.enter_context(nc.semaphore("s_" + e)) for e in ["pe", "act", "dve", "pool", "cc"]}
        dsem = [es.enter_context(nc.semaphore("d%d" % i)) for i in range(self.ndma)]

        def sem_of(key):
            return dsem[key[1]] if isinstance(key, tuple) else sems[key]

        block = es.enter_context(nc.Block())

        def run(name, eng):
            for o in self.ops[name]:
                if o[0] == "wait":
                    eng.wait_ge(sem_of(o[1]), o[2])
                elif o[0] == "op":
                    o[1](eng).then_inc(sems[name], 1)
                elif o[0] == "dma":
                    o[1](eng).then_inc(dsem[o[2]], 16)
                elif o[0] == "cc":
                    o[1](eng).then_inc(sems["cc"], 1)

        @block.tensor
        def _(e):
            run("pe", e)

        @block.scalar
        def _(e):
            run("act", e)

        @block.vector
        def _(e):
            run("dve", e)

        @block.gpsimd
        def _(e):
            run("pool", e)

        @block.sync
        def _(e):
            run("sync", e)


def build():
    nc = bass.Bass("TRN2", target_bir_lowering=False)
    es = ExitStack()
    P = Prog()

    def din(name, shape, dt=F32):
        return nc.dram_tensor(name, list(shape), dt, kind="ExternalInput").ap()

    xT_own = din("xT_own", [128, 8, NTOK])
    xT_full = din("xT_full", [128, 8, SEQ])
    c_col = din("c_col", [128, 8])
    w_ada = din("w_ada", [2, 1024, 6144])
    b_ada_col = din("b_ada_col", [128, 2, 48])
    nmix_col = din("nmix_col", [128, 2, 8])
    nmlp_col = din("nmlp_col", [128, 2, 8])
    nfin_col = din("nfin_col", [128, 8])
    gn_col = din("gn_col", [128, 8])
    a_re_st = din("a_re_st", [128, 16])
    a_im_st = din("a_im_st", [128, 16])
    ldt_st = din("ldt_st", [128, 16])
    bexp_re = din("bexp_re", [128, 4, 128])
    bexp_im = din("bexp_im", [128, 4, 128])
    cexp_re = din("cexp_re", [128, 16, 32])
    cexp_im = din("cexp_im", [128, 16, 32])
    d_col = din("d_col", [128, 4])
    w_glu = din("w_glu", [1024, 2048])
    w_in = din("w_in", [1024, 3088])
    w_g2 = din("w_g2", [16, 512])
    b_gate = din("b_gate", [1, 512])
    w_out = din("w_out", [1024, 1024])
    w_ff1 = din("w_ff1", [2, 1024, 4096])
    w_ff2 = din("w_ff2", [2, 4096, 1024])
    sel = din("sel", [128, 2])
    umat = din("umat", [128, 128])
    ind = din("ind", [128, 2])
    eye32 = din("eye32", [128, 32])
    outT = nc.dram_tensor("outT", [128, 8, NTOK], F32, kind="ExternalOutput").ap()
    z_ibs = [nc.dram_tensor("z_ib%d" % a, [256, SEQ], BF) for a in range(2)]
    z_obs = [nc.dram_tensor("z_ob%d" % a, [512, SEQ], BF) for a in range(2)]
    s_ib = nc.dram_tensor("s_ib", [128, 1024], F32)
    s_ob = nc.dram_tensor("s_ob", [256, 1024], F32)

    def sb(name, shape, dt=F32):
        return es.enter_context(nc.sbuf_tensor(name, list(shape), dt))

    def ps(name, shape, dt=F32):
        return es.enter_context(nc.psum_tensor(name, list(shape), dt))

    A_x = sb("A_x", [128, 8, NTOK])
    A_h = sb("A_h", [128, 8, NTOK], BF)
    A_g = sb("A_g", [128, 16384], BF)
    A_w = [sb("A_w0", [128, 8192], BF), sb("A_w1", [128, 8192], BF)]
    A_s = sb("A_s", [128, 4096])
    A_t = sb("A_t", [128, 4096], BF)
    psA = ps("psA", [128, 4, 512])
    psB = ps("psB", [128, 2, 512])
    psC = ps("psC", [128, 1024])
    bA = [Buf("psA%d" % i) for i in range(4)]
    bB = [Buf("psB%d" % i) for i in range(2)]
    bC = Buf("psC")
    b_x = [[Buf() for _ in range(4)] for _ in range(8)]
    b_h = [Buf() for _ in range(4)]
    b_w = [Buf("w0"), Buf("w1")]
    b_g = Buf("g")
    b_s = Buf("s")
    b_t = Buf("t")

    cst = sb("cst", [128, 16])
    b_c = Buf("cst")
    ones_bf = sb("ones_bf", [128, 128], BF)

    def memset(t, v):
        return lambda e: e.memset(t, v)

    CV = {"one": 1.0, "negpi": -math.pi, "eps": EPS, "zero": 0.0}
    cidx = {k: i for i, k in enumerate(CV)}
    for k, i in cidx.items():
        P.op("dve", memset(cst[:, i:i + 1], CV[k]), writes=[b_c])
    P.op("dve", memset(ones_bf[:], 1.0), writes=[b_c])

    def C(k):
        return cst[:, cidx[k]:cidx[k] + 1]

    small = {}
    b_sm = Buf("small")

    def load_small(name, ap, shape, dt=F32):
        t = sb("sm_" + name, shape, dt)
        P.dma("sync", lambda e, t=t, ap=ap: e.dma_start(out=t[:], in_=ap), writes=[b_sm])
        small[name] = t
        return t

    ccol = sb("sm_ccol", [128, 8])
    b_cc = Buf("ccol")
    P.dma("sync", lambda e: e.dma_start(out=ccol[:], in_=c_col), writes=[b_cc])
    bada = load_small("bada", b_ada_col, [128, 2, 48])
    nmix = load_small("nmix", nmix_col, [128, 2, 8])
    nmlp = load_small("nmlp", nmlp_col, [128, 2, 8])
    nfin = load_small("nfin", nfin_col, [128, 8])
    gnc = load_small("gnc", gn_col, [128, 8])
    arst = load_small("arst", a_re_st, [128, 16])
    aist = load_small("aist", a_im_st, [128, 16])
    ldst = load_small("ldst", ldt_st, [128, 16])
    dcol = load_small("dcol", d_col, [128, 4])
    selt = load_small("selt", sel, [128, 2])
    umt = load_small("umt", umat, [128, 128])
    indt = load_small("indt", ind, [128, 2])
    eyet = load_small("eyet", eye32, [128, 32])
    BT = sb("BT", [128, 2, 4, 128], BF)
    P.dma("pool", lambda e: e.dma_start(out=BT[:, 0], in_=bexp_re), writes=[b_sm])
    P.dma("pool", lambda e: e.dma_start(out=BT[:, 1], in_=bexp_im), writes=[b_sm])
    wg2 = sb("wg2", [16, 512], BF)
    bgb = sb("bgb", [1, 512], BF)
    P.dma("pool", lambda e: e.dma_start(out=wg2[:], in_=w_g2), writes=[b_sm])
    P.dma("pool", lambda e: e.dma_start(out=bgb[:], in_=b_gate), writes=[b_sm])

    def tt(eng, out, a, b, op, reads, writes):
        P.op(eng, lambda e: e.tensor_tensor(out=out, in0=a, in1=b, op=op), reads=reads, writes=writes)

    def ts(eng, out, a, s1, s2, op0, op1, reads, writes):
        if s2 is None:
            P.op(eng, lambda e: e.tensor_scalar(out=out, in0=a, scalar1=s1, scalar2=None, op0=op0),
                 reads=reads, writes=writes)
            return
        P.op(eng, lambda e: e.tensor_scalar(out=out, in0=a, scalar1=s1, scalar2=s2, op0=op0, op1=op1),
             reads=reads, writes=writes)

    def stt(out, a, s, b, op0, op1, reads, writes):
        P.op("dve", lambda e: e.scalar_tensor_tensor(out=out, in0=a, scalar=s, in1=b, op0=op0, op1=op1),
             reads=reads, writes=writes)

    def act(out, a, func, reads, writes, bias=None, scale=1.0):
        kw = {} if bias is None else {"bias": bias}
        P.op("act", lambda e: e.activation(out=out, in_=a, func=func, scale=scale, **kw),
             reads=reads, writes=writes)

    def mm(out, pairs, reads, writes, tp=None):
        def fn(e):
            n = len(pairs)
            ins = None
            for j, (l, r) in enumerate(pairs):
                kw = {} if tp is None else {"tile_position": tp}
                ins = e.matmul(out, l, r, start=(j == 0), stop=(j == n - 1), **kw)
            return ins
        P.op("pe", fn, reads=reads, writes=writes)

    def dump(stage, src, bufs, bf=False):
        if _DBG_STAGE != stage:
            return
        bo = Buf("dbg")
        if bf:
            v = A_x[:].rearrange("p k t -> p (k t)")
            P.op("act", lambda e: e.activation(out=v, in_=src, func=AF.Identity), reads=bufs, writes=[bo])
            P.dma("sync", lambda e: e.dma_start(out=outT, in_=A_x[:]), reads=[bo], writes=[bo])
        else:
            P.dma("sync", lambda e: e.dma_start(out=src[0], in_=src[1]), reads=bufs, writes=[bo])
        P.final_wait("sync", [bo])
        raise _Stop()

    def body():
        cs = sb("cs", [128, 8])
        b_cs = Buf("cs")
        act(cs[:], ccol[:], AF.Silu, [b_cc, b_c], [b_cs])
        modc = sb("modc", [128, 2, 48])
        b_mod = Buf("mod")
        pcount = [0]

        bCs = []
        ada_n = [0]
        b_ada2 = Buf("ada2")

        def ada_issue(l, j):
            wi = 0
            half = ada_n[0] % 2 if ada_n[0] < 8 else 0
            ada_n[0] += 1
            bw_ = b_w[0] if half == 0 else b_ada2
            wv = A_w[wi][:].bitcast(F32)[:, 2048 * half:2048 * (half + 1)].rearrange("p (k c) -> p k c", k=8)
            src = w_ada[l, :, j * 256:(j + 1) * 256].rearrange("(k p) c -> p k c", p=128)
            P.dma("sync", lambda e, wv=wv, src=src: e.dma_start(out=wv, in_=src), writes=[bw_])
            for t in range(2):
                col = l * 48 + 2 * j + t
                mm(psC[:, col:col + 1],
                   [(wv[:, k, t * 128:(t + 1) * 128], cs[:, k:k + 1]) for k in range(8)],
                   [bw_, b_cs], [bC])

        def ada_evac(c0, c1):
            tt("dve", modc[:].rearrange("p l c -> p (l c)")[:, c0:c1], psC[:, c0:c1],
               bada[:].rearrange("p l c -> p (l c)")[:, c0:c1], ALU.add, [bC, b_sm], [b_mod])

        def ada_evac_all():
            pass

        def ada_piece(l, j):
            ada_issue(l, j)

        ada_rest = [(0, j) for j in range(8, 24)] + [(1, j) for j in range(24)]
        for j in range(8):
            ada_piece(0, j)
        ada_evac(0, 16)
        if _DBG_STAGE == 1:
            for lj in ada_rest:
                ada_piece(*lj)
            ada_rest = []
            ada_evac(16, 96)
        dump(1, (outT[:, 0, 0:96], modc[:].rearrange("p l c -> p (l c)")), [b_mod])
        modA = sb("modA", [128, 2, 2, 8])

        def mk_modA(l, s_):
            nrm = nmix if s_ == 0 else nmlp
            stt(modA[:, l, s_, :], modc[:, l, 24 * s_ + 8:24 * s_ + 16], 1.0, nrm[:, l, :], ALU.add, ALU.mult,
                [b_mod, b_sm], [b_mod])

        mk_modA(0, 0)

        def shift(l, s_):
            return modc[:, l, 24 * s_:24 * s_ + 8]

        def gate(l, s_):
            return modc[:, l, 24 * s_ + 16:24 * s_ + 24]

        def rstd_block(xin, nt, div, xreads, tbw=512, br=None, slot=0):
            br = b_s if br is None else br
            sq = A_t[:, 0:nt * tbw].rearrange("p (k c) -> p k c", k=nt)
            act(sq, xin, AF.Square, xreads, [b_t])
            pb = psB[:, slot, 0:tbw]
            mm(pb, [(ones_bf[:], sq[:, k, :]) for k in range(nt)], [b_t, b_c], [bB[slot]])
            r = A_s[:, 3584 - 512 * slot:3584 - 512 * slot + tbw]
            ts("dve", r, pb, 1.0 / div, EPS, ALU.mult, ALU.add, [bB[slot]], [br])
            act(r, r, AF.Sqrt, [br], [br])
            P.op("dve", lambda e: e.reciprocal(out=r, in_=r), reads=[br], writes=[br])
            return r

        myA = sb("myA", [128, 4])
        myB = sb("myB", [128, 4])
        for dst, srcv in ((myA, modA[:, 0, 0, :]), (myB, shift(0, 0))):
            ts("dve", dst[:], srcv[:, 0:4], selt[:, 0:1], None, ALU.mult, ALU.bypass, [b_mod, b_sm], [b_mod])
            stt(dst[:], srcv[:, 4:8], selt[:, 1:2], dst[:], ALU.mult, ALU.add, [b_mod, b_sm], [b_mod])

        uT = A_h[:].rearrange("p k t -> p (k t)").rearrange("p (k t) -> p k t", k=4)
        b_u = Buf("uT")
        TB0 = 256
        xfs = [A_s[:, i_ * 2048:(i_ + 1) * 2048].rearrange("p (k c) -> p k c", k=8) for i_ in range(2)]
        b_xf = [Buf(), Buf()]
        sq0 = A_t[:, 0:2048].rearrange("p (k c) -> p k c", k=8)
        atf = A_t[:].bitcast(F32)
        r0 = [atf[:, 1024:1280], atf[:, 1280:1536]]
        tmp0 = [atf[:, 1536:1792], atf[:, 1792:2048]]
        b_sq, b_r0, b_tmp0 = Buf(), [Buf(), Buf()], [Buf(), Buf()]
        for tb in range(SEQ // TB0):
            xi_ = tb % 2
            xf = xfs[xi_]
            P.dma("pool", lambda e, tb=tb, xf=xf: e.dma_start(out=xf, in_=xT_full[:, :, tb * TB0:(tb + 1) * TB0]),
                  writes=[b_xf[xi_]])
            act(sq0, xf, AF.Square, [b_xf[xi_]], [b_sq])
            pb = psB[:, tb % 2, 0:TB0]
            mm(pb, [(ones_bf[:], sq0[:, k, :]) for k in range(8)], [b_sq, b_c], [bB[tb % 2]])
            r = r0[xi_]
            ts("dve", r, pb, 1.0 / 1024.0, EPS, ALU.mult, ALU.add, [bB[tb % 2]], [b_r0[xi_]])
            act(r, r, AF.Sqrt, [b_r0[xi_]], [b_r0[xi_]])
            P.op("dve", lambda e, r=r: e.reciprocal(out=r, in_=r), reads=[b_r0[xi_]], writes=[b_r0[xi_]])
            for k in range(4):
                tmp = tmp0[k % 2]
                tt("dve", tmp, xf[:, k, :], r, ALU.mult, [b_xf[xi_], b_r0[xi_]], [b_tmp0[k % 2]])
                act(uT[:, k, tb * TB0:(tb + 1) * TB0], tmp, AF.Identity, [b_tmp0[k % 2], b_mod], [b_u],
                    bias=myB[:, k:k + 1], scale=myA[:, k:k + 1])
        P.op("dve", memset(cst[:, 11:12], 0.0), reads=[], writes=[b_s, b_t] + b_xf + [b_sq] + b_r0 + b_tmp0)
        dump(2, uT.rearrange("p k t -> p (k t)"), [b_u], bf=True)
        lam = sb("lam", [128, 16, 16])
        b_l = Buf("lam")
        L_ = lambda i: lam[:, i, :]
        DT, MAG, PH, SN, CSN, LR, LI, DEN, NR, T1, T2, FR, FI = range(13)
        act(L_(DT), ldst[:], AF.Exp, [b_sm, b_c], [b_l])
        tt("dve", L_(MAG), arst[:], L_(DT), ALU.mult, [b_l, b_sm], [b_l])
        act(L_(MAG), L_(MAG), AF.Exp, [b_l], [b_l])
        tt("dve", L_(PH), aist[:], L_(DT), ALU.mult, [b_l, b_sm], [b_l])
        for dst, off in ((SN, 0.0), (CSN, 0.25)):
            ts("dve", L_(T1), L_(PH), 1.0 / (2 * math.pi), off, ALU.mult, ALU.add, [b_l], [b_l])
            for _ in range(5):
                ts("dve", L_(T2), L_(T1), 0.5, None, ALU.is_gt, None, [b_l], [b_l])
                tt("dve", L_(T1), L_(T1), L_(T2), ALU.subtract, [b_l], [b_l])
            act(L_(dst), L_(T1), AF.Sin, [b_l, b_c], [b_l], scale=2 * math.pi)
        tt("dve", L_(LR), L_(MAG), L_(CSN), ALU.mult, [b_l], [b_l])
        tt("dve", L_(LI), L_(MAG), L_(SN), ALU.mult, [b_l], [b_l])
        tt("dve", L_(DEN), arst[:], arst[:], ALU.mult, [b_sm], [b_l])
        tt("dve", L_(T1), aist[:], aist[:], ALU.mult, [b_sm], [b_l])
        tt("dve", L_(DEN), L_(DEN), L_(T1), ALU.add, [b_l], [b_l])
        P.op("dve", lambda e: e.reciprocal(out=L_(DEN), in_=L_(DEN)), reads=[b_l], writes=[b_l])
        ts("dve", L_(NR), L_(LR), -1.0, None, ALU.add, ALU.bypass, [b_l], [b_l])
        tt("dve", L_(T1), L_(NR), arst[:], ALU.mult, [b_l, b_sm], [b_l])
        tt("dve", L_(T2), L_(LI), aist[:], ALU.mult, [b_l, b_sm], [b_l])
        tt("dve", L_(T1), L_(T1), L_(T2), ALU.add, [b_l], [b_l])
        tt("dve", L_(FR), L_(T1), L_(DEN), ALU.mult, [b_l], [b_l])
        tt("dve", L_(T1), L_(LI), arst[:], ALU.mult, [b_l, b_sm], [b_l])
        tt("dve", L_(T2), L_(NR), aist[:], ALU.mult, [b_l, b_sm], [b_l])
        tt("dve", L_(T1), L_(T1), L_(T2), ALU.subtract, [b_l], [b_l])
        tt("dve", L_(FI), L_(T1), L_(DEN), ALU.mult, [b_l], [b_l])
        CT = sb("CT", [128, 2, 16, 32], BF)
        CTf = A_s[:, 1024:2048].rearrange("p (r q c) -> p r q c", r=2, q=16)
        cre = A_s[:, 0:512].rearrange("p (q c) -> p q c", q=16)
        cim = A_s[:, 512:1024].rearrange("p (q c) -> p q c", q=16)
        P.dma("sync", lambda e: e.dma_start(out=cre, in_=cexp_re), writes=[b_s])
        P.dma("sync", lambda e: e.dma_start(out=cim, in_=cexp_im), writes=[b_s])
        ctmp = sb("ctmp", [128, 2, 32])
        for q in range(16):
            fr, fi = lam[:, FR, q:q + 1], lam[:, FI, q:q + 1]
            ts("dve", ctmp[:, 0, :], cim[:, q, :], fi, None, ALU.mult, ALU.bypass, [b_l, b_s], [b_l])
            stt(CTf[:, 0, q, :], cre[:, q, :], fr, ctmp[:, 0, :], ALU.mult, ALU.subtract, [b_l, b_s], [b_s])
            P.op("dve", lambda e, q=q: e.tensor_copy(out=CT[:, 0, q, :], in_=CTf[:, 0, q, :]), reads=[b_s], writes=[b_l])
            ts("dve", ctmp[:, 1, :], cim[:, q, :], fr, -1.0, ALU.mult, ALU.mult, [b_l, b_s], [b_l])
            ts("dve", ctmp[:, 0, :], cre[:, q, :], fi, None, ALU.mult, ALU.bypass, [b_l, b_s], [b_l])
            tt("dve", CT[:, 1, q, :], ctmp[:, 1, :], ctmp[:, 0, :], ALU.subtract, [b_l], [b_l])
            tt("dve", CTf[:, 1, q, :], ctmp[:, 0, :], ctmp[:, 1, :], ALU.subtract, [b_l], [b_s])
        LP = sb("LP", [128, 12, 3, 16])
        P.op("dve", lambda e: e.tensor_copy(out=LP[:, 0, 0, :], in_=L_(LR)), reads=[b_l], writes=[b_l])
        P.op("dve", lambda e: e.tensor_copy(out=LP[:, 0, 1, :], in_=L_(LI)), reads=[b_l], writes=[b_l])
        for k in range(12):
            ts("dve", LP[:, k, 2, :], LP[:, k, 1, :], -1.0, None, ALU.mult, ALU.bypass, [b_l], [b_l])
            if k < 11:
                tt("dve", L_(T1), LP[:, k, 0, :], LP[:, k, 0, :], ALU.mult, [b_l], [b_l])
                tt("dve", L_(T2), LP[:, k, 1, :], LP[:, k, 1, :], ALU.mult, [b_l], [b_l])
                tt("dve", LP[:, k + 1, 0, :], L_(T1), L_(T2), ALU.subtract, [b_l], [b_l])
                tt("dve", L_(T1), LP[:, k, 0, :], LP[:, k, 1, :], ALU.mult, [b_l], [b_l])
                ts("dve", LP[:, k + 1, 1, :], L_(T1), 2.0, None, ALU.mult, ALU.bypass, [b_l], [b_l])

        L = 8
        NCH = SEQ // L
        LPS = sb("LPS", [128, L, 3, 16])
        for c3 in range(3):
            P.op("dve", lambda e, c3=c3: e.tensor_copy(out=LPS[:, 0, c3, :], in_=LP[:, 0, c3, :]),
                 reads=[b_l], writes=[b_l])
        for s1 in range(1, L):
            pr, pi_ = LPS[:, s1 - 1, 0, :], LPS[:, s1 - 1, 1, :]
            tt("dve", L_(T1), pr, LP[:, 0, 0, :], ALU.mult, [b_l], [b_l])
            tt("dve", L_(T2), pi_, LP[:, 0, 1, :], ALU.mult, [b_l], [b_l])
            tt("dve", LPS[:, s1, 0, :], L_(T1), L_(T2), ALU.subtract, [b_l], [b_l])
            tt("dve", L_(T1), pr, LP[:, 0, 1, :], ALU.mult, [b_l], [b_l])
            tt("dve", L_(T2), pi_, LP[:, 0, 0, :], ALU.mult, [b_l], [b_l])
            tt("dve", LPS[:, s1, 1, :], L_(T1), L_(T2), ALU.add, [b_l], [b_l])
            ts("dve", LPS[:, s1, 2, :], LPS[:, s1, 1, :], -1.0, None, ALU.mult, ALU.bypass, [b_l], [b_l])
        CH = A_w[0][:].bitcast(F32)[:, 2048:4096].rearrange("p (a r n) -> p a r n", a=2, r=2)
        bCH = [[Buf(), Buf()], [Buf(), Buf()]]
        P.op("dve", memset(cst[:, 13:14], 0.0), reads=[], writes=[b_ada2] + sum(bCH, []))
        X = A_x[:].rearrange("p k t -> p (k t)").rearrange("p (a r t) -> p a r t", a=2, r=2)
        bX = [[Buf(), Buf()], [Buf(), Buf()]]
        zT = A_g[:].rearrange("p (k t) -> p k t", k=4)
        b_z = Buf("zT")
        xb = A_s[:].bitcast(BF).rearrange("p (r t) -> p r t", r=2)
        ysc = A_t[:].bitcast(F32)[:, 0:1536].rearrange("p (a c) -> p a c", a=3)
        b_y = b_t
        Sb = A_t[:, 3072:4096].rearrange("p (r n) -> p r n", r=2)
        b_sb = Buf("Sb")
        P.op("dve", memset(Sb[:, :, 0:1], 0.0), reads=[], writes=[b_sb, b_t])
        CQ = A_w[1][:].rearrange("p (q r s c) -> p q r s c", q=16, r=2, s=L)
        b_cq = Buf("CQ")
        for s1 in range(L):
            lr_b = LPS[:, s1, 0, :].unsqueeze(2).broadcast_to([128, 16, 32])
            li_b = LPS[:, s1, 1, :].unsqueeze(2).broadcast_to([128, 16, 32])
            cfr, cfi = CTf[:, 0], CTf[:, 1]
            t1 = A_s[:, 2048:2560].rearrange("p (q c) -> p q c", q=16)
            t2 = A_s[:, 2560:3072].rearrange("p (q c) -> p q c", q=16)
            tt("dve", t1, cfr, lr_b, ALU.mult, [b_s, b_l], [b_s])
            tt("dve", t2, cfi, li_b, ALU.mult, [b_s, b_l], [b_s])
            tt("dve", CQ[:, :, 0, s1, :], t1, t2, ALU.subtract, [b_s], [b_cq, b_w[1]])
            tt("dve", t1, cfr, li_b, ALU.mult, [b_s, b_l], [b_s])
            tt("dve", t2, cfi, lr_b, ALU.mult, [b_s, b_l], [b_s])
            tt("dve", t1, t1, t2, ALU.add, [b_s], [b_s])
            ts("dve", CQ[:, :, 1, s1, :], t1, -1.0, None, ALU.mult, ALU.bypass, [b_s], [b_cq])
        pcnt = 0

        def cview(a_, ri):
            return X[:, a_, ri, :].rearrange("p (c j) -> p c j", j=L)

        def hs_level(src, dst, bs, bd, a, b, nb, d, n, sl):
            sr, si, dr, di = src[0], src[1], dst[0], dst[1]
            stt(sl(dr, d, n), sl(sr, 0, n - d), a, sl(sr, d, n), ALU.mult, ALU.add, [bs[0], b_l], [bd[0]])
            stt(sl(di, d, n), sl(si, 0, n - d), a, sl(si, d, n), ALU.mult, ALU.add, [bs[1], b_l], [bd[1]])
            h0 = d // 2 if d > 1 else 0
            bh = [Buf(), Buf()]
            P.op("dve", lambda e: e.tensor_copy(out=sl(dr, h0, d), in_=sl(sr, h0, d)), reads=[bs[0]], writes=[bh[0]])
            P.op("dve", lambda e: e.tensor_copy(out=sl(di, h0, d), in_=sl(si, h0, d)), reads=[bs[1]], writes=[bh[1]])
            stt(sl(dr, d, n), sl(si, 0, n - d), nb, sl(dr, d, n), ALU.mult, ALU.add, [bs[1], b_l, bh[0]], [bd[0]])
            stt(sl(di, d, n), sl(sr, 0, n - d), b, sl(di, d, n), ALU.mult, ALU.add, [bs[0], b_l, bh[1]], [bd[1]])

        def madd(vr, vi, dsel, ssel, k, q, bx):
            a, b, nb = LP[:, k, 0, q:q + 1], LP[:, k, 1, q:q + 1], LP[:, k, 2, q:q + 1]
            dr, di, sr, si = dsel(vr), dsel(vi), ssel(vr), ssel(vi)
            stt(dr, sr, a, dr, ALU.mult, ALU.add, [bx[0], b_l], [bx[0]])
            stt(di, si, a, di, ALU.mult, ALU.add, [bx[1], b_l], [bx[1]])
            stt(dr, si, nb, dr, ALU.mult, ALU.add, [bx[0], bx[1], b_l], [bx[0]])
            stt(di, sr, b, di, ALU.mult, ALU.add, [bx[0], bx[1], b_l], [bx[1]])

        pcb = [0, 0]
        b_g2 = [Buf(), Buf()]
        P.op("dve", memset(cst[:, 9:10], 0.0), reads=[], writes=[b_t] + b_g2)
        Ddiag = sb("Ddiag", [128, 4, 32], BF)
        for i_ in range(4):
            ts("dve", Ddiag[:, i_, :], eyet[:], dcol[:, i_:i_ + 1], None, ALU.mult, ALU.bypass, [b_sm], [b_l])

        def stA(q):
            i, q4, xp = q // 4, q % 4, q % 2
            rows = slice(32 * q4, 32 * q4 + 32)
            for ri in range(2):
                for tb in range(8):
                    pb = pcb[0] % 2
                    pcb[0] += 1
                    mm(psB[:, pb, :], [(BT[rows, ri, i, :], uT[rows, i, tb * 512:(tb + 1) * 512])],
                       [b_sm, b_u], [bB[pb]], tp=(32 * q4, 0))
                    act(X[:, xp, ri, tb * 512:(tb + 1) * 512], psB[:, pb, :], AF.Identity, [bB[pb], b_c],
                        [bX[xp][ri]])

        def stB1(q):
            xp = q % 2
            vr, vi = cview(xp, 0), cview(xp, 1)
            madd(vr, vi, lambda v: v[:, :, 1::2], lambda v: v[:, :, 0::2], 0, q, bX[xp])
            madd(vr, vi, lambda v: v[:, :, 3::4], lambda v: v[:, :, 1::4], 1, q, bX[xp])
            madd(vr, vi, lambda v: v[:, :, 7], lambda v: v[:, :, 3], 2, q, bX[xp])
            madd(vr, vi, lambda v: v[:, :, 5], lambda v: v[:, :, 3], 1, q, bX[xp])
            madd(vr, vi, lambda v: v[:, :, 2::2], lambda v: v[:, :, 1:6:2], 0, q, bX[xp])

        def stB2(q):
            xp = q % 2
            for ri in range(2):
                act(CH[:, 0, ri, :], cview(xp, ri)[:, :, L - 1], AF.Identity, [bX[xp][ri], b_c], [bCH[0][ri]])
            for k in range(9):
                s_, d_ = k % 2, 1 - (k % 2)
                hs_level([CH[:, s_, 0, :], CH[:, s_, 1, :]], [CH[:, d_, 0, :], CH[:, d_, 1, :]], bCH[s_], bCH[d_],
                         LP[:, k + 3, 0, q:q + 1], LP[:, k + 3, 1, q:q + 1], LP[:, k + 3, 2, q:q + 1], 1 << k, NCH,
                         lambda v, lo, hi: v[:, lo:hi])

        def stCconv(q):
            i, q4, xp = q // 4, q % 4, q % 2
            rows = slice(32 * q4, 32 * q4 + 32)
            for ri in range(2):
                act(Sb[:, ri, 1:NCH], CH[:, 1, ri, 0:NCH - 1], AF.Identity, [bCH[1][ri], b_c], [b_sb])
                act(xb[:, ri, :], X[:, xp, ri, :], AF.Identity, [bX[xp][ri], b_c], [b_s])
            for tb in range(4):
                ymm_emit(q, tb)

        def ymm_emit(q, tb):
            i, q4 = q // 4, q % 4
            rows = slice(32 * q4, 32 * q4 + 32)
            tsl = slice(tb * 512, (tb + 1) * 512)
            pb = (pcb[1] + tb) % 4
            pv = psA[rows, pb, :].rearrange("p (c j) -> p c j", j=L)

            def ymm(e):
                tp = (0, 32 * q4)
                e.matmul(psA[rows, pb, :], Ddiag[rows, i, :], uT[rows, i, tsl], start=True, stop=False,
                         tile_position=(32 * q4, 32 * q4))
                e.matmul(psA[rows, pb, :], CT[:, 0, q, :], xb[:, 0, tsl], start=False, stop=False, tile_position=tp)
                ins = e.matmul(psA[rows, pb, :], CT[:, 1, q, :], xb[:, 1, tsl], start=False, stop=False,
                               tile_position=tp)
                for s1 in range(L):
                    for ri in range(2):
                        ins = e.matmul(pv[:, :, s1], CQ[:, q, ri, s1, :], Sb[:, ri, tb * 64:(tb + 1) * 64],
                                       start=False, stop=(ri == 1), tile_position=tp,
                                       skip_group_check=True)
                return ins
            P.op("pe", ymm, reads=[b_l, b_s, b_cq, b_sb, b_u], writes=[bA[pb]])

        def stCevac(q):
            i, q4 = q // 4, q % 4
            rows = slice(32 * q4, 32 * q4 + 32)
            for tb in range(8):
                tsl = slice(tb * 512, (tb + 1) * 512)
                pb = (pcb[1] + tb) % 4
                act(zT[rows, i, tsl], psA[rows, pb, :], AF.Identity, [bA[pb], b_c], [b_z], bias=C("zero")[rows])
                if tb + 4 < 8:
                    ymm_emit(q, tb + 4)
            pcb[1] += 8
            if q4 == 3:
                def gelu_prep(tb):
                    tsl = slice(tb * 512, (tb + 1) * 512)
                    yb, t1, bt1 = zT[:, i, tsl], ysc[:, tb % 2, :], b_g2[tb % 2]
                    act(t1, yb, AF.Square, [b_z, b_c], [bt1], scale=math.sqrt(0.044715))
                    stt(t1, t1, 1.0, yb, ALU.add, ALU.mult, [bt1, b_z], [bt1])
                    act(t1, t1, AF.Sigmoid, [bt1, b_c], [bt1], scale=2.0 * math.sqrt(2.0 / math.pi))

                gelu_prep(0)
                for tb in range(8):
                    if tb + 1 < 8:
                        gelu_prep(tb + 1)
                    tsl = slice(tb * 512, (tb + 1) * 512)
                    yb = zT[:, i, tsl]
                    tt("dve", yb, yb, ysc[:, tb % 2, :], ALU.mult, [b_g2[tb % 2], b_z], [b_z])

        stA(0)
        stB1(0)
        stB2(0)
        for q in range(16):
            if q + 1 < 16:
                stA(q + 1)
            stCconv(q)
            for _ in range(2):
                if ada_rest:
                    ada_issue(*ada_rest.pop(0))
            if q + 1 < 16:
                stB1(q + 1)
            stCevac(q)
            if q + 1 < 16:
                stB2(q + 1)
            if q >= 12:
                for _ in range(2):
                    if ada_rest:
                        ada_issue(*ada_rest.pop(0))
        P.op("dve", memset(cst[:, 10:11], 0.0), reads=[], writes=[b_t, b_sb, b_cq, b_w[1], b_w[0], bC] + bCs + b_g2 + sum(bCH, []))

        dump(3, zT.rearrange("p k t -> p (k t)"), [b_z], bf=True)
        b_zo = Buf("z_ob")
        for a in range(2):
            P.dma("sync", lambda e, a=a: e.dma_start(out=z_ibs[a].ap().rearrange("(j p) t -> p j t", p=128),
                                                      in_=zT[:, 2 * a:2 * a + 2, :]), reads=[b_z], writes=[b_zo])
        for a in range(2):
            P.cc(lambda e, a=a: e.collective_compute("AllGather", ALU.bypass, replica_groups=PAIRS,
                                                     ins=[z_ibs[a][:, :]], outs=[z_obs[a][:, :]]),
                 reads=[b_zo], writes=[b_zo])
        while ada_rest:
            ada_piece(*ada_rest.pop(0))
        ada_evac(16, 96)
        mk_modA(0, 1)
        mk_modA(1, 0)
        mk_modA(1, 1)
        for k in range(8):
            P.dma("sync", lambda e, k=k: e.dma_start(out=A_x[:, k, :], in_=xT_own[:, k, :]),
                  reads=[], writes=[bX[0][0], bX[0][1], bX[1][0], bX[1][1]] + b_x[k])
        zc = A_t[:].rearrange("p (k c) -> p k c", k=8)
        zc2 = A_s[:].bitcast(BF)[:, 0:4096].rearrange("p (k c) -> p k c", k=8)
        for tb in range(4):
            for dstz, off, bz in ((zc, 0, b_t), (zc2, NTOK, b_s)):
                for a in range(2):
                    for r in range(2):
                        srcz = z_obs[a][r * 256:(r + 1) * 256, off + tb * 512:off + (tb + 1) * 512].rearrange(
                            "(j p) t -> p j t", p=128)
                        t0 = r * 4 + a * 2
                        P.dma("sync", lambda e, d_=dstz[:, t0:t0 + 2, :], s_=srcz: e.dma_start(out=d_, in_=s_),
                              reads=[b_zo], writes=[bz])
            hv = A_h[:, :, tb * 512:(tb + 1) * 512]
            ts("dve", hv, zc, selt[:, 0:1], None, ALU.mult, ALU.bypass, [b_t, b_sm, b_u], [b_h[tb], b_u])
            stt(hv, zc2, selt[:, 1:2], hv, ALU.mult, ALU.add, [b_s, b_sm], [b_h[tb]])

        wq_cnt = [0]

        def load_w(dst, src, bw):
            P.dma("pool", lambda e: e.dma_start(out=dst, in_=src), writes=[bw])

        for j in range(2):
            wi = wq_cnt[0] % 2
            wq_cnt[0] += 1
            W = A_w[wi][:].rearrange("p (k c) -> p k c", k=8)
            load_w(W[:, :, 0:512], w_glu[:, j * 512:(j + 1) * 512].rearrange("(k p) c -> p k c", p=128), b_w[wi])
            load_w(W[:, :, 512:1024], w_glu[:, 1024 + j * 512:1024 + (j + 1) * 512].rearrange("(k p) c -> p k c", p=128),
                   b_w[wi])
            for tb in range(4):
                tsl = slice(tb * 512, (tb + 1) * 512)
                for t in range(4):
                    n = 4 * j + t
                    pb = pcnt % 4
                    pcnt += 1
                    mm(psA[:, pb, :], [(W[:, k, 512 + t * 128:512 + (t + 1) * 128], A_h[:, k, tsl]) for k in range(8)],
                       [b_w[wi], b_h[tb]], [bA[pb]])
                    sg = ysc[:, 0, :]
                    act(sg, psA[:, pb, :], AF.Sigmoid, [bA[pb], b_c], [b_y])
                    pb2 = pcnt % 4
                    pcnt += 1
                    mm(psA[:, pb2, :], [(W[:, k, t * 128:(t + 1) * 128], A_h[:, k, tsl]) for k in range(8)],
                       [b_w[wi], b_h[tb]], [bA[pb2]])
                    y = ysc[:, 1, :]
                    tt("dve", y, psA[:, pb2, :], sg, ALU.mult, [bA[pb2], b_y], [b_y])
                    stt(A_x[:, n, tsl], y, gate(0, 0)[:, n:n + 1], A_x[:, n, tsl], ALU.mult, ALU.add,
                        [b_y, b_mod, b_x[n][tb]], [b_x[n][tb]])

        b_tm = [Buf(), Buf(), Buf(), Buf()]
        b_rr = Buf("rstd")
        b_rr2 = [b_rr, Buf("rstd1")]

        def norm_mod(l, s_):
            P.op("dve", memset(cst[:, 15:16], 0.0), reads=[], writes=[b_s] + b_rr2 + b_tm)
            for tb in range(4):
                tsl = slice(tb * 512, (tb + 1) * 512)
                xr = [b_x[k][tb] for k in range(8)]
                brr = b_rr2[tb % 2]
                r = rstd_block(A_x[:, :, tsl], 8, 1024.0, xr, br=brr, slot=tb % 2)
                for k in range(8):
                    tmp = A_s[:, (k % 4) * 512:(k % 4 + 1) * 512]
                    tt("dve", tmp, A_x[:, k, tsl], r, ALU.mult, [brr, b_x[k][tb]], [b_tm[k % 4]])
                    act(A_h[:, k, tsl], tmp, AF.Identity, [b_tm[k % 4], b_mod], [b_h[tb]],
                        bias=shift(l, s_)[:, k:k + 1], scale=modA[:, l, s_, k:k + 1])
            P.op("dve", memset(cst[:, 15:16], 0.0), reads=[], writes=[b_s] + b_rr2 + b_tm)

        bAT = [Buf(), Buf()]

        b_ys = [Buf(), Buf(), Buf()]

        def mlp(l):
            nonlocal pcnt
            aT = A_g[:, 0:4096].rearrange("p (a f c) -> p a f c", a=2, f=4)
            first = [True]
            Wv_ = {}
            bCh = [Buf(), Buf()]
            s2_banks = [(psB[:, 0, :], bB[0]), (psB[:, 1, :], bB[1]), (psC[:, 0:512], bCh[0]), (psC[:, 512:1024], bCh[1])]
            P.op("dve", memset(cst[:, 12:13], 0.0), reads=[], writes=[b_t, bC] + b_ys + bCh)

            def s1(n):
                nonlocal pcnt
                e8, tb = n // 4, n % 4
                if tb == 0:
                    wi = wq_cnt[0] % 2
                    wq_cnt[0] += 1
                    W1 = A_w[wi][:, 0:4096].rearrange("p (k c) -> p k c", k=8)
                    W2 = A_w[wi][:, 4096:8192].rearrange("p (k c) -> p k c", k=4)
                    load_w(W1, w_ff1[l, :, e8 * 512:(e8 + 1) * 512].rearrange("(k p) c -> p k c", p=128), b_w[wi])
                    load_w(W2, w_ff2[l, e8 * 512:(e8 + 1) * 512, :].rearrange("(k p) c -> p k c", p=128), b_w[wi])
                    Wv_[e8] = (wi, W1, W2)
                wi, W1, W2 = Wv_[e8]
                tsl = slice(tb * 512, (tb + 1) * 512)
                ap_ = n % 2
                for f in range(4):
                    pb = pcnt % 4
                    pcnt += 1
                    mm(psA[:, pb, :], [(W1[:, k, f * 128:(f + 1) * 128], A_h[:, k, tsl]) for k in range(8)],
                       [b_w[wi], b_h[tb]], [bA[pb]])
                    sl3 = (4 * n + f) % 3
                    rl = ysc[:, sl3, :]
                    act(rl, psA[:, pb, :], AF.Relu, [bA[pb], b_c], [b_ys[sl3]])
                    extra = [b_g, b_z] if first[0] else []
                    first[0] = False
                    act(aT[:, ap_, f, :], rl, AF.Square, [b_ys[sl3], b_c], [bAT[ap_]] + extra)

            def s2(n):
                nonlocal pcnt
                e8, tb = n // 4, n % 4
                wi, W1, W2 = Wv_[e8]
                tsl = slice(tb * 512, (tb + 1) * 512)
                ap_ = n % 2
                for nn in range(8):
                    pb = pcnt % 4
                    pcnt += 1
                    pbank, pbuf = s2_banks[pb]
                    mm(pbank, [(W2[:, f, nn * 128:(nn + 1) * 128], aT[:, ap_, f, :]) for f in range(4)],
                       [b_w[wi], bAT[ap_]], [pbuf])
                    stt(A_x[:, nn, tsl], pbank, gate(l, 1)[:, nn:nn + 1], A_x[:, nn, tsl], ALU.mult, ALU.add,
                        [pbuf, b_mod, b_x[nn][tb]], [b_x[nn][tb]])

            s1(0)
            for n in range(32):
                if n + 1 < 32:
                    s1(n + 1)
                s2(n)
            P.op("dve", memset(cst[:, 12:13], 0.0), reads=[], writes=[b_t, bC] + b_ys + bCh)

        allx = [b_x[k][tb] for k in range(8) for tb in range(4)]
        dump(4, (outT, A_x[:]), allx)
        norm_mod(0, 1)
        mlp(0)
        dump(5, (outT, A_x[:]), allx)

        norm_mod(1, 0)
        Wq = A_w[0][:, 0:4096].rearrange("p (k c) -> p k c", k=8)
        Wk = A_w[0][:, 4096:8192].rearrange("p (k c) -> p k c", k=8)
        Wv = A_w[1][:].rearrange("p (k c) -> p k c", k=8)
        Wr = A_g[:, 0:8192].rearrange("p (k c) -> p k c", k=8)
        Wo = A_g[:, 8192:16384].rearrange("p (k c) -> p k c", k=8)
        Wg = sb("Wg", [128, 8, 16], BF)
        win = lambda a, b: w_in[:, a:b].rearrange("(k p) c -> p k c", p=128)
        load_w(Wq, win(0, 512), b_w[0])
        load_w(Wk, win(512, 1024), b_w[0])
        load_w(Wv, win(1024, 2048), b_w[1])
        load_w(Wg[:], win(2048, 2064), b_g)
        P.dma("pool", lambda e: e.dma_start(out=Wr, in_=win(2064, 3088)), writes=[b_g, bAT[0], bAT[1]])
        load_w(Wo, w_out.rearrange("(k p) c -> p k c", p=128), b_g)

        S = sb("S", [128, 4, 256])
        Sbf = sb("Sbf", [128, 1, 4, 256], BF)
        b_S = Buf("S")
        b_Sbf = [Buf(), Buf()]
        Sbfs = [Sbf[:, 0],
                LP[:].rearrange("p a b c -> p (a b c)").bitcast(BF)[:, 0:1024].rearrange("p (h e) -> p h e", h=4)]
        gl = sb("gl", [16, 128], BF)
        la = A_s[:, 0:512]
        Ef = A_s[:, 512:1024]
        gsc = A_s[:, 1024:2048].rearrange("p (k t) -> p k t", k=8)
        rsd = A_s[:, 2048:2560].rearrange("p (k t) -> p k t", k=4)
        asb = A_s[:].bitcast(BF)
        kdec = [A_t[:, 0:512], asb[:, 5376:5888]]
        vtok = [A_t[:, 512:1536], asb[:, 5888:6912]]
        qTt = [A_t[:, 3584:4096].rearrange("p (k t) -> p k t", k=4),
               asb[:, 6912:7424].rearrange("p (k t) -> p k t", k=4)]
        srs = [BT[:].rearrange("p a b c -> p (a b c)").rearrange("p (k t) -> p k t", k=8),
               CT[:].rearrange("p a b c -> p (a b c)").rearrange("p (k t) -> p k t", k=8)]
        osq = A_t[:, 1536:2560].rearrange("p (k t) -> p k t", k=8)
        gat = A_t[:, 2560:3584].rearrange("p (k t) -> p k t", k=8)
        dec = sb("dec", [128, 2, 4, 2])
        b_gl, b_la, b_gs, b_osq, b_rsd, b_gat = [Buf() for _ in range(6)]
        b_kd, b_v, b_dec, b_q, b_sr = [[Buf(), Buf()] for _ in range(5)]
        P.op("dve", memset(S[:], 0.0), writes=[b_S])

        def gla_front(tt_, final):
            st = tt_ % 2
            tk = slice(tt_ * 128, (tt_ + 1) * 128)
            tb = tt_ // 4
            hT = lambda k: A_h[:, k, tk]
            mm(psB[0:16, 0, 0:128], [(Wg[:, k, :], hT(k)) for k in range(8)], [b_g, b_h[tb]], [bB[0]])
            act(gl[:], psB[0:16, 0, 0:128], AF.Identity, [bB[0], b_c], [b_gl], bias=C("zero")[0:16])
            mm(psA[:, 0, :], [(ones_bf[0:1, :], bgb[:]), (gl[:], wg2[:])], [b_c, b_sm, b_gl], [bA[0]])
            act(Ef, psA[:, 0, :], AF.Exp, [bA[0], b_c], [b_la], scale=-1.0)
            act(la, Ef, AF.Ln, [b_la, b_c], [b_la], bias=C("one"))
            mm(psA[:, 1, :], [(umt[:], la)], [b_sm, b_la], [bA[1]])
            for h in range(4):
                mm(psB[:, 1, 2 * h:2 * h + 2], [(la[:, h * 128:(h + 1) * 128], indt[:])], [b_la, b_sm], [bB[1]])
            mm(psA[:, 2, :], [(hT(k), Wk[:, k, :]) for k in range(8)], [b_w[0], b_h[tb]], [bA[2]])
            act(Ef, psA[:, 1, :], AF.Exp, [bA[1], b_c], [b_la], scale=-1.0 / 16.0)
            act(dec[:, st].rearrange("p h c -> p (h c)"), psB[:, 1, 0:8], AF.Exp, [bB[1], b_c], [b_dec[st]],
                scale=-1.0 / 16.0)
            tt("dve", kdec[st], psA[:, 2, :], Ef, ALU.mult, [bA[2], b_la], [b_kd[st]])
            for hh in range(2):
                mm(psA[:, 3, :], [(hT(k), Wv[:, k, hh * 512:(hh + 1) * 512]) for k in range(8)],
                   [b_w[1], b_h[tb]], [bA[3]])
                act(vtok[st][:, hh * 512:(hh + 1) * 512], psA[:, 3, :], AF.Identity, [bA[3], b_c], [b_v[st]])
            if final:
                for h in range(4):
                    mm(psB[:, 0, h * 128:(h + 1) * 128], [(Wq[:, k, h * 128:(h + 1) * 128], hT(k)) for k in range(8)],
                       [b_w[0], b_h[tb]], [bB[0]])
                act(qTt[st].rearrange("p k t -> p (k t)"), psB[:, 0, :], AF.Identity, [bB[0], b_c], [b_q[st]],
                    scale=128.0 ** -0.5)
                pr = psA[:, 0:2, :].rearrange("p a c -> p (a c)")
                for t8 in range(8):
                    mm(pr[:, t8 * 128:(t8 + 1) * 128], [(Wr[:, k, t8 * 128:(t8 + 1) * 128], hT(k)) for k in range(8)],
                       [b_g, b_h[tb]], [bA[0], bA[1]])
                act(srs[st].rearrange("p k t -> p (k t)"), pr, AF.Silu, [bA[0], bA[1], b_c], [b_sr[st]])

        def gla_back(tt_, final):
            st = tt_ % 2
            tk = slice(tt_ * 128, (tt_ + 1) * 128)
            tb = tt_ // 4
            for cc in range(2):
                rws = slice(64 * cc, 64 * cc + 64)
                for h in range(4):
                    mm(psC[:, h * 256:(h + 1) * 256],
                       [(kdec[st][rws, h * 128:(h + 1) * 128], vtok[st][rws, h * 256:(h + 1) * 256])],
                       [b_kd[st], b_v[st]], [bC], tp=(64 * cc, 0))
                for h in range(4):
                    stt(S[:, h, :], S[:, h, :], dec[:, st, h, cc:cc + 1], psC[:, h * 256:(h + 1) * 256], ALU.mult,
                        ALU.add, [b_S, b_dec[st], bC], [b_S])
                if final:
                    act(Sbfs[cc].rearrange("p h e -> p (h e)"), S[:].rearrange("p h e -> p (h e)"), AF.Identity,
                        [b_S, b_c], [b_Sbf[cc]])
            if final:
                for cc in range(2):
                    for h in range(4):
                        for e2 in range(2):
                            t8 = 2 * h + e2
                            mm(psB[:, 1, t8 * 64:(t8 + 1) * 64],
                               [(Sbfs[cc][:, h, e2 * 128:(e2 + 1) * 128], qTt[st][:, h, 64 * cc:64 * cc + 64])],
                               [b_Sbf[cc], b_q[st]], [bB[1]])
                    act(gsc[:, :, 64 * cc:64 * cc + 64], psB[:, 1, :].rearrange("p (k t) -> p k t", k=8),
                        AF.Identity, [bB[1], b_c], [b_gs])
            if final:
                act(osq, gsc, AF.Square, [b_gs, b_c], [b_osq])
                for h in range(4):
                    mm(psB[:, 0, h * 128:(h + 1) * 128],
                       [(ones_bf[:], osq[:, 2 * h, :]), (ones_bf[:], osq[:, 2 * h + 1, :])], [b_osq, b_c], [bB[0]])
                ts("dve", A_s[:, 2048:2560], psB[:, 0, :], 1.0 / 256.0, EPS, ALU.mult, ALU.add, [bB[0]], [b_rsd])
                act(A_s[:, 2048:2560], A_s[:, 2048:2560], AF.Sqrt, [b_rsd], [b_rsd])
                P.op("dve", lambda e: e.reciprocal(out=A_s[:, 2048:2560], in_=A_s[:, 2048:2560]),
                     reads=[b_rsd], writes=[b_rsd])
                for t8 in range(8):
                    stt(gsc[:, t8, :], gsc[:, t8, :], gnc[:, t8:t8 + 1], rsd[:, t8 // 2, :], ALU.mult, ALU.mult,
                        [b_gs, b_sm, b_rsd], [b_gs])
                tt("dve", gat, gsc, srs[st], ALU.mult, [b_gs, b_sr[st]], [b_gat])
                po = psA[:, 2:4, :].rearrange("p a c -> p (a c)")
                for n in range(8):
                    mm(po[:, n * 128:(n + 1) * 128], [(Wo[:, k, n * 128:(n + 1) * 128], gat[:, k, :]) for k in range(8)],
                       [b_g, b_gat], [bA[2], bA[3]])
                for n in range(8):
                    stt(A_x[:, n, tk], po[:, n * 128:(n + 1) * 128], gate(1, 0)[:, n:n + 1], A_x[:, n, tk],
                        ALU.mult, ALU.add, [bA[2], bA[3], b_mod, b_x[n][tb]], [b_x[n][tb]])

        def gla_pass(final, first_front_done=False):
            if not first_front_done:
                gla_front(0, final)
            for tt_ in range(16):
                if tt_ + 1 < 16:
                    gla_front(tt_ + 1, final)
                gla_back(tt_, final)

        gla_tmp = [b_s, b_t, b_gl, b_la, b_gs, b_osq, b_rsd, b_gat, b_sm, b_l] + sum([b_kd, b_v, b_dec, b_q, b_sr], [])
        bar = sb("bar", [128, 1])

        def barrier(bufs):
            P.op("dve", memset(bar[:], 0.0), writes=bufs)

        barrier(gla_tmp)
        gla_pass(False)
        b_so = Buf("s_ob")
        P.dma("sync", lambda e: e.dma_start(out=s_ib[:, :], in_=S[:].rearrange("p h e -> p (h e)")), reads=[b_S], writes=[b_so])
        P.cc(lambda e: e.collective_compute("AllGather", ALU.bypass, replica_groups=PAIRS,
                                            ins=[s_ib[:, :]], outs=[s_ob[:, :]]), reads=[b_so], writes=[b_so])
        P.dma("sync", lambda e: e.dma_start(out=S[:].rearrange("p h e -> p (h e)"), in_=s_ob[0:128, :]),
              reads=[b_so], writes=[b_S])
        gla_front(0, True)
        ts("dve", S[:].rearrange("p h e -> p (h e)"), S[:].rearrange("p h e -> p (h e)"), selt[:, 1:2], None,
           ALU.mult, ALU.bypass, [b_S, b_sm], [b_S])
        gla_pass(True, first_front_done=True)
        barrier(gla_tmp)
        dump(6, (outT, A_x[:]), allx)

        norm_mod(1, 1)
        mlp(1)
        dump(7, (outT, A_x[:]), allx)

        b_out = Buf("out")
        for tb in range(4):
            tsl = slice(tb * 512, (tb + 1) * 512)
            xr = [b_x[k][tb] for k in range(8)]
            r = rstd_block(A_x[:, :, tsl], 8, 1024.0, xr)
            for k in range(8):
                stt(A_x[:, k, tsl], A_x[:, k, tsl], nfin[:, k:k + 1], r, ALU.mult, ALU.mult,
                    [b_s, b_sm, b_x[k][tb]], [b_x[k][tb]])
            P.dma("sync", lambda e, tsl=tsl: e.dma_start(out=outT[:, :, tsl], in_=A_x[:, :, tsl]),
                  reads=xr, writes=[b_out])
        P.final_wait("sync", [b_out])
    try:
        body()
    except _Stop:
        pass
    P.emit(nc, es)
    es.close()
    return nc


_NC = None
_LAST = None


def _tiles(a2d):
    C, T = a2d.shape
    return np.ascontiguousarray(a2d.reshape(C // 128, 128, T).transpose(1, 0, 2))


def _col(v):
    return np.ascontiguousarray(v.reshape(-1, 128).T)


def kernel(x, c, w_ada, b_ada, norm_mix, norm_mlp, s5_a_re, s5_a_im, s5_log_dt, s5_b_re, s5_b_im,
           s5_c_re, s5_c_im, s5_d, s5_w_glu, gla_w_in, gla_w_gate2, gla_b_gate, gla_g_norm, gla_w_out,
           w_ff1, w_ff2, norm_final):
    global _NC
    f = lambda a: np.ascontiguousarray(np.asarray(a, dtype=np.float32))
    x, c = f(x), f(c)
    if _NC is None:
        _NC = build()
    umat = np.zeros((128, 128), np.float32)
    for s in range(128):
        for s2 in range(s + 1, (s // 64 + 1) * 64):
            umat[s2, s] = 1.0
    ind = np.zeros((128, 2), np.float32)
    ind[:64, 0] = 1.0
    ind[64:, 1] = 1.0
    common = {
        "w_ada": f(w_ada),
        "b_ada_col": np.ascontiguousarray(np.stack([_col(f(b_ada)[l]) for l in range(2)], axis=1)),
        "nmix_col": np.ascontiguousarray(np.stack([_col(f(norm_mix)[l]) for l in range(2)], axis=1)),
        "nmlp_col": np.ascontiguousarray(np.stack([_col(f(norm_mlp)[l]) for l in range(2)], axis=1)),
        "nfin_col": _col(f(norm_final)), "gn_col": _col(f(gla_g_norm)[0]),
        "w_glu": f(s5_w_glu)[0], "w_in": f(gla_w_in)[0], "w_g2": f(gla_w_gate2)[0], "b_gate": f(gla_b_gate),
        "w_out": f(gla_w_out)[0], "w_ff1": f(w_ff1), "w_ff2": f(w_ff2), "umat": umat, "ind": ind,
        "eye32": np.ascontiguousarray(np.tile(np.eye(32, dtype=np.float32), (4, 1))),
    }
    are, aim, ldt = f(s5_a_re)[0], f(s5_a_im)[0], f(s5_log_dt)[0]
    bre, bim, cre, cim, dsk = f(s5_b_re)[0], f(s5_b_im)[0], f(s5_c_re)[0], f(s5_c_im)[0], f(s5_d)[0]
    in_maps = []
    for core in range(8):
        b, hf = core // 2, core % 2
        g0 = 32 * hf
        xT = np.ascontiguousarray(x[b].T)
        order = list(range(4 * hf, 4 * hf + 4)) + list(range(4 * (1 - hf), 4 * (1 - hf) + 4))
        xfull = _tiles(xT)[:, order, :]
        st = lambda a: np.ascontiguousarray(a[g0:g0 + 32].reshape(16, 128).T)
        bexp = []
        for bsrc in (bre, bim):
            t = np.zeros((128, 4, 128), np.float32)
            for gl_ in range(32):
                i, g8 = gl_ // 8, gl_ % 8
                g2 = gl_ % 2
                t[g8 * 16:(g8 + 1) * 16, i, g2 * 64:(g2 + 1) * 64] = bsrc[g0 + gl_].T
            bexp.append(t)
        cexp = []
        for csrc in (cre, cim):
            t = np.zeros((128, 16, 32), np.float32)
            for gl_ in range(32):
                q, g2 = gl_ // 2, gl_ % 2
                t[g2 * 64:(g2 + 1) * 64, q, g2 * 16:(g2 + 1) * 16] = csrc[g0 + gl_].T
            cexp.append(t)
        sel = np.zeros((128, 2), np.float32)
        sel[:, hf] = 1.0
        m = dict(common)
        m.update({
            "xT_own": np.ascontiguousarray(_tiles(xT)[:, :, hf * NTOK:(hf + 1) * NTOK]),
            "xT_full": np.ascontiguousarray(xfull),
            "c_col": _col(c[b]),
            "a_re_st": st(are), "a_im_st": st(aim),
            "ldt_st": np.ascontiguousarray(np.repeat(ldt[g0:g0 + 32].reshape(16, 2).T, 64, axis=0)),
            "bexp_re": bexp[0], "bexp_im": bexp[1], "cexp_re": cexp[0], "cexp_im": cexp[1],
            "d_col": _col(dsk[512 * hf:512 * hf + 512]), "sel": sel,
        })
        in_maps.append(m)
    res = run_bass_kernel_spmd(_NC, in_maps, core_ids=list(range(8)))
    global _LAST
    _LAST = res
    out = np.empty((4, SEQ, 1024), np.float32)
    for core in range(8):
        b, hf = core // 2, core % 2
        o = np.asarray(res.results[core]["outT"])
        out[b, hf * NTOK:(hf + 1) * NTOK, :] = o.transpose(1, 0, 2).reshape(1024, NTOK).T
    return out
```
